# Optimizing a Trainium2 kernel written in Bass

```python
import math
import jax, jax.numpy as jnp
from jax import lax
import numpy as np

D_MODEL = 2048
BATCH = 4
SEQ = 2048
DEPTH = 2
DEC_BATCH = 128
DEC_SEQ = 8
PAST_LEN = 16384
PAGE_SIZE = 128

M_WIDTH = D_MODEL // 2
C_WIDTH = D_MODEL - M_WIDTH
N_HEADS = 4
HEAD_DIM = M_WIDTH // N_HEADS
CONV_K = 31
CHUNK = 64
LN_EPS = 1e-5
DN_ALPHA = (2 * DEPTH) ** 0.25
DN_BETA = (8 * DEPTH) ** -0.25
SPLITS = (M_WIDTH, 2 * M_WIDTH, 3 * M_WIDTH, 4 * M_WIDTH, 5 * M_WIDTH,
          5 * M_WIDTH + C_WIDTH, 5 * M_WIDTH + 2 * C_WIDTH, 5 * M_WIDTH + 3 * C_WIDTH,
          5 * M_WIDTH + 3 * C_WIDTH + N_HEADS)
F_OFF = 5 * M_WIDTH + 3 * C_WIDTH + N_HEADS
PROJ_COLS = 5 * M_WIDTH + 3 * C_WIDTH + 2 * N_HEADS

kernel_name = "hybrid_mlstm_conformer_conv_decode_step"


def _layernorm(x):
    x = x.astype(jnp.float32)
    mu = jnp.mean(x, axis=-1, keepdims=True)
    var = jnp.mean(jnp.square(x - mu), axis=-1, keepdims=True)
    return (x - mu) * lax.rsqrt(var + LN_EPS)


def _mlstm(q, k, v, ig, fg, C0, n0, m0):
    B, S, H, DH = q.shape
    L = math.gcd(S, CHUNK)
    NC = S // L
    to_chunks = lambda t: t.reshape(B, NC, L, H, DH).transpose(1, 0, 3, 2, 4)
    gate_chunks = lambda t: t.reshape(B, NC, L, H).transpose(1, 0, 3, 2)
    logf = jax.nn.log_sigmoid(fg)
    causal = jnp.tril(jnp.ones((L, L), dtype=bool))

    def step(carry, inp):
        C, n, m = carry
        qc, kc, vc, ic, lf = inp
        b = jnp.cumsum(lf, axis=-1)
        dmat = b[..., :, None] - b[..., None, :] + ic[..., None, :]
        dmat = jnp.where(causal, dmat, -jnp.inf)
        inter = b + m[..., None]
        m_t = jnp.maximum(inter, jnp.max(dmat, axis=-1))
        w_inter = jnp.exp(inter - m_t)
        s = jnp.einsum('bhtd,bhsd->bhts', qc, kc) * jnp.exp(dmat - m_t[..., None])
        num = (w_inter[..., None] * jnp.einsum('bhtd,bhde->bhte', qc, C)
               + jnp.einsum('bhts,bhse->bhte', s, vc))
        den = w_inter * jnp.einsum('bhtd,bhd->bht', qc, n) + jnp.sum(s, axis=-1)
        h = num / jnp.maximum(jnp.abs(den), jnp.exp(-m_t))[..., None]
        m_new = m_t[..., -1]
        decay = jnp.exp(b[..., -1] + m - m_new)
        ws = jnp.exp(b[..., -1:] - b + ic - m_new[..., None])
        C_new = decay[..., None, None] * C + jnp.einsum('bhs,bhsd,bhse->bhde', ws, kc, vc)
        n_new = decay[..., None] * n + jnp.einsum('bhs,bhsd->bhd', ws, kc)
        return (C_new, n_new, m_new), h

    (C1, n1, m1), h = lax.scan(step, (C0, n0, m0),
                               (to_chunks(q), to_chunks(k), to_chunks(v),
                                gate_chunks(ig), gate_chunks(logf)))
    h = h.transpose(1, 0, 3, 2, 4).reshape(B, S, H, DH)
    return h, C1, n1, m1


def _conv_module(ga, gg, buf, w_dw, b_dw, cln_g, cln_b):
    u = ga.astype(jnp.float32) * jax.nn.sigmoid(gg.astype(jnp.float32))
    full = jnp.concatenate([buf.astype(jnp.float32), u], axis=1)
    y = lax.conv_general_dilated(full, w_dw.astype(jnp.float32)[:, None, :],
                                 window_strides=(1,), padding='VALID',
                                 dimension_numbers=('NWC', 'WIO', 'NWC'),
                                 feature_group_count=C_WIDTH)
    y = y + b_dw.astype(jnp.float32)
    y = _layernorm(y) * cln_g.astype(jnp.float32) + cln_b.astype(jnp.float32)
    return jax.nn.silu(y), full[:, -(CONV_K - 1):, :]


def _layer(x, C0, n0, m0, buf, w_in, b_in, hn_g, w_dw, b_dw, cln_g, cln_b, w_out, ln_g, ln_b):
    B, S, _ = x.shape
    proj = jnp.einsum('bsd,de->bse', x, w_in) + b_in
    q, k, v, o, zm, ga, gg, zc, ig, fg = jnp.split(proj, SPLITS, axis=-1)
    heads = lambda t: t.astype(jnp.float32).reshape(B, S, N_HEADS, HEAD_DIM)
    h, C1, n1, m1 = _mlstm(heads(q), heads(k) * (HEAD_DIM ** -0.5), heads(v),
                           ig.astype(jnp.float32), fg.astype(jnp.float32), C0, n0, m0)
    h = (_layernorm(h) * hn_g.astype(jnp.float32).reshape(N_HEADS, HEAD_DIM)).reshape(B, S, M_WIDTH)
    mix_m = h * jax.nn.sigmoid(o.astype(jnp.float32)) * jax.nn.silu(zm.astype(jnp.float32))
    u, buf1 = _conv_module(ga, gg, buf, w_dw, b_dw, cln_g, cln_b)
    mix_c = u * jax.nn.silu(zc.astype(jnp.float32))
    mix = jnp.concatenate([mix_m, mix_c], axis=-1).astype(x.dtype)
    out = jnp.einsum('bse,ed->bsd', mix, w_out)
    y = _layernorm(DN_ALPHA * x.astype(jnp.float32) + out.astype(jnp.float32))
    y = y * ln_g.astype(jnp.float32) + ln_b.astype(jnp.float32)
    return y.astype(x.dtype), C1, n1, m1, buf1


def setup_inputs(seed: int = 0) -> dict:
    key = jax.random.key(seed)
    ks = jax.random.split(key, 20)
    f32 = jnp.float32
    nrm = lambda k, shape, s: s * jax.random.normal(k, shape, f32)
    b_in = nrm(ks[7], (DEPTH, PROJ_COLS), 0.02)
    b_in = b_in.at[:, F_OFF:].add(jnp.linspace(3.0, 6.0, N_HEADS, dtype=f32))
    return {
        "x_prompt": nrm(ks[0], (BATCH, SEQ, D_MODEL), 1.0),
        "x_sample": nrm(ks[1], (DEC_BATCH, DEC_SEQ, D_MODEL), 1.0),
        "state_C": nrm(ks[2], (DEPTH, DEC_BATCH, N_HEADS, HEAD_DIM, HEAD_DIM), 0.1),
        "state_n": nrm(ks[3], (DEPTH, DEC_BATCH, N_HEADS, HEAD_DIM), 0.1),
        "state_m": nrm(ks[4], (DEPTH, DEC_BATCH, N_HEADS), 1.0),
        "state_conv": nrm(ks[5], (DEPTH, DEC_BATCH, CONV_K - 1, C_WIDTH), 0.5),
        "w_in": nrm(ks[6], (DEPTH, D_MODEL, PROJ_COLS), D_MODEL ** -0.5),
        "b_in": b_in,
        "hn_g": 1.0 + nrm(ks[8], (DEPTH, M_WIDTH), 0.02),
        "w_dw": nrm(ks[9], (DEPTH, CONV_K, C_WIDTH), CONV_K ** -0.5),
        "b_dw": nrm(ks[10], (DEPTH, C_WIDTH), 0.02),
        "cln_g": 1.0 + nrm(ks[11], (DEPTH, C_WIDTH), 0.02),
        "cln_b": nrm(ks[12], (DEPTH, C_WIDTH), 0.02),
        "w_out": nrm(ks[13], (DEPTH, D_MODEL, D_MODEL), DN_BETA * D_MODEL ** -0.5),
        "ln_g": 1.0 + nrm(ks[14], (DEPTH, D_MODEL), 0.02),
        "ln_b": nrm(ks[15], (DEPTH, D_MODEL), 0.02),
    }


def reference(x_prompt, x_sample, state_C, state_n, state_m, state_conv,
              w_in, b_in, hn_g, w_dw, b_dw, cln_g, cln_b, w_out, ln_g, ln_b):
    Bp = x_prompt.shape[0]
    f32 = jnp.float32
    yp, ys = x_prompt, x_sample
    Cp_l, np_l, mp_l, cp_l = [], [], [], []
    Cs_l, ns_l, ms_l, cs_l = [], [], [], []
    for l in range(DEPTH):
        weights = (w_in[l], b_in[l], hn_g[l], w_dw[l], b_dw[l], cln_g[l], cln_b[l],
                   w_out[l], ln_g[l], ln_b[l])
        C0 = jnp.zeros((Bp, N_HEADS, HEAD_DIM, HEAD_DIM), f32)
        n0 = jnp.zeros((Bp, N_HEADS, HEAD_DIM), f32)
        m0 = jnp.zeros((Bp, N_HEADS), f32)
        buf0 = jnp.zeros((Bp, CONV_K - 1, C_WIDTH), f32)
        yp, Cp, npr, mp, cp = _layer(yp, C0, n0, m0, buf0, *weights)
        ys, Cs, ns, ms, cs = _layer(ys, state_C[l].astype(f32), state_n[l].astype(f32),
                                    state_m[l].astype(f32), state_conv[l], *weights)
        Cp_l.append(Cp); np_l.append(npr); mp_l.append(mp); cp_l.append(cp)
        Cs_l.append(Cs); ns_l.append(ns); ms_l.append(ms); cs_l.append(cs)
    return (yp, ys,
            jnp.stack(Cp_l), jnp.stack(np_l), jnp.stack(mp_l), jnp.stack(cp_l),
            jnp.stack(Cs_l), jnp.stack(ns_l), jnp.stack(ms_l), jnp.stack(cs_l))
```

```python
import numpy as np
import concourse.bass as bass
import concourse.mybir as mybir
from concourse.bass_utils import run_bass_kernel_spmd

F32 = mybir.dt.float32
BF16 = mybir.dt.bfloat16
AF = mybir.ActivationFunctionType
ALU = mybir.AluOpType
AX = mybir.AxisListType

D_MODEL = 2048
DEPTH = 2
NH = 4
DH = 256
MW = 1024
CW = 1024
CONVK = 31
PROJ = 8200
NPT = 1024
NST = 128
T = NPT + NST
NTT = T // 128
NCH = NPT // 128
KT = D_MODEL // 128
NSEQ = 16
LN_EPS = 1e-5
DN_ALPHA = (2 * DEPTH) ** 0.25
GW = 256
NEG_BIG = -1.0e30
import os
DEBUG = bool(int(os.environ.get('KDEBUG', '0')))

OFF_Q, OFF_K, OFF_V, OFF_O, OFF_ZM = 0, 1024, 2048, 3072, 4096
OFF_GA, OFF_GG, OFF_ZC, OFF_GATE = 5120, 6144, 7168, 8192


class Res:
    __slots__ = ("name", "w", "r")

    def __init__(self, name, inherit=None):
        self.name = name
        self.w = None
        self.r = dict(inherit) if inherit else {}


class Sched:
    ENGS = ("pe", "act", "dve", "pool", "sp")

    def __init__(self, nc):
        self.nc = nc
        self.ops = {e: [] for e in self.ENGS}
        self.sems = {}
        self.cnt = {}
        self.waited = {e: {} for e in self.ENGS}
        self._ctx = []
        for e in self.ENGS:
            self._mksem("E_" + e)

    def _mksem(self, key):
        cm = self.nc.semaphore(key)
        h = cm.__enter__()
        self._ctx.append(cm)
        self.sems[key] = h
        self.cnt[key] = 0
        return h

    def chan(self, name):
        key = "D_" + name
        if key not in self.sems:
            self._mksem(key)
        return key

    def _need(self, eng, tick, waits):
        if tick is None:
            return
        key, val = tick
        if self.waited[eng].get(key, 0) >= val:
            return
        self.waited[eng][key] = val
        waits[key] = max(waits.get(key, 0), val)

    def op(self, eng, method, kw, reads=(), writes=(), inc=True, chan=None, amount=16, noncontig=False):
        fn = (method, kw, noncontig)
        own = "E_" + eng
        waits = {}
        same = chan is None
        for r in reads:
            self._need(eng, r.w, waits)
        for w in writes:
            strict = not (same and eng == "pe")
            if w.w is not None and (strict or w.w[0] != own):
                self._need(eng, w.w, waits)
            for k, v in w.r.items():
                if not strict and k == own:
                    continue
                self._need(eng, (k, v), waits)
        if chan is not None:
            self.cnt[chan] += amount
            tick = (chan, self.cnt[chan])
            self.ops[eng].append((waits, fn, (chan, amount)))
        elif inc:
            self.cnt[own] += 1
            tick = (own, self.cnt[own])
            self.ops[eng].append((waits, fn, (own, 1)))
        else:
            tick = (own, self.cnt[own] + 1)
            self.ops[eng].append((waits, fn, None))
        for r in reads:
            if r.r.get(tick[0], 0) < tick[1]:
                r.r[tick[0]] = tick[1]
        for w in writes:
            w.w = tick
            w.r = {}
        return tick

    def dma(self, eng, out, in_, reads=(), writes=(), chan="x", noncontig=False):
        ck = self.chan(chan)
        return self.op(eng, "dma_start", dict(out=out, in_=in_), reads=reads, writes=writes, chan=ck, noncontig=noncontig)

    def wait_all(self, eng, ress):
        waits = {}
        for r in ress:
            self._need(eng, r.w, waits)
            for k, v in r.r.items():
                self._need(eng, (k, v), waits)
        self.ops[eng].append((waits, None, None))

    def emit(self):
        nc = self.nc
        getters = {"pe": "tensor", "act": "scalar", "dve": "vector", "pool": "gpsimd", "sp": "sync"}
        with nc.Block() as block:
            for e in self.ENGS:
                ops = self.ops[e]
                if not ops:
                    continue

                def body(engh, ops=ops):
                    for waits, fn, inc in ops:
                        for k, v in waits.items():
                            engh.wait_ge(self.sems[k], v)
                        if fn is None:
                            continue
                        method, kw, noncontig = fn
                        if noncontig:
                            with nc.allow_non_contiguous_dma(reason="tiny strided transfer"):
                                ins = getattr(engh, method)(**kw)
                        else:
                            ins = getattr(engh, method)(**kw)
                        if inc is not None:
                            ins.then_inc(self.sems[inc[0]], inc[1])

                getattr(block, getters[e])(body)


class Arena:
    def __init__(self, nc, words):
        self.words = words
        self.t = nc.alloc_sbuf_tensor("arena", [128, words], F32)
        self.top = 0
        self.stack = []
        self.live = []
        self.dead = []
        self.peak = 0

    def push(self):
        self.stack.append((self.top, len(self.live)))

    def pop(self):
        top, nlive = self.stack.pop()
        for (s, e, ress) in self.live[nlive:]:
            ticks = {}
            for r in ress:
                if r.w is not None:
                    ticks[r.w[0]] = max(ticks.get(r.w[0], 0), r.w[1])
                for k, v in r.r.items():
                    ticks[k] = max(ticks.get(k, 0), v)
            self.dead.append((s, e, ticks))
        del self.live[nlive:]
        self.top = top

    def alloc(self, name, shape, dt):
        n = int(np.prod(shape[1:]))
        words = n if dt == F32 else (n + 1) // 2
        words = (words + 7) // 8 * 8
        a0 = self.top
        a1 = a0 + words
        if a1 > self.words:
            raise RuntimeError(f"SBUF arena overflow allocating {name}: need {a1} words of {self.words}")
        self.top = a1
        self.peak = max(self.peak, a1)
        inherit = {}
        keep = []
        for (s, e, ticks) in self.dead:
            if s < a1 and a0 < e:
                for k, v in ticks.items():
                    inherit[k] = max(inherit.get(k, 0), v)
                if not (a0 <= s and e <= a1):
                    keep.append((s, e, ticks))
            else:
                keep.append((s, e, ticks))
        self.dead = keep
        v = self.t[:, a0:a1]
        if dt != F32:
            v = v.bitcast(dt)
        v = v[:, 0:n]
        if len(shape) == 3:
            v = v.rearrange("p (a b) -> p a b", b=shape[2])
        elif len(shape) == 4:
            v = v.rearrange("p (a b c) -> p a b c", b=shape[2], c=shape[3])
        ress = []
        self.live.append((a0, a1, ress))
        t_ = Tile(name, v, ress, inherit)
        t_.base, t_.nwords = a0, words
        return t_


def _ticks_of(tiles):
    t = {}
    for tl in tiles:
        for r in tl._ress:
            if r.w is not None:
                t[r.w[0]] = max(t.get(r.w[0], 0), r.w[1])
            for k, v in r.r.items():
                t[k] = max(t.get(k, 0), v)
    return t


def arena_alloc_at(arena, name, shape, dt, a0, over_tiles):
    n = int(np.prod(shape[1:]))
    words = n if dt == F32 else (n + 1) // 2
    words = (words + 7) // 8 * 8
    v = arena.t[:, a0:a0 + words]
    if dt != F32:
        v = v.bitcast(dt)
    v = v[:, 0:n]
    if len(shape) == 3:
        v = v.rearrange("p (a b) -> p a b", b=shape[2])
    inh = _ticks_of(over_tiles)
    for tl in over_tiles:
        for k, vv in (tl._inh or {}).items():
            inh[k] = max(inh.get(k, 0), vv)
    for (s_, e_, ticks) in arena.dead:
        if s_ < a0 + words and a0 < e_:
            for k, vv in ticks.items():
                inh[k] = max(inh.get(k, 0), vv)
    t_ = Tile(name, v, [], inh)
    t_.base, t_.nwords = a0, words
    return t_


def arena_adopt(arena, tile):
    assert arena.top == tile.base, (arena.top, tile.base)
    arena.top += tile.nwords
    arena.peak = max(arena.peak, arena.top)
    arena.live.append((tile.base, tile.base + tile.nwords, tile._ress))


class SubArena:
    def __init__(self, arena, tile, base_words, nwords):
        self.arena = arena
        self.tile = tile
        self.base = base_words
        self.nwords = nwords
        self.top = 0
        self.inherit = dict(tile._inh) if tile._inh else {}
        for r in tile._ress:
            if r.w is not None:
                self.inherit[r.w[0]] = max(self.inherit.get(r.w[0], 0), r.w[1])
            for k, v in r.r.items():
                self.inherit[k] = max(self.inherit.get(k, 0), v)
        self.ress = []

    def alloc(self, name, shape, dt):
        n = int(np.prod(shape[1:]))
        words = n if dt == F32 else (n + 1) // 2
        words = (words + 7) // 8 * 8
        a0 = self.base + self.top
        a1 = a0 + words
        if self.top + words > self.nwords:
            raise RuntimeError(f"sub-arena overflow allocating {name}")
        self.top += words
        v = self.arena.t[:, a0:a1]
        if dt != F32:
            v = v.bitcast(dt)
        v = v[:, 0:n]
        if len(shape) == 3:
            v = v.rearrange("p (a b) -> p a b", b=shape[2])
        elif len(shape) == 4:
            v = v.rearrange("p (a b c) -> p a b c", b=shape[2], c=shape[3])
        return Tile(name, v, self.ress, self.inherit)

    def close(self):
        ticks = {}
        for r in self.ress:
            if r.w is not None:
                ticks[r.w[0]] = max(ticks.get(r.w[0], 0), r.w[1])
            for k, v in r.r.items():
                ticks[k] = max(ticks.get(k, 0), v)
        for r in self.tile._ress:
            for k, v in ticks.items():
                if r.r.get(k, 0) < v:
                    r.r[k] = v


class Tile:
    def __init__(self, name, ap, ress, inherit):
        self.name = name
        self.ap = ap
        self._ress = ress
        self._inh = inherit
        self._d = {}

    def R(self, *idx):
        r = self._d.get(idx)
        if r is None:
            r = Res(f"{self.name}{idx}", self._inh)
            self._d[idx] = r
            self._ress.append(r)
        return r

    def __getitem__(self, k):
        return self.ap[k]


class PSum:
    def __init__(self, nc):
        self.t = nc.alloc_psum_tensor("ps", [128, 8, 512], F32)
        self.tb = self.t.bitcast(BF16)
        self.res = [Res(f"bank{i}") for i in range(8)]
        self.ptr = 0
        self.pinned = set()

    def get(self, n=1):
        for _ in range(16):
            if self.ptr + n > 8:
                self.ptr = 0
            b = self.ptr
            if any((b + i) in self.pinned for i in range(n)):
                self.ptr = (b + 1) % 8
                continue
            self.ptr = (self.ptr + n) % 8
            return b, self.res[b:b + n]
        raise RuntimeError("no free PSUM banks")


def build_program():
    nc = bass.Bass("TRN2", target_bir_lowering=False)
    S = Sched(nc)

    def din(name, shape):
        return nc.dram_tensor(name, shape, F32, kind="ExternalInput").ap()

    def dout(name, shape):
        return nc.dram_tensor(name, shape, F32, kind="ExternalOutput").ap()

    xT_d = din("xT", [D_MODEL, T])
    xtok_d = din("xtok", [T, D_MODEL])
    sC_d = din("sC", [DEPTH, NSEQ, NH, DH, DH])
    snT_d = din("snT", [DEPTH, 128, 128])
    smT_d = din("smT", [DEPTH, NH, NSEQ])
    scvT_d = din("scvT", [DEPTH, 128, 8 * NSEQ * 30])
    scvK_d = din("scvK", [DEPTH, 128, 8 * NSEQ * 22])
    w_in_d = din("w_in", [DEPTH, D_MODEL, PROJ])
    w_out_d = din("w_out", [DEPTH, D_MODEL, D_MODEL])
    bfm_d = din("bfm", [DEPTH, 128, 64])
    bg_d = din("bg", [DEPTH, 1, 8])
    bv_d = din("bv", [DEPTH, 1, MW])
    hng_d = din("hng", [DEPTH, 128, 8])
    wdw_d = din("wdw", [DEPTH, 128, 8, CONVK])
    bdw_d = din("bdw", [DEPTH, 128, 8])
    clng_d = din("clng", [DEPTH, 128, 8])
    clnb_d = din("clnb", [DEPTH, 128, 8])
    lng_d = din("lng", [DEPTH, 1, D_MODEL])
    lnb_d = din("lnb", [DEPTH, 1, D_MODEL])
    flag_d = din("flag", [128, 1])
    y_d = dout("y", [T, D_MODEL])
    Cp_d = dout("Cp", [DEPTH, NH, DH, DH])
    npo_d = dout("npo", [DEPTH, 128, 8])
    mpo_d = dout("mpo", [DEPTH, NH, 1])
    cpo_d = dout("cpo", [DEPTH, 128, 8, 30])
    Cs_d = dout("Cs", [DEPTH, NSEQ, NH, DH, DH])
    nso_d = dout("nso", [DEPTH, 128, 128])
    mso_d = dout("mso", [DEPTH, NH, NSEQ])
    cso_old_d = dout("cso_old", [DEPTH, 128, 8 * NSEQ * 22])
    cso_new_d = dout("cso_new", [DEPTH, 128, 8 * NSEQ * 8])

    y1_d = nc.dram_tensor("y1scr", [T, D_MODEL], F32).ap()
    R_y1 = [Res(f"y1_{tt}") for tt in range(NTT)]
    bnc1 = [nc.dram_tensor(f"bnc1_{l}", [128, 248], F32).ap() for l in range(DEPTH)]
    gat1 = [nc.dram_tensor(f"gat1_{l}", [256, 248], F32).ap() for l in range(DEPTH)]
    bnc2 = [nc.dram_tensor(f"bnc2_{l}", [128, 2056], F32).ap() for l in range(DEPTH)]
    gat2 = [nc.dram_tensor(f"gat2_{l}", [256, 2056], F32).ap() for l in range(DEPTH)]
    wob_d = [nc.dram_tensor(f"wob{l}", [D_MODEL, D_MODEL], BF16).ap() for l in range(DEPTH)]
    R_wob = [Res(f"wob{l}") for l in range(DEPTH)]
    GROUPS = [[0, 1], [2, 3], [4, 5], [6, 7]]

    A = Arena(nc, 52800)
    PS = PSum(nc)
    ps = PS.t
    psb = PS.tb

    def ACT(R, W, **kw):
        S.op("act", "activation", kw, R, W)

    def DV(m, R, W, **kw):
        S.op("dve", m, kw, R, W)

    def PL(m, R, W, **kw):
        S.op("pool", m, kw, R, W)

    def MM(R, W, inc=True, **kw):
        S.op("pe", "matmul", kw, R, W, inc=inc)

    def TR(R, W, out, in_):
        S.op("pe", "transpose", dict(out=out, in_=in_, identity=identb), R, W)

    cst = A.alloc("cst", [128, 544], F32)
    c_off = [0]

    def cslice(n):
        a = c_off[0]
        c_off[0] += n
        return cst.ap[:, a:a + n]

    ident = cslice(128)
    maskp = cslice(128)
    masks = cslice(128)
    blk = cslice(16)
    ones_c = cslice(128)
    i4 = cslice(4)
    epsc = cslice(1)
    flag = cslice(1)
    R_c = cst.R()
    cstb = A.alloc("cstb", [128, 128], BF16)
    identb = cstb.ap[:, 0:128]
    R_cb = cstb.R()

    PL("memset", [], [R_c], ap=cst.ap[:, 0:544], constant=0.0)
    PL("memset", [], [R_c], ap=ident, constant=1.0)
    PL("affine_select", [R_c], [R_c], out=ident, in_=ident, pattern=[[-1, 128]], compare_op=ALU.is_equal, fill=0.0, base=0, channel_multiplier=1)
    PL("memset", [], [R_c], ap=maskp, constant=1.0)
    PL("affine_select", [R_c], [R_c], out=maskp, in_=maskp, pattern=[[1, 128]], compare_op=ALU.is_ge, fill=0.0, base=0, channel_multiplier=-1)
    PL("memset", [], [R_c], ap=blk, constant=1.0)
    PL("affine_select", [R_c], [R_c], out=blk, in_=blk, pattern=[[-8, 16]], compare_op=ALU.is_ge, fill=0.0, base=0, channel_multiplier=1)
    PL("affine_select", [R_c], [R_c], out=blk, in_=blk, pattern=[[8, 16]], compare_op=ALU.is_ge, fill=0.0, base=7, channel_multiplier=-1)
    PL("tensor_tensor", [R_c], [R_c], out=masks.rearrange("p (j i) -> p j i", i=8), in0=maskp.rearrange("p (j i) -> p j i", i=8),
       in1=blk.unsqueeze(2).to_broadcast([128, 16, 8]), op=ALU.mult)
    PL("memset", [], [R_c], ap=ones_c, constant=1.0)
    PL("tensor_copy", [R_c], [R_c], out=i4[0:4, :], in_=ident[0:4, 0:4])
    PL("memset", [], [R_c], ap=epsc, constant=LN_EPS)
    S.dma("sp", flag, flag_d[:, :], writes=[R_c], chan="flag")
    PL("tensor_copy", [R_c], [R_cb], out=identb, in_=ident)

    PO = {}
    o = 0
    for nm, n in (("bfm", 64), ("hng", 8), ("bk16", 8), ("wdw", 8 * CONVK), ("bdw", 8), ("clng", 8), ("clnb", 8), ("bg", 8)):
        PO[nm] = (o, n)
        o += n
    prm = A.alloc("prm", [128, DEPTH, o], F32)
    R_prm = prm.R()

    def P(l, nm):
        a, n = PO[nm]
        return prm.ap[:, l, a:a + n]

    for l in range(DEPTH):
        S.dma("sp", P(l, "bfm"), bfm_d[l], writes=[R_prm], chan="cst")
        S.dma("sp", P(l, "hng"), hng_d[l], writes=[R_prm], chan="cst")
        S.dma("sp", P(l, "wdw"), wdw_d[l].rearrange("p c j -> p (c j)"), writes=[R_prm], chan="cst")
        S.dma("sp", P(l, "bdw"), bdw_d[l], writes=[R_prm], chan="cst")
        S.dma("sp", P(l, "clng"), clng_d[l], writes=[R_prm], chan="cst")
        S.dma("sp", P(l, "clnb"), clnb_d[l], writes=[R_prm], chan="cst")
        S.dma("sp", P(l, "bg"), bg_d[l].partition_broadcast(128), writes=[R_prm], chan="cst")
    for l in range(DEPTH):
        DV("tensor_scalar", [R_prm], [R_prm], out=P(l, "bk16"), in0=P(l, "bfm")[:, 8:16], scalar1=1.0 / 16.0, scalar2=None, op0=ALU.mult)

    xT = A.alloc("xT", [128, KT, T], BF16)
    wbufs = [A.alloc(f"wb{i}", [128, KT, GW], BF16) for i in range(3)]
    wstate = {"i": 0}

    S.dma("pool", xT.ap[:, :, :], xT_d.rearrange("(k p) t -> p k t", p=128), writes=[xT.R(tt) for tt in range(NTT)], chan="xT")

    def load_w(src_ap, ncols=GW):
        i = wstate["i"] % 3
        wstate["i"] += 1
        wb = wbufs[i]
        S.dma("pool", wb.ap[:, :, 0:ncols], src_ap.rearrange("(k p) n -> p k n", p=128), writes=[wb.R(0), wb.R(1)], chan=f"w{i}")
        return wb, wb

    def xT_res(t0, t1):
        return [xT.R(tt) for tt in range(t0 // 128, (t1 + 127) // 128)]

    TG = [(0, 384), (384, 768), (768, 1152)]

    def proj_fm(wb, wres, j, evac):
        b0, bres = PS.get(3)
        for kt in range(KT):
            for g, (t0, t1) in enumerate(TG):
                last = (kt == KT - 1 and g == 2)
                MM([wres.R(j)] + xT_res(t0, t1), ([bres[g]] if kt == 0 else []) + (bres if last else []), inc=last,
                   out=ps[:, b0 + g, 0:384], lhsT=wb.ap[:, kt, j * 128:(j + 1) * 128], rhs=xT.ap[:, kt, t0:t1],
                   start=(kt == 0), stop=(kt == KT - 1))
        for g in range(3):
            evac(g, ps[:, b0 + g, 0:384], bres[g])

    def proj_fm_gen(wb, wres, j, evac, seg=4):
        b0, bres = PS.get(3)
        PS.pinned.update((b0, b0 + 1, b0 + 2))
        for kt in range(KT):
            for g, (t0, t1) in enumerate(TG):
                last = (kt == KT - 1 and g == 2)
                MM([wres.R(j)] + xT_res(t0, t1), ([bres[g]] if kt == 0 else []) + (bres if last else []), inc=last,
                   out=ps[:, b0 + g, 0:384], lhsT=wb.ap[:, kt, j * 128:(j + 1) * 128], rhs=xT.ap[:, kt, t0:t1],
                   start=(kt == 0), stop=(kt == KT - 1))
            if kt % seg == seg - 1 and kt != KT - 1:
                yield
        PS.pinned.difference_update((b0, b0 + 1, b0 + 2))
        for g in range(3):
            evac(g, ps[:, b0 + g, 0:384], bres[g])
        yield

    def layer(l):
        A.push()
        kstate = {}
        mixc = A.alloc("mixc", [128, 8, T], BF16)

        gt = A.alloc("gates", [128, 1100], F32)
        g_off = [0]

        def gsl(n):
            a = g_off[0]
            g_off[0] += n
            assert g_off[0] <= 1100
            return gt.ap[:, a:a + n]

        Gt = gsl(72).rearrange("p (t c) -> p t c", c=8)
        Lt = gsl(36).rearrange("p (t c) -> p t c", c=4)
        gg_ = gsl(36).rearrange("p (t c) -> p t c", c=4)
        nb = gsl(36).rearrange("p (t c) -> p t c", c=4)
        eg = gsl(36).rearrange("p (t c) -> p t c", c=4)
        enb = gsl(36).rearrange("p (t c) -> p t c", c=4)
        Et = gsl(36).rearrange("p (t c) -> p t c", c=4)
        EB = gsl(32).rearrange("p (h c) -> p h c", c=8)
        BCs = gsl(192).rearrange("p (h v j) -> p h v j", v=3, j=16)
        EMF = gsl(4)
        Bs = gsl(24)
        Gmx = gsl(24)
        eBv = gsl(8)
        Mloc = gsl(1)
        Btot = gsl(1)
        mA = gsl(1)
        minit = gsl(1)
        mfin = gsl(1)
        emfv = gsl(1)
        mprev = gsl(16)
        mnew = gsl(16)
        V3f = gsl(48)
        V3 = V3f.rearrange("p (v j) -> p v j", j=16)
        tmp4 = gsl(16)
        X4 = gsl(192)
        R_g = gt.R()

        def bcast4(src, n, dst, dst_view):
            X = X4[0:4, 0:4 * n].rearrange("p (h n) -> p h n", n=n)
            DV("tensor_tensor", [R_g, R_c], [R_g], out=X, in0=src.unsqueeze(1).to_broadcast([4, 4, n]),
               in1=i4[0:4, :].unsqueeze(2).to_broadcast([4, 4, n]), op=ALU.mult)
            b0, br = PS.get(1)
            MM([R_g, R_c], br, out=ps[:, b0, 0:4 * n], lhsT=ones_c[0:4, :], rhs=X4[0:4, 0:4 * n], start=True, stop=True)
            ACT(br, [R_g], out=dst, in_=dst_view(ps[:, b0, 0:4 * n]), func=AF.Copy)

        def gate_chain():
            wg, wgres = load_w(w_in_d[l][:, OFF_GATE:OFF_GATE + 8], ncols=8)
            bg0, bgres = PS.get(1)
            for tt in range(NTT):
                for kt in range(KT):
                    last = (tt == NTT - 1 and kt == KT - 1)
                    MM([wgres.R(0), xT.R(tt)], (bgres if (tt == 0 and kt == 0) or last else []), inc=last,
                       out=ps[:, bg0, tt * 8:(tt + 1) * 8], lhsT=xT.ap[:, kt, tt * 128:(tt + 1) * 128], rhs=wg.ap[:, kt, 0:8],
                       start=(kt == 0), stop=(kt == KT - 1))
            DV("tensor_tensor", bgres + [R_prm], [R_g], out=Gt, in0=ps[:, bg0, 0:72].rearrange("p (t c) -> p t c", c=8),
               in1=P(l, "bg").unsqueeze(1).to_broadcast([128, NTT, 8]), op=ALU.add)
            ACT([R_g], [R_g], out=Et, in_=Gt[:, :, 4:8], func=AF.Exp, scale=-1.0)
            ACT([R_g], [R_g], out=Lt, in_=Et, func=AF.Ln, bias=1.0)
            bc0, bcres = PS.get(1)
            for tt in range(NTT):
                m_ = maskp if tt < NCH else masks
                MM([R_c, R_g], bcres, out=ps[:, bc0, tt * 4:(tt + 1) * 4], lhsT=m_, rhs=Lt[:, tt, :], start=True, stop=True)
            csv = ps[:, bc0, 0:36].rearrange("p (t c) -> p t c", c=4)
            DV("tensor_tensor", bcres + [R_g], [R_g], out=gg_, in0=Gt[:, :, 0:4], in1=csv, op=ALU.add)
            ACT(bcres, [R_g], out=nb, in_=csv, func=AF.Copy)
            ACT([R_g], [R_g], out=eg, in_=gg_, func=AF.Exp)
            ACT([R_g], [R_g], out=enb, in_=nb, func=AF.Exp)
            bs0, bsres = PS.get(1)
            for c in range(NCH):
                MM([R_c, R_g], bsres, out=ps[0:4, bs0, c:c + 1], lhsT=Lt[:, c, :], rhs=ones_c[:, 0:1], start=True, stop=True)
            MM([R_c, R_g], bsres, out=ps[0:4, bs0, 8:24], lhsT=Lt[:, NCH, :], rhs=blk, start=True, stop=True)
            DV("tensor_copy", bsres, [R_g], out=Bs[0:4, :], in_=ps[0:4, bs0, 0:24])
            gT0, gTres = PS.get(3)
            for tt in range(NTT):
                bnk, col = gT0 + tt // 4, (tt % 4) * 128
                MM([R_c, R_g], [gTres[tt // 4]], out=ps[0:4, bnk, col:col + 128], lhsT=gg_[:, tt, :], rhs=ident, start=True, stop=True)
            for hb in range(2):
                DV("tensor_reduce", [gTres[hb]], [R_g], out=Gmx[0:4, hb * 4:(hb + 1) * 4],
                   in_=ps[0:4, gT0 + hb, 0:512].rearrange("p (c t) -> p c t", t=128), axis=AX.X, op=ALU.max)
            DV("tensor_reduce", [gTres[2]], [R_g], out=Gmx[0:4, 8:24], in_=ps[0:4, gT0 + 2, 0:128].rearrange("p (j i) -> p j i", i=8),
               axis=AX.X, op=ALU.max)
            DV("memset", [], [R_g], ap=Mloc[0:4, :], constant=NEG_BIG)
            for c in range(NCH):
                DV("scalar_tensor_tensor", [R_g], [R_g], out=Mloc[0:4, :], in0=Mloc[0:4, :], scalar=Gmx[0:4, c:c + 1], in1=Bs[0:4, c:c + 1],
                   op0=ALU.max, op1=ALU.subtract)
            DV("tensor_reduce", [R_g], [R_g], out=Btot[0:4, :], in_=Bs[0:4, 0:8], axis=AX.X, op=ALU.add)
            DV("tensor_scalar", [R_g], [R_g], out=Btot[0:4, :], in0=Btot[0:4, :], scalar1=-1.0, scalar2=None, op0=ALU.mult)
            DV("tensor_tensor", [R_g], [R_g], out=mA[0:4, :], in0=Btot[0:4, :], in1=Mloc[0:4, :], op=ALU.max)
            ACT([R_g], [R_g], out=eBv[0:4, :], in_=Bs[0:4, 0:8], func=AF.Exp, scale=-1.0)

            bcast4(eBv[0:4, :], 8, EB, lambda a: a.rearrange("p (h c) -> p h c", c=8))
            S.dma("sp", mprev[0:4, :], smT_d[l], writes=[R_g], chan="gat")
            DV("tensor_tensor", [R_g], [R_g], out=mnew[0:4, :], in0=mprev[0:4, :], in1=Gmx[0:4, 8:24], op=ALU.max)
            DV("tensor_tensor", [R_g], [R_g], out=mnew[0:4, :], in0=mnew[0:4, :], in1=Bs[0:4, 8:24], op=ALU.subtract)
            DV("tensor_tensor", [R_g], [R_g], out=tmp4[0:4, :], in0=mprev[0:4, :], in1=mnew[0:4, :], op=ALU.subtract)
            DV("tensor_tensor", [R_g], [R_g], out=tmp4[0:4, :], in0=tmp4[0:4, :], in1=Bs[0:4, 8:24], op=ALU.subtract)
            ACT([R_g], [R_g], out=V3[0:4, 0, :], in_=tmp4[0:4, :], func=AF.Exp)
            DV("tensor_tensor", [R_g], [R_g], out=tmp4[0:4, :], in0=mnew[0:4, :], in1=Bs[0:4, 8:24], op=ALU.add)
            ACT([R_g], [R_g], out=V3[0:4, 1, :], in_=tmp4[0:4, :], func=AF.Exp, scale=-1.0)
            ACT([R_g], [R_g], out=V3[0:4, 2, :], in_=mprev[0:4, :], func=AF.Exp)
            bcast4(V3f[0:4, :], 48, BCs, lambda a: a.rearrange("p (h v j) -> p h v j", v=3, j=16))
            S.dma("sp", mso_d[l], mnew[0:4, :], reads=[R_g], chan="osm")


        A.push()
        conv_base = A.top
        ub = A.alloc("ub", [128, 8, 30 + NPT], BF16)
        ubs = A.alloc("ubs", [128, 8, NSEQ, 38], BF16)
        dgt = [A.alloc(f"dgt{i}", [128, CONVK, 128], BF16) for i in range(2)]
        ycv = A.alloc("ycv", [128, 8, T], F32)
        tl32 = A.alloc("tl32", [128, 8, 30], F32)
        us32 = A.alloc("us32", [128, 8, NSEQ, 8], F32)
        YA = SubArena(A, dgt[1], dgt[1].base, dgt[1].nwords)
        hsb = YA.alloc("hsb", [128, 8, NSEQ * 30], BF16)
        P1 = A.alloc("P1", [128, 248], F32)
        hrecv = A.alloc("hrecv", [128, 248], F32)
        sgt = [A.alloc(f"sgt{i}", [128, 384], F32) for i in range(6)]
        sgi = [0]
        RUB = [ub.R(ct) for ct in range(8)]
        RUS = [ubs.R(ct) for ct in range(8)]
        S.dma("pool", hsb.ap[:, :, :].rearrange("p c x -> p (c x)"), scvT_d[l], writes=[hsb.R()], chan="scv")
        ACT([hsb.R()], RUS, out=ubs.ap[:, :, :, 0:30], in_=hsb.ap[:, :, :].rearrange("p c (j t) -> p c j t", t=30), func=AF.Copy)
        YA.close()
        S.dma("sp", cso_old_d[l], scvK_d[l], chan="ocs0")

        for ct in range(8):
            i_w = wstate["i"] % 3
            wstate["i"] += 1
            wb = wbufs[i_w]
            for half, off in ((0, OFF_GA), (1, OFF_GG)):
                S.dma("pool", wb.ap[:, :, half * 128:(half + 1) * 128],
                      w_in_d[l][:, off + ct * 128:off + (ct + 1) * 128].rearrange("(k p) n -> p k n", p=128), writes=[wb.R(half)], chan=f"w{i_w}{'ab'[half]}")
            held = {}

            def evac_g(g, pap, bres, ct=ct):
                st = sgt[sgi[0] % 6]
                sgi[0] += 1
                ACT([bres, R_prm], [st.R()], out=st.ap[:, :], in_=pap, func=AF.Sigmoid, bias=P(l, "bfm")[:, 48 + ct:49 + ct])
                held[g] = st
            proj_fm(wb, wb, 1, evac_g)

            def evac_a(g, pa, bra, ct=ct):
                st = held[g]
                bga = P(l, "bfm")[:, 40 + ct:41 + ct]
                t0, t1 = TG[g]
                if g < 2:
                    DV("scalar_tensor_tensor", [bra, st.R(), R_prm], [RUB[ct]], out=ub.ap[:, ct, 30 + t0:30 + t1], in0=pa, scalar=bga, in1=st.ap[:, :],
                       op0=ALU.add, op1=ALU.mult)
                else:
                    DV("scalar_tensor_tensor", [bra, st.R(), R_prm], [RUB[ct]], out=ub.ap[:, ct, 30 + 768:30 + NPT], in0=pa[:, 0:256], scalar=bga,
                       in1=st.ap[:, 0:256], op0=ALU.add, op1=ALU.mult)
                    DV("scalar_tensor_tensor", [bra, st.R(), R_prm], [tl32.R()], out=tl32.ap[:, ct, :], in0=pa[:, 226:256], scalar=bga,
                       in1=st.ap[:, 226:256], op0=ALU.add, op1=ALU.mult)
                    DV("scalar_tensor_tensor", [bra, st.R(), R_prm], [us32.R()], out=us32.ap[:, ct, :, :], in0=pa[:, 256:384].rearrange("p (j i) -> p j i", i=8),
                       scalar=bga, in1=st.ap[:, 256:384].rearrange("p (j i) -> p j i", i=8), op0=ALU.add, op1=ALU.mult)
                    ACT([us32.R()], [RUS[ct]], out=ubs.ap[:, ct, :, 30:38], in_=us32.ap[:, ct, :, :], func=AF.Copy)
            proj_fm(wb, wb, 0, evac_a)
            if ct == 0:
                gate_chain()
        zc_pre = [load_w(w_in_d[l][:, OFF_ZC + pair * GW:OFF_ZC + (pair + 1) * GW]) for pair in range(2)]
        R_P1 = P1.R()
        DV("memset", [], [R_P1], ap=P1.ap[:, 240:248], constant=0.0)
        DV("tensor_copy", [tl32.R()], [R_P1], out=P1.ap[:, 0:240].rearrange("p (c t) -> p c t", t=30), in_=tl32.ap[:, :, :])
        DV("tensor_copy", [R_g], [R_P1], out=P1.ap[0:4, 240:241], in_=mA[0:4, :])
        R_b1 = Res("bnc1")
        R_g1 = Res("gat1")
        S.dma("sp", bnc1[l][:, :], P1.ap[:, :], reads=[R_P1], writes=[R_b1], chan="x1")
        S.dma("sp", cpo_d[l].rearrange("p c t -> p (c t)"), P1.ap[:, 0:240], reads=[R_P1], chan="ocp")
        S.dma("sp", cso_new_d[l], us32.ap[:, :, :, :].rearrange("p c j i -> p (c j i)"), reads=[us32.R()], chan="ocs1")
        S.op("pool", "collective_compute", dict(kind="AllGather", op=ALU.bypass, replica_groups=GROUPS, ins=[bnc1[l][:, :]], outs=[gat1[l][:, :]]),
             reads=[R_b1], writes=[R_g1], chan=S.chan(f"cc1_{l}"), amount=1)
        R_hr = hrecv.R()
        S.dma("sp", hrecv.ap[:, :], gat1[l][0:128, :], reads=[R_g1], writes=[R_hr], chan="x1r")
        for pair in range(4):
            wz, wzres = zc_pre[pair] if pair < 2 else load_w(w_in_d[l][:, OFF_ZC + pair * GW:OFF_ZC + (pair + 1) * GW])
            for j in range(2):
                ct = pair * 2 + j

                def evac_z(g, pap, bres, ct=ct):
                    t0, t1 = TG[g]
                    ACT([bres, R_prm], [mixc.R(ct, g)], out=mixc.ap[:, ct, t0:t1], in_=pap, func=AF.Silu, bias=P(l, "bfm")[:, 56 + ct:57 + ct])
                proj_fm(wz, wzres, j, evac_z)
        DV("tensor_scalar", [R_hr, R_c] + RUB, RUB, out=ub.ap[:, :, 0:30], in0=hrecv.ap[:, 0:240].rearrange("p (c t) -> p c t", t=30),
           scalar1=flag, scalar2=None, op0=ALU.mult)
        DV("tensor_scalar", [R_hr, R_c], [R_g], out=minit[0:4, :], in0=hrecv.ap[0:4, 240:241], scalar1=flag[0:4, :], scalar2=None, op0=ALU.mult)
        S.dma("pool", wob_d[l][:, :], w_out_d[l][:, :], writes=[R_wob[l]], chan="wob")
        wd = P(l, "wdw").rearrange("p (c j) -> p c j", j=CONVK)
        segs = [(0, 512), (512, 1024), (1024, 1152)]
        NDV = 6
        cacc = [A.alloc(f"cacc{i}", [128, T], F32) for i in range(2)]
        for ct in range(8):
            dg = dgt[ct % 2]
            DV("tensor_tensor", [R_cb, R_prm], [dg.R()], out=dg.ap[:, NDV:CONVK, :], in0=identb.unsqueeze(1).to_broadcast([128, CONVK - NDV, 128]),
               in1=wd[:, ct, NDV:CONVK].unsqueeze(2).to_broadcast([128, CONVK - NDV, 128]), op=ALU.mult)
            acc = cacc[ct % 2]
            accR = acc.R()
            for smp in (False, True):
                if smp:
                    dst = acc.ap[:, NPT:T].rearrange("p (j i) -> p j i", i=8)
                    rr = RUS[ct]
                    taps = [ubs.ap[:, ct, :, j:j + 8] for j in range(NDV)]
                else:
                    dst = acc.ap[:, 0:NPT]
                    rr = RUB[ct]
                    taps = [ub.ap[:, ct, j:j + NPT] for j in range(NDV)]
                DV("tensor_scalar", [rr, R_prm], [accR], out=dst, in0=taps[0], scalar1=wd[:, ct, 0:1], scalar2=P(l, "bdw")[:, ct:ct + 1],
                   op0=ALU.mult, op1=ALU.add)
                for j in range(1, NDV):
                    DV("scalar_tensor_tensor", [rr, R_prm, accR], [accR], out=dst, in0=taps[j], scalar=wd[:, ct, j:j + 1], in1=dst,
                       op0=ALU.mult, op1=ALU.add)
            for s, (t0, t1) in enumerate(segs):
                n = t1 - t0
                b0, br = PS.get(1)
                for j in range(NDV, CONVK):
                    if s < 2:
                        rhs = ub.ap[:, ct, t0 + j:t0 + j + 512]
                        rr = RUB[ct]
                    else:
                        rhs = ubs.ap[:, ct, :, j:j + 8]
                        rr = RUS[ct]
                    last = j == CONVK - 1
                    MM([dg.R(), rr], (br if j == NDV or last else []), inc=last, out=ps[:, b0, 0:n], lhsT=dg.ap[:, j, :], rhs=rhs,
                       start=(j == NDV), stop=last)
                DV("tensor_tensor", br + [accR], [ycv.R(ct)], out=ycv.ap[:, ct, t0:t1], in0=ps[:, b0, 0:n], in1=acc.ap[:, t0:t1], op=ALU.add)
        ysq = [A.alloc(f"ysq{i}", [128, 512], F32) for i in range(2)]
        UA = SubArena(A, ub, ub.base, ub.nwords)
        mean = UA.alloc("cmean", [128, T], F32)
        rstd = UA.alloc("crstd", [128, T], F32)
        msq = UA.alloc("cmsq", [128, T], F32)
        R_st = mean.R()
        qi = [0]
        for s, (t0, t1) in enumerate(segs):
            n = t1 - t0
            b0, br = PS.get(2)
            for ct in range(8):
                yv = ycv.ap[:, ct, t0:t1]
                sq = ysq[qi[0] % 2]
                qi[0] += 1
                ACT([ycv.R(ct)], [sq.R()], out=sq.ap[:, 0:n], in_=yv, func=AF.Square)
                MM([ycv.R(ct), R_c], [br[0]], out=ps[:, b0, 0:n], lhsT=ones_c, rhs=yv, start=(ct == 0), stop=(ct == 7))
                MM([sq.R(), R_c], [br[1]], out=ps[:, b0 + 1, 0:n], lhsT=ones_c, rhs=sq.ap[:, 0:n], start=(ct == 0), stop=(ct == 7))
            DV("tensor_scalar", [br[0]], [R_st], out=mean.ap[:, t0:t1], in0=ps[:, b0, 0:n], scalar1=1.0 / CW, scalar2=None, op0=ALU.mult)
            DV("tensor_scalar", [br[1]], [R_st], out=rstd.ap[:, t0:t1], in0=ps[:, b0 + 1, 0:n], scalar1=1.0 / CW, scalar2=None, op0=ALU.mult)
        DV("tensor_tensor", [R_st], [msq.R()], out=msq.ap[:, :], in0=mean.ap[:, :], in1=mean.ap[:, :], op=ALU.mult)
        DV("tensor_tensor", [R_st, msq.R()], [R_st], out=rstd.ap[:, :], in0=rstd.ap[:, :], in1=msq.ap[:, :], op=ALU.subtract)
        ACT([R_st, R_c], [R_st], out=rstd.ap[:, :], in_=rstd.ap[:, :], func=AF.Sqrt, bias=epsc)
        DV("reciprocal", [R_st], [R_st], out=rstd.ap[:, :], in_=rstd.ap[:, :])
        swt = [A.alloc(f"swt{i}", [128, 384], BF16) for i in range(2)]
        wi = [0]
        assert conv_base + 4608 >= ub.base + ub.nwords and conv_base + 2 * 4608 <= ycv.base
        kT = arena_alloc_at(A, "kT", [128, 8, T], BF16, conv_base + 4608, [ubs] + dgt)
        kstate["kT"] = kT

        def norm_units():
            for ct in range(8):
                for g, (t0, t1) in enumerate(TG):
                    yv = ycv.ap[:, ct, t0:t1]
                    DV("tensor_tensor", [ycv.R(ct), R_st], [ycv.R(ct)], out=yv, in0=yv, in1=mean.ap[:, t0:t1], op=ALU.subtract)
                    DV("tensor_tensor", [ycv.R(ct), R_st], [ycv.R(ct)], out=yv, in0=yv, in1=rstd.ap[:, t0:t1], op=ALU.mult)
                    sw = swt[wi[0] % 2]
                    wi[0] += 1
                    ACT([ycv.R(ct), R_prm], [sw.R()], out=sw.ap[:, :], in_=yv, func=AF.Silu, scale=P(l, "clng")[:, ct:ct + 1], bias=P(l, "clnb")[:, ct:ct + 1])
                    DV("tensor_tensor", [sw.R(), mixc.R(ct, g)], [mixc.R(ct, g)], out=mixc.ap[:, ct, t0:t1], in0=mixc.ap[:, ct, t0:t1], in1=sw.ap[:, :], op=ALU.mult)
                    yield
        nu = norm_units()
        for grp in range(4):
            wk, wkres = load_w(w_in_d[l][:, OFF_K + grp * GW:OFF_K + (grp + 1) * GW])
            for j in range(2):
                tile_ = grp * 2 + j

                def evac_k(g, pap, bres, tile_=tile_):
                    t0, t1 = TG[g]
                    ACT([bres, R_prm], [kT.R(tile_, g)], out=kT.ap[:, tile_, t0:t1], in_=pap, func=AF.Identity, scale=1.0 / 16.0,
                        bias=P(l, "bk16")[:, tile_:tile_ + 1])
                for _ in proj_fm_gen(wk, wkres, j, evac_k, seg=6):
                    next(nu, None)
        for _ in nu:
            pass
        UA.close()
        A.pop()

        A.push()
        mixm = A.alloc("mixm", [128, 8, T], BF16)
        A.push()
        kT = kstate["kT"]
        arena_adopt(A, kT)
        vp = A.alloc("vp", [128, NTT, NH, 257], BF16)
        Dst = A.alloc("Dst", [128, NH, 2, 257], F32)
        bvb = A.alloc("bvb", [128, MW], F32)
        S.dma("sp", bvb.ap[:, :], bv_d[l].partition_broadcast(128), writes=[bvb.R()], chan="bvb")
        DV("tensor_copy", [R_g], [vp.R(tt, h) for tt in range(NTT) for h in range(NH)], out=vp.ap[:, :, :, 256], in_=eg)
        qTs = A.alloc("qTs", [128, 8, NST], BF16)
        hs = A.alloc("hs", [128, NCH + 1, 256], F32)
        hn = [A.alloc(f"hn{i}", [128, 256], BF16) for i in range(2)]
        atm = [A.alloc(f"atm{i}", [128, 128], BF16) for i in range(2)]
        stt_ = A.alloc("bnst", [128, NCH + 1, 6], F32)
        mvt = A.alloc("bnmv", [128, NCH + 1, 2], F32)
        rdt = A.alloc("rdt", [128, 4], F32)
        A.push()
        vtmp = [A.alloc(f"vtmp{i}", [128, 256], F32) for i in range(2)]
        vi = [0]
        vw = {}

        def vproj_unit(h, tt):
            if tt == 0:
                vw[h] = load_w(w_in_d[l][:, OFF_V + h * GW:OFF_V + (h + 1) * GW])
            wv, wvres = vw[h]
            b0, br = PS.get(1)
            for kt in range(KT):
                last = kt == KT - 1
                MM([wvres.R(0), wvres.R(1), xT.R(tt)], (br if kt == 0 or last else []), inc=last,
                   out=ps[:, b0, 0:256], lhsT=xT.ap[:, kt, tt * 128:(tt + 1) * 128], rhs=wv.ap[:, kt, :], start=(kt == 0), stop=last)
            vt = vtmp[vi[0] % 2]
            vi[0] += 1
            DV("tensor_tensor", br + [bvb.R()], [vt.R()], out=vt.ap[:, :], in0=ps[:, b0, 0:256], in1=bvb.ap[:, h * 256:(h + 1) * 256], op=ALU.add)
            ACT([vt.R(), R_g], [vp.R(tt, h)], out=vp.ap[:, tt, h, 0:256], in_=vt.ap[:, :], func=AF.Identity, scale=eg[:, tt, h:h + 1])

        k2t = [A.alloc(f"k2t{i}", [128, 256], BF16) for i in range(3)]
        k2i = [0]

        def state_step(h, c, Dt, DR):
            tb, tbr = PS.get(1)
            for dt_ in range(2):
                TR([kT.R(h * 2 + dt_, c // 3), R_cb], tbr, out=psb[:, tb, dt_ * 128:(dt_ + 1) * 128], in_=kT.ap[:, h * 2 + dt_, c * 128:(c + 1) * 128])
            k2 = k2t[k2i[0] % 3]
            k2i[0] += 1
            ACT(tbr + [R_g], [k2.R()], out=k2.ap[:, :], in_=psb[:, tb, 0:256], func=AF.Identity, scale=EB[:, h, c:c + 1])
            b0, br = PS.get(2)
            for dt_ in range(2):
                MM([k2.R(), vp.R(c, h)], [br[dt_]], out=ps[:, b0 + dt_, 0:257], lhsT=k2.ap[:, dt_ * 128:(dt_ + 1) * 128], rhs=vp.ap[:, c, h, :],
                   start=True, stop=True)
            DV("scalar_tensor_tensor", br + [R_g, DR], [DR], out=Dt, in0=Dt, scalar=EB[:, h, c:c + 1], in1=ps[:, b0:b0 + 2, 0:257],
               op0=ALU.mult, op1=ALU.add)

        for tt in range(NTT):
            vproj_unit(0, tt)
        for h in range(NH):
            DV("memset", [], [Dst.R(h)], ap=Dst.ap[:, h, :, :], constant=0.0)
            for i in range(NTT):
                if i < NCH:
                    state_step(h, i, Dst.ap[:, h, :, :], Dst.R(h))
                if h + 1 < NH:
                    vproj_unit(h + 1, i)
        DV("tensor_tensor", [R_g], [R_g], out=mfin[0:4, :], in0=minit[0:4, :], in1=Btot[0:4, :], op=ALU.add)
        DV("tensor_tensor", [R_g], [R_g], out=mfin[0:4, :], in0=mfin[0:4, :], in1=Mloc[0:4, :], op=ALU.max)
        S.dma("sp", mpo_d[l], mfin[0:4, :], reads=[R_g], chan="omp")
        ACT([R_g], [R_g], out=emfv[0:4, :], in_=mfin[0:4, :], func=AF.Exp, scale=-1.0)
        bcast4(emfv[0:4, :], 1, EMF, lambda a: a)

        wpre = {("q", 0): load_w(w_in_d[l][:, OFF_Q:OFF_Q + GW]), ("o", 0): load_w(w_in_d[l][:, OFF_O:OFF_O + GW])}
        R_b2 = Res("bnc2")
        R_g2 = Res("gat2")
        Dflat = Dst.ap[:, :, :, :].rearrange("p h d e -> p (h d e)")
        DstR = [Dst.R(h) for h in range(NH)]
        S.dma("sp", bnc2[l][:, :], Dflat, reads=DstR, writes=[R_b2], chan="x2")
        S.op("pool", "collective_compute", dict(kind="AllGather", op=ALU.bypass, replica_groups=GROUPS, ins=[bnc2[l][:, :]], outs=[gat2[l][:, :]]),
             reads=[R_b2], writes=[R_g2], chan=S.chan(f"cc2_{l}"), amount=1)
        S.dma("sp", Dflat, gat2[l][0:128, :], reads=[R_g2], writes=DstR, chan="x2r")
        qT = [A.alloc(f"qT{i}", [128, 2, T], BF16) for i in range(2)]
        Dbf = [A.alloc(f"Dbf{i}", [128, 2, 257], BF16) for i in range(2)]
        zmt = [A.alloc(f"zmt{i}", [128, 384], BF16) for i in range(3)]
        zi = [0]
        cot = [A.alloc(f"co{i}", [128, 2, 257], F32) for i in range(NH)]
        ai = [0]
        di = [0]
        hi_ = [0]

        def den_and_hs(U_bank, ubr, cidx, enb_col):
            R_rd = rdt.R()
            ACT(ubr, [R_rd], out=rdt.ap[:, 0:1], in_=ps[:, U_bank, 256:257], func=AF.Abs)
            DV("tensor_scalar", [R_rd, R_g], [R_rd], out=rdt.ap[:, 1:2], in0=rdt.ap[:, 0:1], scalar1=enb_col, scalar2=None, op0=ALU.max)
            DV("reciprocal", [R_rd], [R_rd], out=rdt.ap[:, 2:3], in_=rdt.ap[:, 1:2])
            ACT(ubr + [R_rd], [hs.R(cidx)], out=hs.ap[:, cidx, :], in_=ps[:, U_bank, 0:256], func=AF.Identity, scale=rdt.ap[:, 2:3])
            DV("bn_stats", [hs.R(cidx)], [stt_.R(cidx)], out=stt_.ap[:, cidx, :], in_=hs.ap[:, cidx, :])
            DV("bn_aggr", [stt_.R(cidx)], [mvt.R()], out=mvt.ap[:, cidx, :], in_=stt_.ap[:, cidx:cidx + 1, :])

        def finish_head(h, cidxs, tok0_of):
            c0, c1 = cidxs[0], cidxs[-1] + 1
            ACT([mvt.R(), R_c], [mvt.R()], out=mvt.ap[:, c0:c1, 1], in_=mvt.ap[:, c0:c1, 1], func=AF.Sqrt, bias=epsc)
            DV("reciprocal", [mvt.R()], [mvt.R()], out=mvt.ap[:, c0:c1, 1], in_=mvt.ap[:, c0:c1, 1])
            for cidx in cidxs:
                hb = hn[hi_[0] % 2]
                hi_[0] += 1
                DV("tensor_scalar", [hs.R(cidx), mvt.R()], [hb.R()], out=hb.ap[:, :], in0=hs.ap[:, cidx, :], scalar1=mvt.ap[:, cidx, 0:1],
                   scalar2=mvt.ap[:, cidx, 1:2], op0=ALU.subtract, op1=ALU.mult)
                tb, tbr = PS.get(1)
                for et in range(2):
                    TR([hb.R(), R_cb], tbr, out=psb[:, tb, et * 128:(et + 1) * 128], in_=hb.ap[:, et * 128:(et + 1) * 128])
                t0 = tok0_of(cidx)
                g = t0 // 384
                mr = [mixm.R(h * 2, g), mixm.R(h * 2 + 1, g)]
                DV("tensor_tensor", tbr + mr, mr, out=mixm.ap[:, h * 2:h * 2 + 2, t0:t0 + 128], in0=mixm.ap[:, h * 2:h * 2 + 2, t0:t0 + 128],
                   in1=psb[:, tb, 0:256].rearrange("p (a t) -> p a t", t=128), op=ALU.mult)

        def proj_tasks(h):
            q = qT[h % 2]
            wcache = {}

            def getw(nm, off):
                if nm not in wcache:
                    if (nm, h) in wpre:
                        wcache[nm] = wpre.pop((nm, h))
                    else:
                        wcache[nm] = load_w(w_in_d[l][:, off + h * GW:off + (h + 1) * GW])
                return wcache[nm]
            tasks = []
            for j in range(2):
                tile_ = h * 2 + j

                def task_q(j=j, tile_=tile_):
                    wq, wqres = getw("q", OFF_Q)

                    def evac_q(g, pap, bres):
                        t0, t1 = TG[g]
                        ACT([bres, R_prm], [q.R(j, g)], out=q.ap[:, j, t0:t1], in_=pap, func=AF.Identity, bias=P(l, "bfm")[:, tile_:tile_ + 1])
                        if g == 2:
                            ACT([bres, R_prm], [qTs.R(tile_)], out=qTs.ap[:, tile_, :], in_=pap[:, 256:384], func=AF.Identity,
                                bias=P(l, "bfm")[:, tile_:tile_ + 1])
                    yield from proj_fm_gen(wq, wqres, j, evac_q)
                tasks.append(task_q)
            for j in range(2):
                tile_ = h * 2 + j

                def task_o(j=j, tile_=tile_):
                    wo, wores = getw("o", OFF_O)

                    def evac_o(g, pap, bres):
                        t0, t1 = TG[g]
                        ACT([bres, R_prm], [mixm.R(tile_, g)], out=mixm.ap[:, tile_, t0:t1], in_=pap, func=AF.Sigmoid,
                            bias=P(l, "bfm")[:, 24 + tile_:25 + tile_])
                    yield from proj_fm_gen(wo, wores, j, evac_o)
                tasks.append(task_o)
            for j in range(2):
                tile_ = h * 2 + j

                def task_zm(j=j, tile_=tile_):
                    wz, wzres = getw("zm", OFF_ZM)

                    def evac_zm(g, pap, bres):
                        t0, t1 = TG[g]
                        zt = zmt[zi[0] % 3]
                        zi[0] += 1
                        ACT([bres, R_prm], [zt.R()], out=zt.ap[:, :], in_=pap, func=AF.Silu, bias=P(l, "bfm")[:, 32 + tile_:33 + tile_])
                        DV("scalar_tensor_tensor", [zt.R(), R_prm, mixm.R(tile_, g)], [mixm.R(tile_, g)], out=mixm.ap[:, tile_, t0:t1], in0=zt.ap[:, :],
                           scalar=P(l, "hng")[:, tile_:tile_ + 1], in1=mixm.ap[:, tile_, t0:t1], op0=ALU.mult, op1=ALU.mult)
                    yield from proj_fm_gen(wz, wzres, j, evac_zm)
                tasks.append(task_zm)
            return tasks

        def run_all(tasks):
            for t_ in tasks:
                for _ in t_():
                    pass

        def seg_stream(tasks):
            for t_ in tasks:
                yield from t_()

        run_all(proj_tasks(0))
        atm8 = [A.alloc(f"atm8_{i}", [128, 128], BF16) for i in range(NCH)]
        k2s8 = [A.alloc(f"k2s8_{i}", [128, 256], BF16) for i in range(NCH)]
        for h in range(NH):
            q = qT[h % 2]
            stream = seg_stream(proj_tasks(h + 1)) if h + 1 < NH else iter(())
            Dt = Dst.ap[:, h, :, :]
            DR = Dst.R(h)
            DV("tensor_scalar", [DR, R_c], [DR], out=Dt, in0=Dt, scalar1=flag, scalar2=None, op0=ALU.mult)
            for c in range(NCH):
                g = c // 3
                tk = slice(c * 128, (c + 1) * 128)
                a0, abr = PS.get(1)
                for dt_ in range(2):
                    MM([kT.R(h * 2 + dt_, g), q.R(dt_, g)], abr, out=ps[:, a0, 0:128], lhsT=kT.ap[:, h * 2 + dt_, tk], rhs=q.ap[:, dt_, tk],
                       start=(dt_ == 0), stop=(dt_ == 1))
                DV("tensor_tensor", abr + [R_c], [atm8[c].R()], out=atm8[c].ap[:, :], in0=ps[:, a0, 0:128], in1=maskp, op=ALU.mult)
                tb, tbr = PS.get(1)
                for dt_ in range(2):
                    TR([kT.R(h * 2 + dt_, g), R_cb], tbr, out=psb[:, tb, dt_ * 128:(dt_ + 1) * 128], in_=kT.ap[:, h * 2 + dt_, tk])
                ACT(tbr + [R_g], [k2s8[c].R()], out=k2s8[c].ap[:, :], in_=psb[:, tb, 0:256], func=AF.Identity, scale=EB[:, h, c:c + 1])
            for c in range(NCH):
                g = c // 3
                tk = slice(c * 128, (c + 1) * 128)
                k2 = k2s8[c]
                b0, br = PS.get(2)
                for dt_ in range(2):
                    MM([k2.R(), vp.R(c, h)], [br[dt_]], out=ps[:, b0 + dt_, 0:257], lhsT=k2.ap[:, dt_ * 128:(dt_ + 1) * 128], rhs=vp.ap[:, c, h, :],
                       start=True, stop=True)
                db = Dbf[di[0] % 2]
                di[0] += 1
                ACT([DR], [db.R()], out=db.ap[:, :, :], in_=Dt, func=AF.Copy)
                DV("scalar_tensor_tensor", br + [R_g, DR], [DR], out=Dt, in0=Dt, scalar=EB[:, h, c:c + 1], in1=ps[:, b0:b0 + 2, 0:257],
                   op0=ALU.mult, op1=ALU.add)
                u0, ubr = PS.get(1)
                MM([atm8[c].R(), vp.R(c, h)], ubr, out=ps[:, u0, 0:257], lhsT=atm8[c].ap[:, :], rhs=vp.ap[:, c, h, :], start=True, stop=False)
                for dt_ in range(2):
                    MM([q.R(dt_, g), db.R()], ubr, out=ps[:, u0, 0:257], lhsT=q.ap[:, dt_, tk], rhs=db.ap[:, dt_, :], start=False, stop=(dt_ == 1))
                den_and_hs(u0, ubr, c, enb[:, c, h:h + 1])
                for _ in range(3):
                    next(stream, None)
            for _ in stream:
                pass
            finish_head(h, list(range(NCH)), lambda cidx: cidx * 128)
            co = cot[h]
            DV("tensor_scalar", [DR, R_g], [co.R()], out=co.ap[:, :, :], in0=Dt, scalar1=EMF[:, h:h + 1], scalar2=None, op0=ALU.mult)
            S.dma("sp", Cp_d[l, h].rearrange("(d p) e -> p d e", p=128), co.ap[:, :, 0:256], reads=[co.R()], chan=f"oCp{h}")
            S.dma("sp", npo_d[l][:, h * 2:h * 2 + 2], co.ap[:, :, 256], reads=[co.R()], chan=f"onp{h}", noncontig=True)

        A.pop()
        A.push()
        XA = SubArena(A, xT, xT.base, xT.nwords)
        kts = A.alloc("kts", [128, MW], BF16)
        for tile_ in range(8):
            tb, tbr = PS.get(1)
            TR([kT.R(tile_, 2), R_cb], tbr, out=psb[:, tb, 0:128], in_=kT.ap[:, tile_, NPT:T])
            ACT(tbr, [kts.R()], out=kts.ap[:, tile_ * 128:(tile_ + 1) * 128], in_=psb[:, tb, 0:128], func=AF.Copy)
        snT = A.alloc("snT", [128, 128], F32)
        S.dma("sp", snT.ap[:, :], snT_d[l], writes=[snT.R()], chan="snT")
        nout = A.alloc("nout", [128, 128], F32)
        KZ = [XA.alloc(f"KZ{i}", [128, NSEQ, 256], BF16) for i in range(2)]
        QZ = [XA.alloc(f"QZ{i}", [128, 2, 2176], BF16) for i in range(2)]
        for i in range(2):
            PL("memset", [], [QZ[i].R()], ap=QZ[i].ap[:, :, :], constant=0.0)
        NCI = 6
        Cin = [A.alloc(f"Cin{i}", [128, 2, 257], F32) for i in range(NCI)]
        Cbf = [A.alloc(f"Cbf{i}", [128, 2, 257], BF16) for i in range(3)]
        Ctm = [A.alloc(f"Ctm{i}", [128, 2, 257], F32) for i in range(2)]
        Cou = [A.alloc(f"Cou{i}", [128, 2, 257], F32) for i in range(4)]
        it = [0]

        def issue_cin(idx):
            if idx >= NH * NSEQ:
                return
            hh, jj = idx // NSEQ, idx % NSEQ
            ci_ = Cin[idx % NCI]
            S.dma("sp", ci_.ap[:, :, 0:256], sC_d[l, jj, hh].rearrange("(d p) e -> p d e", p=128), writes=[ci_.R()], chan=f"ci{idx % NCI}")

        def prep_n(idx):
            if idx >= NH * NSEQ:
                return
            hh, jj = idx // NSEQ, idx % NSEQ
            ci_ = Cin[idx % NCI]
            col_ = (jj * NH + hh) * 2
            DV("tensor_copy", [snT.R()], [ci_.R("n")], out=ci_.ap[:, :, 256], in_=snT.ap[:, col_:col_ + 2])

        def prep_c(idx):
            if idx >= NH * NSEQ:
                return
            hh, jj = idx // NSEQ, idx % NSEQ
            ci_ = Cin[idx % NCI]
            cb_ = Cbf[idx % 3]
            ACT([ci_.R(), ci_.R("n"), R_g], [cb_.R()], out=cb_.ap[:, :, :], in_=ci_.ap[:, :, :], func=AF.Identity, scale=BCs[:, hh, 2, jj:jj + 1])

        for i in range(NCI - 1):
            issue_cin(i)
            prep_n(i)
        prep_c(0)
        ams = {}

        def make_setup(h):
            kz = KZ[h % 2]
            qz = QZ[h % 2]
            pieces = []
            for qq in range(4):
                def kz_piece(qq=qq):
                    DV("tensor_tensor", [kts.R(), R_c], [kz.R()], out=kz.ap[:, qq * 4:(qq + 1) * 4, :],
                       in0=kts.ap[:, h * 256:(h + 1) * 256].unsqueeze(1).to_broadcast([128, 4, 256]),
                       in1=blk[:, qq * 4:(qq + 1) * 4].unsqueeze(2).to_broadcast([128, 4, 256]), op=ALU.mult)
                pieces.append(kz_piece)

            def qz_piece():
                for dt_ in range(2):
                    DV("tensor_copy", [qTs.R(h * 2 + dt_)], [qz.R()], out=qz.ap[:, dt_, :].rearrange("p (j x) -> p j x", x=136)[:, :, 0:8],
                       in_=qTs.ap[:, h * 2 + dt_, :].rearrange("p (j i) -> p j i", i=8))
            pieces.append(qz_piece)

            def at_piece():
                a0, abr = PS.get(1)
                for dt_ in range(2):
                    MM([kT.R(h * 2 + dt_, 2), qTs.R(h * 2 + dt_)], abr, out=ps[:, a0, 0:128], lhsT=kT.ap[:, h * 2 + dt_, NPT:T], rhs=qTs.ap[:, h * 2 + dt_, :],
                       start=(dt_ == 0), stop=(dt_ == 1))
                am_ = A.alloc(f"ams{h}", [128, 128], BF16) if False else amsb[h % 2]
                DV("tensor_tensor", abr + [R_c], [am_.R()], out=am_.ap[:, :], in0=ps[:, a0, 0:128], in1=masks, op=ALU.mult)
                ams[h] = am_
            pieces.append(at_piece)
            return pieces

        amsb = [A.alloc(f"amsb{i}", [128, 128], BF16) for i in range(2)]
        setups = [make_setup(h) for h in range(NH)]
        for p_ in setups[0]:
            p_()
        for h in range(NH):
            kz = KZ[h % 2]
            qz = QZ[h % 2]
            am = ams[h]
            u0, ubr = PS.get(1)
            MM([am.R(), vp.R(NCH, h)], ubr, out=ps[:, u0, 0:257], lhsT=am.ap[:, :], rhs=vp.ap[:, NCH, h, :], start=True, stop=False)
            PS.pinned.add(u0)
            for j in range(NSEQ):
                k_ = it[0]
                it[0] += 1
                ci = Cin[k_ % NCI]
                cb = Cbf[k_ % 3]
                ctm = Ctm[k_ % 2]
                cu = Cou[k_ % 4]
                issue_cin(k_ + NCI - 1)
                prep_n(k_ + NCI - 1)
                if h + 1 < NH and 2 <= j < 2 + len(setups[h + 1]):
                    setups[h + 1][j - 2]()
                col = (j * NH + h) * 2
                prep_c(k_ + 1)
                for dt_ in range(2):
                    MM([qz.R(), cb.R()], ubr, out=ps[:, u0, 0:257], lhsT=qz.ap[:, dt_, j * 128:(j + 1) * 128], rhs=cb.ap[:, dt_, :],
                       start=False, stop=(j == NSEQ - 1 and dt_ == 1))
                b0, br = PS.get(2)
                for dt_ in range(2):
                    MM([kz.R(), vp.R(NCH, h)], [br[dt_]], out=ps[:, b0 + dt_, 0:257], lhsT=kz.ap[:, j, dt_ * 128:(dt_ + 1) * 128], rhs=vp.ap[:, NCH, h, :],
                       start=True, stop=True)
                ACT(br + [R_g], [ctm.R()], out=ctm.ap[:, :, :], in_=ps[:, b0:b0 + 2, 0:257], func=AF.Identity, scale=BCs[:, h, 1, j:j + 1])
                DV("scalar_tensor_tensor", [ci.R(), ci.R("n"), ctm.R(), R_g], [cu.R()], out=cu.ap[:, :, :], in0=ci.ap[:, :, :], scalar=BCs[:, h, 0, j:j + 1],
                   in1=ctm.ap[:, :, :], op0=ALU.mult, op1=ALU.add)
                S.dma("sp", Cs_d[l, j, h].rearrange("(d p) e -> p d e", p=128), cu.ap[:, :, 0:256], reads=[cu.R()], chan=f"cu{k_ % 4}")
                DV("tensor_copy", [cu.R()], [nout.R()], out=nout.ap[:, col:col + 2], in_=cu.ap[:, :, 256])
            PS.pinned.discard(u0)
            den_and_hs(u0, ubr, NCH, enb[:, NCH, h:h + 1])
            finish_head(h, [NCH], lambda cidx: NPT)
        S.dma("sp", nso_d[l], nout.ap[:, :], reads=[nout.R()], chan="ons")
        XA.close()
        A.pop()
        A.pop()

        A.push()
        z = A.alloc("z", [128, NTT, D_MODEL], F32)
        lgb = A.alloc("lgb", [128, D_MODEL], F32)
        lbb = A.alloc("lbb", [128, D_MODEL], F32)
        S.dma("sp", lgb.ap[:, :], lng_d[l].partition_broadcast(128), writes=[lgb.R()], chan="lnp0")
        S.dma("sp", lbb.ap[:, :], lnb_d[l].partition_broadcast(128), writes=[lbb.R()], chan="lnp1")
        for tt in range(NTT):
            if l == 0:
                S.dma("sp", z.ap[:, tt, :], xtok_d[tt * 128:(tt + 1) * 128, :], writes=[z.R(tt)], chan=f"xr{tt}")
            else:
                S.dma("sp", z.ap[:, tt, :], y1_d[tt * 128:(tt + 1) * 128, :], reads=[R_y1[tt]], writes=[z.R(tt)], chan=f"xr{tt}")

        if DEBUG and l == 0:
            dbg_d = nc.dram_tensor("dbg", [128, 16, NST], F32, kind="ExternalOutput").ap()
            XD = SubArena(A, xT, xT.base, xT.nwords)
            dbt = XD.alloc("dbt", [128, 16, NST], F32)
            ACT([mixm.R(k, 2) for k in range(8)], [dbt.R()], out=dbt.ap[:, 0:8, :], in_=mixm.ap[:, :, NPT:T], func=AF.Copy)
            ACT([mixc.R(k, 2) for k in range(8)], [dbt.R()], out=dbt.ap[:, 8:16, :], in_=mixc.ap[:, :, NPT:T], func=AF.Copy)
            S.dma("sp", dbg_d[:, :, :], dbt.ap[:, :, :], reads=[dbt.R()], chan="dbg")
            XD.close()

        def mix_tile(kt, tt):
            src = mixm if kt < 8 else mixc
            return src.ap[:, kt % 8, tt * 128:(tt + 1) * 128], src.R(kt % 8, tt // 3)

        st2 = A.alloc("st2", [128, 2, 4, 6], F32)
        mv2 = A.alloc("mv2", [128, 2, 4], F32)
        ybf = [A.alloc(f"ybf{i}", [128, D_MODEL], BF16) for i in range(2)]

        def ln_gen(tts):
            for tt in tts:
                zt_ = z.ap[:, tt, :]
                zR = z.R(tt)
                sb_ = tt % 2
                sR, mR = st2.R(sb_), mv2.R(sb_)
                mv = mv2.ap[:, sb_, :]
                for q4 in range(4):
                    DV("bn_stats", [zR], [sR], out=st2.ap[:, sb_, q4, :], in_=z.ap[:, tt, q4 * 512:(q4 + 1) * 512])
                    if q4 % 2 == 1:
                        yield
                DV("bn_aggr", [sR], [mR], out=mv[:, 0:2], in_=st2.ap[:, sb_, :, :])
                ACT([mR, R_c], [mR], out=mv[:, 1:2], in_=mv[:, 1:2], func=AF.Sqrt, bias=epsc)
                DV("reciprocal", [mR], [mR], out=mv[:, 1:2], in_=mv[:, 1:2])
                DV("scalar_tensor_tensor", [mR], [mR], out=mv[:, 2:3], in0=mv[:, 0:1], scalar=-1.0, in1=mv[:, 1:2], op0=ALU.mult, op1=ALU.mult)
                yield
                ACT([zR, mR], [zR], out=zt_, in_=zt_, func=AF.Identity, scale=mv[:, 1:2], bias=mv[:, 2:3])
                yield
                DV("tensor_tensor", [zR, lgb.R()], [zR], out=zt_, in0=zt_, in1=lgb.ap[:, :], op=ALU.mult)
                yield
                DV("tensor_tensor", [zR, lbb.R()], [zR], out=zt_, in0=zt_, in1=lbb.ap[:, :], op=ALU.add)
                if l == DEPTH - 1:
                    S.dma("sp", y_d[tt * 128:(tt + 1) * 128, :], zt_, reads=[zR], chan="oy")
                    yield
                else:
                    S.dma("sp", y1_d[tt * 128:(tt + 1) * 128, :], zt_, reads=[zR], writes=[R_y1[tt]], chan=f"oy1_{tt}")
                    yb = ybf[tt % 2]
                    ACT([zR], [yb.R()], out=yb.ap[:, :], in_=zt_, func=AF.Copy)
                    yield
                    for hf in range(2):
                        tb, tbr = PS.get(1)
                        for k8 in range(8):
                            kt = hf * 8 + k8
                            TR([yb.R(), R_cb], tbr, out=psb[:, tb, k8 * 128:(k8 + 1) * 128], in_=yb.ap[:, kt * 128:(kt + 1) * 128])
                        ACT(tbr, [xT.R(tt)], out=xT.ap[:, hf * 8:(hf + 1) * 8, tt * 128:(tt + 1) * 128],
                            in_=psb[:, tb, 0:1024].rearrange("p (k t) -> p k t", t=128), func=AF.Copy)
                        yield

        PASSES = [(0, 3), (3, 6), (6, 9)]
        prev = iter(())
        for (ta, tb_) in PASSES:
            for cg in range(D_MODEL // GW):
                i_w = wstate["i"] % 3
                wstate["i"] += 1
                wo_ = wores_ = wbufs[i_w]
                S.dma("pool", wo_.ap[:, :, :], wob_d[l][:, cg * GW:(cg + 1) * GW].rearrange("(k p) n -> p k n", p=128),
                      reads=[R_wob[l]], writes=[wo_.R(0), wo_.R(1)], chan=f"w{i_w}")
                for tt in range(ta, tb_):
                    b0, br = PS.get(1)
                    for kt in range(KT):
                        mt, mr = mix_tile(kt, tt)
                        last = kt == KT - 1
                        MM([wores_.R(0), wores_.R(1), mr], (br if kt == 0 or last else []), inc=last, out=ps[:, b0, 0:GW], lhsT=mt, rhs=wo_.ap[:, kt, :],
                           start=(kt == 0), stop=last)
                    zv = z.ap[:, tt, cg * GW:(cg + 1) * GW]
                    DV("scalar_tensor_tensor", br + [z.R(tt)], [z.R(tt)], out=zv, in0=zv, scalar=float(DN_ALPHA), in1=ps[:, b0, 0:GW],
                       op0=ALU.mult, op1=ALU.add)
                    next(prev, None)
            for _ in prev:
                pass
            prev = ln_gen(range(ta, tb_))
        for _ in prev:
            pass
        A.pop()
        A.pop()
        A.pop()

    for l in range(DEPTH):
        layer(l)

    final = {k: v for k, v in S.cnt.items() if k.startswith("D_") and v > 0}
    S.ops["sp"].append((final, None, None))
    S.emit()
    return nc, A.peak


def _get_program():
    nc, _peak = build_program()
    return nc


def kernel(x_prompt, x_sample, state_C, state_n, state_m, state_conv, w_in, b_in, hn_g, w_dw, b_dw,
           cln_g, cln_b, w_out, ln_g, ln_b):
    f32 = np.float32
    x_prompt = np.asarray(x_prompt, f32)
    x_sample = np.asarray(x_sample, f32)
    state_C = np.asarray(state_C, f32)
    state_n = np.asarray(state_n, f32)
    state_m = np.asarray(state_m, f32)
    state_conv = np.asarray(state_conv, f32)
    w_in = np.ascontiguousarray(np.asarray(w_in, f32))
    w_out = np.ascontiguousarray(np.asarray(w_out, f32))
    b_in = np.asarray(b_in, f32)
    hn_g = np.asarray(hn_g, f32)
    w_dw = np.asarray(w_dw, f32)
    b_dw = np.asarray(b_dw, f32)
    cln_g = np.asarray(cln_g, f32)
    cln_b = np.asarray(cln_b, f32)
    ln_g = np.asarray(ln_g, f32)
    ln_b = np.asarray(ln_b, f32)

    def fm(v, ntile):
        return np.ascontiguousarray(v.reshape(DEPTH, ntile, 128).transpose(0, 2, 1))

    shared = {
        "w_in": w_in, "w_out": w_out,
        "bfm": fm(b_in[:, :8192], 64),
        "bg": np.ascontiguousarray(b_in[:, 8192:8200].reshape(DEPTH, 1, 8)),
        "bv": np.ascontiguousarray(b_in[:, OFF_V:OFF_V + MW].reshape(DEPTH, 1, MW)),
        "hng": fm(hn_g, 8),
        "wdw": np.ascontiguousarray(w_dw.reshape(DEPTH, CONVK, 8, 128).transpose(0, 3, 2, 1)),
        "bdw": fm(b_dw, 8), "clng": fm(cln_g, 8), "clnb": fm(cln_b, 8),
        "lng": np.ascontiguousarray(ln_g.reshape(DEPTH, 1, D_MODEL)),
        "lnb": np.ascontiguousarray(ln_b.reshape(DEPTH, 1, D_MODEL)),
    }
    in_maps = []
    for c in range(8):
        b, hf = c // 2, c % 2
        xp = x_prompt[b, hf * NPT:(hf + 1) * NPT, :]
        xs = x_sample[c * NSEQ:(c + 1) * NSEQ].reshape(NST, D_MODEL)
        xtok = np.ascontiguousarray(np.concatenate([xp, xs], 0))
        sl = slice(c * NSEQ, (c + 1) * NSEQ)
        sn = state_n[:, sl]
        snT = sn.reshape(DEPTH, NSEQ, NH, 2, 128).transpose(0, 4, 1, 2, 3).reshape(DEPTH, 128, 128)
        scv = state_conv[:, sl]
        scvT = scv.reshape(DEPTH, NSEQ, 30, 8, 128).transpose(0, 4, 3, 1, 2)
        scvK = np.ascontiguousarray(scvT[:, :, :, :, 8:30]).reshape(DEPTH, 128, 8 * NSEQ * 22)
        m = dict(shared)
        m.update({
            "xT": np.ascontiguousarray(xtok.T), "xtok": xtok,
            "sC": np.ascontiguousarray(state_C[:, sl]),
            "snT": np.ascontiguousarray(snT),
            "smT": np.ascontiguousarray(state_m[:, sl].transpose(0, 2, 1)),
            "scvT": np.ascontiguousarray(scvT).reshape(DEPTH, 128, 8 * NSEQ * 30), "scvK": scvK,
            "flag": np.full((128, 1), float(hf), f32),
        })
        in_maps.append(m)
    nc = _get_program()
    res = run_bass_kernel_spmd(nc, in_maps, core_ids=list(range(8)))
    R = res.results
    B = x_prompt.shape[0]
    y_prompt = np.empty_like(x_prompt)
    y_sample = np.empty_like(x_sample)
    Cp = np.empty((DEPTH, B, NH, DH, DH), f32)
    npr = np.empty((DEPTH, B, NH, DH), f32)
    mp = np.empty((DEPTH, B, NH), f32)
    cp = np.empty((DEPTH, B, CONVK - 1, CW), f32)
    Cs = np.empty((DEPTH, 128, NH, DH, DH), f32)
    ns = np.empty((DEPTH, 128, NH, DH), f32)
    ms = np.empty((DEPTH, 128, NH), f32)
    cs = np.empty((DEPTH, 128, CONVK - 1, CW), f32)
    for c in range(8):
        r = R[c]
        b, hf = c // 2, c % 2
        y = np.asarray(r["y"], f32)
        y_prompt[b, hf * NPT:(hf + 1) * NPT] = y[:NPT]
        y_sample[c * NSEQ:(c + 1) * NSEQ] = y[NPT:].reshape(NSEQ, 8, D_MODEL)
        sl = slice(c * NSEQ, (c + 1) * NSEQ)
        Cs[:, sl] = np.asarray(r["Cs"], f32)
        ns[:, sl] = np.asarray(r["nso"], f32).reshape(DEPTH, 128, NSEQ, NH, 2).transpose(0, 2, 3, 4, 1).reshape(DEPTH, NSEQ, NH, DH)
        ms[:, sl] = np.asarray(r["mso"], f32).transpose(0, 2, 1)
        cso = np.concatenate([np.asarray(r["cso_old"], f32).reshape(DEPTH, 128, 8, NSEQ, 22),
                              np.asarray(r["cso_new"], f32).reshape(DEPTH, 128, 8, NSEQ, 8)], axis=4)
        cs[:, sl] = cso.transpose(0, 3, 4, 2, 1).reshape(DEPTH, NSEQ, 30, CW)
        if hf == 1:
            Cp[:, b] = np.asarray(r["Cp"], f32)
            npr[:, b] = np.asarray(r["npo"], f32).reshape(DEPTH, 128, NH, 2).transpose(0, 2, 3, 1).reshape(DEPTH, NH, DH)
            mp[:, b] = np.asarray(r["mpo"], f32).reshape(DEPTH, NH)
            cp[:, b] = np.asarray(r["cpo"], f32).transpose(0, 3, 2, 1).reshape(DEPTH, 30, CW)
    return (y_prompt, y_sample, Cp, npr, mp, cp, Cs, ns, ms, cs)
```

```python
import numpy as np
import concourse.bass as bass
import concourse.mybir as mybir
from concourse.bass_utils import run_bass_kernel_spmd

F32 = mybir.dt.float32
BF16 = mybir.dt.bfloat16
AF = mybir.ActivationFunctionType
ALU = mybir.AluOpType
AX = mybir.AxisListType

D_MODEL = 2048
DEPTH = 2
NH = 4
DH = 256
MW = 1024
CW = 1024
CONVK = 31
PROJ = 8200
NPT = 1024
NST = 128
T = NPT + NST
NTT = T // 128
NCH = NPT // 128
KT = D_MODEL // 128
NSEQ = 16
LN_EPS = 1e-5
DN_ALPHA = (2 * DEPTH) ** 0.25
GW = 256
NEG_BIG = -1.0e30
import os
DEBUG = bool(int(os.environ.get('KDEBUG', '0')))

OFF_Q, OFF_K, OFF_V, OFF_O, OFF_ZM = 0, 1024, 2048, 3072, 4096
OFF_GA, OFF_GG, OFF_ZC, OFF_GATE = 5120, 6144, 7168, 8192


class Res:
    __slots__ = ("name", "w", "r")

    def __init__(self, name, inherit=None):
        self.name = name
        self.w = None
        self.r = dict(inherit) if inherit else {}


class Sched:
    ENGS = ("pe", "act", "dve", "pool", "sp")

    def __init__(self, nc):
        self.nc = nc
        self.ops = {e: [] for e in self.ENGS}
        self.sems = {}
        self.cnt = {}
        self.waited = {e: {} for e in self.ENGS}
        self._ctx = []
        for e in self.ENGS:
            self._mksem("E_" + e)

    def _mksem(self, key):
        cm = self.nc.semaphore(key)
        h = cm.__enter__()
        self._ctx.append(cm)
        self.sems[key] = h
        self.cnt[key] = 0
        return h

    def chan(self, name):
        key = "D_" + name
        if key not in self.sems:
            self._mksem(key)
        return key

    def _need(self, eng, tick, waits):
        if tick is None:
            return
        key, val = tick
        if self.waited[eng].get(key, 0) >= val:
            return
        self.waited[eng][key] = val
        waits[key] = max(waits.get(key, 0), val)

    def op(self, eng, method, kw, reads=(), writes=(), inc=True, chan=None, amount=16, noncontig=False):
        fn = (method, kw, noncontig)
        own = "E_" + eng
        waits = {}
        same = chan is None
        for r in reads:
            self._need(eng, r.w, waits)
        for w in writes:
            strict = not (same and eng == "pe")
            if w.w is not None and (strict or w.w[0] != own):
                self._need(eng, w.w, waits)
            for k, v in w.r.items():
                if not strict and k == own:
                    continue
                self._need(eng, (k, v), waits)
        if chan is not None:
            self.cnt[chan] += amount
            tick = (chan, self.cnt[chan])
            self.ops[eng].append((waits, fn, (chan, amount)))
        elif inc:
            self.cnt[own] += 1
            tick = (own, self.cnt[own])
            self.ops[eng].append((waits, fn, (own, 1)))
        else:
            tick = (own, self.cnt[own] + 1)
            self.ops[eng].append((waits, fn, None))
        for r in reads:
            if r.r.get(tick[0], 0) < tick[1]:
                r.r[tick[0]] = tick[1]
        for w in writes:
            w.w = tick
            w.r = {}
        return tick

    def dma(self, eng, out, in_, reads=(), writes=(), chan="x", noncontig=False):
        ck = self.chan(chan)
        return self.op(eng, "dma_start", dict(out=out, in_=in_), reads=reads, writes=writes, chan=ck, noncontig=noncontig)

    def wait_all(self, eng, ress):
        waits = {}
        for r in ress:
            self._need(eng, r.w, waits)
            for k, v in r.r.items():
                self._need(eng, (k, v), waits)
        self.ops[eng].append((waits, None, None))

    def emit(self):
        nc = self.nc
        getters = {"pe": "tensor", "act": "scalar", "dve": "vector", "pool": "gpsimd", "sp": "sync"}
        with nc.Block() as block:
            for e in self.ENGS:
                ops = self.ops[e]
                if not ops:
                    continue

                def body(engh, ops=ops):
                    for waits, fn, inc in ops:
                        for k, v in waits.items():
                            engh.wait_ge(self.sems[k], v)
                        if fn is None:
                            continue
                        method, kw, noncontig = fn
                        if noncontig:
                            with nc.allow_non_contiguous_dma(reason="tiny strided transfer"):
                                ins = getattr(engh, method)(**kw)
                        else:
                            ins = getattr(engh, method)(**kw)
                        if inc is not None:
                            ins.then_inc(self.sems[inc[0]], inc[1])

                getattr(block, getters[e])(body)


class Arena:
    def __init__(self, nc, words):
        self.words = words
        self.t = nc.alloc_sbuf_tensor("arena", [128, words], F32)
        self.top = 0
        self.stack = []
        self.live = []
        self.dead = []
        self.peak = 0

    def push(self):
        self.stack.append((self.top, len(self.live)))

    def pop(self):
        top, nlive = self.stack.pop()
        for (s, e, ress) in self.live[nlive:]:
            ticks = {}
            for r in ress:
                if r.w is not None:
                    ticks[r.w[0]] = max(ticks.get(r.w[0], 0), r.w[1])
                for k, v in r.r.items():
                    ticks[k] = max(ticks.get(k, 0), v)
            self.dead.append((s, e, ticks))
        del self.live[nlive:]
        self.top = top

    def alloc(self, name, shape, dt):
        n = int(np.prod(shape[1:]))
        words = n if dt == F32 else (n + 1) // 2
        words = (words + 7) // 8 * 8
        a0 = self.top
        a1 = a0 + words
        if a1 > self.words:
            raise RuntimeError(f"SBUF arena overflow allocating {name}: need {a1} words of {self.words}")
        self.top = a1
        self.peak = max(self.peak, a1)
        inherit = {}
        keep = []
        for (s, e, ticks) in self.dead:
            if s < a1 and a0 < e:
                for k, v in ticks.items():
                    inherit[k] = max(inherit.get(k, 0), v)
                if not (a0 <= s and e <= a1):
                    keep.append((s, e, ticks))
            else:
                keep.append((s, e, ticks))
        self.dead = keep
        v = self.t[:, a0:a1]
        if dt != F32:
            v = v.bitcast(dt)
        v = v[:, 0:n]
        if len(shape) == 3:
            v = v.rearrange("p (a b) -> p a b", b=shape[2])
        elif len(shape) == 4:
            v = v.rearrange("p (a b c) -> p a b c", b=shape[2], c=shape[3])
        ress = []
        self.live.append((a0, a1, ress))
        t_ = Tile(name, v, ress, inherit)
        t_.base, t_.nwords = a0, words
        return t_


def _ticks_of(tiles):
    t = {}
    for tl in tiles:
        for r in tl._ress:
            if r.w is not None:
                t[r.w[0]] = max(t.get(r.w[0], 0), r.w[1])
            for k, v in r.r.items():
                t[k] = max(t.get(k, 0), v)
    return t


def arena_alloc_at(arena, name, shape, dt, a0, over_tiles):
    n = int(np.prod(shape[1:]))
    words = n if dt == F32 else (n + 1) // 2
    words = (words + 7) // 8 * 8
    v = arena.t[:, a0:a0 + words]
    if dt != F32:
        v = v.bitcast(dt)
    v = v[:, 0:n]
    if len(shape) == 3:
        v = v.rearrange("p (a b) -> p a b", b=shape[2])
    inh = _ticks_of(over_tiles)
    for tl in over_tiles:
        for k, vv in (tl._inh or {}).items():
            inh[k] = max(inh.get(k, 0), vv)
    for (s_, e_, ticks) in arena.dead:
        if s_ < a0 + words and a0 < e_:
            for k, vv in ticks.items():
                inh[k] = max(inh.get(k, 0), vv)
    t_ = Tile(name, v, [], inh)
    t_.base, t_.nwords = a0, words
    return t_


def arena_adopt(arena, tile):
    assert arena.top == tile.base, (arena.top, tile.base)
    arena.top += tile.nwords
    arena.peak = max(arena.peak, arena.top)
    arena.live.append((tile.base, tile.base + tile.nwords, tile._ress))


class SubArena:
    def __init__(self, arena, tile, base_words, nwords):
        self.arena = arena
        self.tile = tile
        self.base = base_words
        self.nwords = nwords
        self.top = 0
        self.inherit = dict(tile._inh) if tile._inh else {}
        for r in tile._ress:
            if r.w is not None:
                self.inherit[r.w[0]] = max(self.inherit.get(r.w[0], 0), r.w[1])
            for k, v in r.r.items():
                self.inherit[k] = max(self.inherit.get(k, 0), v)
        self.ress = []

    def alloc(self, name, shape, dt):
        n = int(np.prod(shape[1:]))
        words = n if dt == F32 else (n + 1) // 2
        words = (words + 7) // 8 * 8
        a0 = self.base + self.top
        a1 = a0 + words
        if self.top + words > self.nwords:
            raise RuntimeError(f"sub-arena overflow allocating {name}")
        self.top += words
        v = self.arena.t[:, a0:a1]
        if dt != F32:
            v = v.bitcast(dt)
        v = v[:, 0:n]
        if len(shape) == 3:
            v = v.rearrange("p (a b) -> p a b", b=shape[2])
        elif len(shape) == 4:
            v = v.rearrange("p (a b c) -> p a b c", b=shape[2], c=shape[3])
        return Tile(name, v, self.ress, self.inherit)

    def close(self):
        ticks = {}
        for r in self.ress:
            if r.w is not None:
                ticks[r.w[0]] = max(ticks.get(r.w[0], 0), r.w[1])
            for k, v in r.r.items():
                ticks[k] = max(ticks.get(k, 0), v)
        for r in self.tile._ress:
            for k, v in ticks.items():
                if r.r.get(k, 0) < v:
                    r.r[k] = v


class Tile:
    def __init__(self, name, ap, ress, inherit):
        self.name = name
        self.ap = ap
        self._ress = ress
        self._inh = inherit
        self._d = {}

    def R(self, *idx):
        r = self._d.get(idx)
        if r is None:
            r = Res(f"{self.name}{idx}", self._inh)
            self._d[idx] = r
            self._ress.append(r)
        return r

    def __getitem__(self, k):
        return self.ap[k]


class PSum:
    def __init__(self, nc):
        self.t = nc.alloc_psum_tensor("ps", [128, 8, 512], F32)
        self.tb = self.t.bitcast(BF16)
        self.res = [Res(f"bank{i}") for i in range(8)]
        self.ptr = 0
        self.pinned = set()

    def get(self, n=1):
        for _ in range(16):
            if self.ptr + n > 8:
                self.ptr = 0
            b = self.ptr
            if any((b + i) in self.pinned for i in range(n)):
                self.ptr = (b + 1) % 8
                continue
            self.ptr = (self.ptr + n) % 8
            return b, self.res[b:b + n]
        raise RuntimeError("no free PSUM banks")


def build_program():
    nc = bass.Bass("TRN2", target_bir_lowering=False)
    S = Sched(nc)

    def din(name, shape):
        return nc.dram_tensor(name, shape, F32, kind="ExternalInput").ap()

    def dout(name, shape):
        return nc.dram_tensor(name, shape, F32, kind="ExternalOutput").ap()

    xT_d = din("xT", [D_MODEL, T])
    xtok_d = din("xtok", [T, D_MODEL])
    sC_d = din("sC", [DEPTH, NSEQ, NH, DH, DH])
    snT_d = din("snT", [DEPTH, 128, 128])
    smT_d = din("smT", [DEPTH, NH, NSEQ])
    scvT_d = din("scvT", [DEPTH, 128, 8 * NSEQ * 30])
    scvK_d = din("scvK", [DEPTH, 128, 8 * NSEQ * 22])
    w_in_d = din("w_in", [DEPTH, D_MODEL, PROJ])
    w_out_d = din("w_out", [DEPTH, D_MODEL, D_MODEL])
    bfm_d = din("bfm", [DEPTH, 128, 64])
    bg_d = din("bg", [DEPTH, 1, 8])
    bv_d = din("bv", [DEPTH, 1, MW])
    hng_d = din("hng", [DEPTH, 128, 8])
    wdw_d = din("wdw", [DEPTH, 128, 8, CONVK])
    bdw_d = din("bdw", [DEPTH, 128, 8])
    clng_d = din("clng", [DEPTH, 128, 8])
    clnb_d = din("clnb", [DEPTH, 128, 8])
    lng_d = din("lng", [DEPTH, 1, D_MODEL])
    lnb_d = din("lnb", [DEPTH, 1, D_MODEL])
    flag_d = din("flag", [128, 1])
    y_d = dout("y", [T, D_MODEL])
    Cp_d = dout("Cp", [DEPTH, NH, DH, DH])
    npo_d = dout("npo", [DEPTH, 128, 8])
    mpo_d = dout("mpo", [DEPTH, NH, 1])
    cpo_d = dout("cpo", [DEPTH, 128, 8, 30])
    Cs_d = dout("Cs", [DEPTH, NSEQ, NH, DH, DH])
    nso_d = dout("nso", [DEPTH, 128, 128])
    mso_d = dout("mso", [DEPTH, NH, NSEQ])
    cso_old_d = dout("cso_old", [DEPTH, 128, 8 * NSEQ * 22])
    cso_new_d = dout("cso_new", [DEPTH, 128, 8 * NSEQ * 8])

    y1_d = nc.dram_tensor("y1scr", [T, D_MODEL], F32).ap()
    R_y1 = [Res(f"y1_{tt}") for tt in range(NTT)]
    bnc1 = [nc.dram_tensor(f"bnc1_{l}", [128, 248], F32).ap() for l in range(DEPTH)]
    gat1 = [nc.dram_tensor(f"gat1_{l}", [256, 248], F32).ap() for l in range(DEPTH)]
    bnc2 = [nc.dram_tensor(f"bnc2_{l}", [128, 2056], F32).ap() for l in range(DEPTH)]
    gat2 = [nc.dram_tensor(f"gat2_{l}", [256, 2056], F32).ap() for l in range(DEPTH)]
    wob_d = [nc.dram_tensor(f"wob{l}", [D_MODEL, D_MODEL], BF16).ap() for l in range(DEPTH)]
    R_wob = [Res(f"wob{l}") for l in range(DEPTH)]
    GROUPS = [[0, 1], [2, 3], [4, 5], [6, 7]]

    A = Arena(nc, 52800)
    PS = PSum(nc)
    ps = PS.t
    psb = PS.tb

    def ACT(R, W, **kw):
        S.op("act", "activation", kw, R, W)

    def DV(m, R, W, **kw):
        S.op("dve", m, kw, R, W)

    def PL(m, R, W, **kw):
        S.op("pool", m, kw, R, W)

    def MM(R, W, inc=True, **kw):
        S.op("pe", "matmul", kw, R, W, inc=inc)

    def TR(R, W, out, in_):
        S.op("pe", "transpose", dict(out=out, in_=in_, identity=identb), R, W)

    cst = A.alloc("cst", [128, 544], F32)
    c_off = [0]

    def cslice(n):
        a = c_off[0]
        c_off[0] += n
        return cst.ap[:, a:a + n]

    ident = cslice(128)
    maskp = cslice(128)
    masks = cslice(128)
    blk = cslice(16)
    ones_c = cslice(128)
    i4 = cslice(4)
    epsc = cslice(1)
    flag = cslice(1)
    R_c = cst.R()
    cstb = A.alloc("cstb", [128, 128], BF16)
    identb = cstb.ap[:, 0:128]
    R_cb = cstb.R()

    PL("memset", [], [R_c], ap=cst.ap[:, 0:544], constant=0.0)
    PL("memset", [], [R_c], ap=ident, constant=1.0)
    PL("affine_select", [R_c], [R_c], out=ident, in_=ident, pattern=[[-1, 128]], compare_op=ALU.is_equal, fill=0.0, base=0, channel_multiplier=1)
    PL("memset", [], [R_c], ap=maskp, constant=1.0)
    PL("affine_select", [R_c], [R_c], out=maskp, in_=maskp, pattern=[[1, 128]], compare_op=ALU.is_ge, fill=0.0, base=0, channel_multiplier=-1)
    PL("memset", [], [R_c], ap=blk, constant=1.0)
    PL("affine_select", [R_c], [R_c], out=blk, in_=blk, pattern=[[-8, 16]], compare_op=ALU.is_ge, fill=0.0, base=0, channel_multiplier=1)
    PL("affine_select", [R_c], [R_c], out=blk, in_=blk, pattern=[[8, 16]], compare_op=ALU.is_ge, fill=0.0, base=7, channel_multiplier=-1)
    PL("tensor_tensor", [R_c], [R_c], out=masks.rearrange("p (j i) -> p j i", i=8), in0=maskp.rearrange("p (j i) -> p j i", i=8),
       in1=blk.unsqueeze(2).to_broadcast([128, 16, 8]), op=ALU.mult)
    PL("memset", [], [R_c], ap=ones_c, constant=1.0)
    PL("tensor_copy", [R_c], [R_c], out=i4[0:4, :], in_=ident[0:4, 0:4])
    PL("memset", [], [R_c], ap=epsc, constant=LN_EPS)
    S.dma("sp", flag, flag_d[:, :], writes=[R_c], chan="flag")
    PL("tensor_copy", [R_c], [R_cb], out=identb, in_=ident)

    PO = {}
    o = 0
    for nm, n in (("bfm", 64), ("hng", 8), ("bk16", 8), ("wdw", 8 * CONVK), ("bdw", 8), ("clng", 8), ("clnb", 8), ("bg", 8)):
        PO[nm] = (o, n)
        o += n
    prm = A.alloc("prm", [128, DEPTH, o], F32)
    R_prm = prm.R()

    def P(l, nm):
        a, n = PO[nm]
        return prm.ap[:, l, a:a + n]

    for l in range(DEPTH):
        S.dma("sp", P(l, "bfm"), bfm_d[l], writes=[R_prm], chan="cst")
        S.dma("sp", P(l, "hng"), hng_d[l], writes=[R_prm], chan="cst")
        S.dma("sp", P(l, "wdw"), wdw_d[l].rearrange("p c j -> p (c j)"), writes=[R_prm], chan="cst")
        S.dma("sp", P(l, "bdw"), bdw_d[l], writes=[R_prm], chan="cst")
        S.dma("sp", P(l, "clng"), clng_d[l], writes=[R_prm], chan="cst")
        S.dma("sp", P(l, "clnb"), clnb_d[l], writes=[R_prm], chan="cst")
        S.dma("sp", P(l, "bg"), bg_d[l].partition_broadcast(128), writes=[R_prm], chan="cst")
    for l in range(DEPTH):
        DV("tensor_scalar", [R_prm], [R_prm], out=P(l, "bk16"), in0=P(l, "bfm")[:, 8:16], scalar1=1.0 / 16.0, scalar2=None, op0=ALU.mult)

    xT = A.alloc("xT", [128, KT, T], BF16)
    wbufs = [A.alloc(f"wb{i}", [128, KT, GW], BF16) for i in range(3)]
    wstate = {"i": 0}

    S.dma("pool", xT.ap[:, :, :], xT_d.rearrange("(k p) t -> p k t", p=128), writes=[xT.R(tt) for tt in range(NTT)], chan="xT")

    def load_w(src_ap, ncols=GW):
        i = wstate["i"] % 3
        wstate["i"] += 1
        wb = wbufs[i]
        S.dma("pool", wb.ap[:, :, 0:ncols], src_ap.rearrange("(k p) n -> p k n", p=128), writes=[wb.R(0), wb.R(1)], chan=f"w{i}")
        return wb, wb

    def xT_res(t0, t1):
        return [xT.R(tt) for tt in range(t0 // 128, (t1 + 127) // 128)]

    TG = [(0, 384), (384, 768), (768, 1152)]

    def proj_fm(wb, wres, j, evac):
        b0, bres = PS.get(3)
        for kt in range(KT):
            for g, (t0, t1) in enumerate(TG):
                last = (kt == KT - 1 and g == 2)
                MM([wres.R(j)] + xT_res(t0, t1), ([bres[g]] if kt == 0 else []) + (bres if last else []), inc=last,
                   out=ps[:, b0 + g, 0:384], lhsT=wb.ap[:, kt, j * 128:(j + 1) * 128], rhs=xT.ap[:, kt, t0:t1],
                   start=(kt == 0), stop=(kt == KT - 1))
        for g in range(3):
            evac(g, ps[:, b0 + g, 0:384], bres[g])

    def proj_fm_gen(wb, wres, j, evac, seg=4):
        b0, bres = PS.get(3)
        PS.pinned.update((b0, b0 + 1, b0 + 2))
        for kt in range(KT):
            for g, (t0, t1) in enumerate(TG):
                last = (kt == KT - 1 and g == 2)
                MM([wres.R(j)] + xT_res(t0, t1), ([bres[g]] if kt == 0 else []) + (bres if last else []), inc=last,
                   out=ps[:, b0 + g, 0:384], lhsT=wb.ap[:, kt, j * 128:(j + 1) * 128], rhs=xT.ap[:, kt, t0:t1],
                   start=(kt == 0), stop=(kt == KT - 1))
            if kt % seg == seg - 1 and kt != KT - 1:
                yield
        PS.pinned.difference_update((b0, b0 + 1, b0 + 2))
        for g in range(3):
            evac(g, ps[:, b0 + g, 0:384], bres[g])
        yield

    def layer(l):
        A.push()
        kstate = {}
        mixc = A.alloc("mixc", [128, 8, T], BF16)

        gt = A.alloc("gates", [128, 1100], F32)
        g_off = [0]

        def gsl(n):
            a = g_off[0]
            g_off[0] += n
            assert g_off[0] <= 1100
            return gt.ap[:, a:a + n]

        Gt = gsl(72).rearrange("p (t c) -> p t c", c=8)
        Lt = gsl(36).rearrange("p (t c) -> p t c", c=4)
        gg_ = gsl(36).rearrange("p (t c) -> p t c", c=4)
        nb = gsl(36).rearrange("p (t c) -> p t c", c=4)
        eg = gsl(36).rearrange("p (t c) -> p t c", c=4)
        enb = gsl(36).rearrange("p (t c) -> p t c", c=4)
        Et = gsl(36).rearrange("p (t c) -> p t c", c=4)
        EB = gsl(32).rearrange("p (h c) -> p h c", c=8)
        BCs = gsl(192).rearrange("p (h v j) -> p h v j", v=3, j=16)
        EMF = gsl(4)
        Bs = gsl(24)
        Gmx = gsl(24)
        eBv = gsl(8)
        Mloc = gsl(1)
        Btot = gsl(1)
        mA = gsl(1)
        minit = gsl(1)
        mfin = gsl(1)
        emfv = gsl(1)
        mprev = gsl(16)
        mnew = gsl(16)
        V3f = gsl(48)
        V3 = V3f.rearrange("p (v j) -> p v j", j=16)
        tmp4 = gsl(16)
        X4 = gsl(192)
        R_g = gt.R()

        def bcast4(src, n, dst, dst_view):
            X = X4[0:4, 0:4 * n].rearrange("p (h n) -> p h n", n=n)
            DV("tensor_tensor", [R_g, R_c], [R_g], out=X, in0=src.unsqueeze(1).to_broadcast([4, 4, n]),
               in1=i4[0:4, :].unsqueeze(2).to_broadcast([4, 4, n]), op=ALU.mult)
            b0, br = PS.get(1)
            MM([R_g, R_c], br, out=ps[:, b0, 0:4 * n], lhsT=ones_c[0:4, :], rhs=X4[0:4, 0:4 * n], start=True, stop=True)
            ACT(br, [R_g], out=dst, in_=dst_view(ps[:, b0, 0:4 * n]), func=AF.Copy)

        def gate_chain():
            wg, wgres = load_w(w_in_d[l][:, OFF_GATE:OFF_GATE + 8], ncols=8)
            bg0, bgres = PS.get(1)
            for tt in range(NTT):
                for kt in range(KT):
                    last = (tt == NTT - 1 and kt == KT - 1)
                    MM([wgres.R(0), xT.R(tt)], (bgres if (tt == 0 and kt == 0) or last else []), inc=last,
                       out=ps[:, bg0, tt * 8:(tt + 1) * 8], lhsT=xT.ap[:, kt, tt * 128:(tt + 1) * 128], rhs=wg.ap[:, kt, 0:8],
                       start=(kt == 0), stop=(kt == KT - 1))
            DV("tensor_tensor", bgres + [R_prm], [R_g], out=Gt, in0=ps[:, bg0, 0:72].rearrange("p (t c) -> p t c", c=8),
               in1=P(l, "bg").unsqueeze(1).to_broadcast([128, NTT, 8]), op=ALU.add)
            ACT([R_g], [R_g], out=Et, in_=Gt[:, :, 4:8], func=AF.Exp, scale=-1.0)
            ACT([R_g], [R_g], out=Lt, in_=Et, func=AF.Ln, bias=1.0)
            bc0, bcres = PS.get(1)
            for tt in range(NTT):
                m_ = maskp if tt < NCH else masks
                MM([R_c, R_g], bcres, out=ps[:, bc0, tt * 4:(tt + 1) * 4], lhsT=m_, rhs=Lt[:, tt, :], start=True, stop=True)
            csv = ps[:, bc0, 0:36].rearrange("p (t c) -> p t c", c=4)
            DV("tensor_tensor", bcres + [R_g], [R_g], out=gg_, in0=Gt[:, :, 0:4], in1=csv, op=ALU.add)
            ACT(bcres, [R_g], out=nb, in_=csv, func=AF.Copy)
            ACT([R_g], [R_g], out=eg, in_=gg_, func=AF.Exp)
            ACT([R_g], [R_g], out=enb, in_=nb, func=AF.Exp)
            bs0, bsres = PS.get(1)
            for c in range(NCH):
                MM([R_c, R_g], bsres, out=ps[0:4, bs0, c:c + 1], lhsT=Lt[:, c, :], rhs=ones_c[:, 0:1], start=True, stop=True)
            MM([R_c, R_g], bsres, out=ps[0:4, bs0, 8:24], lhsT=Lt[:, NCH, :], rhs=blk, start=True, stop=True)
            DV("tensor_copy", bsres, [R_g], out=Bs[0:4, :], in_=ps[0:4, bs0, 0:24])
            gT0, gTres = PS.get(3)
            for tt in range(NTT):
                bnk, col = gT0 + tt // 4, (tt % 4) * 128
                MM([R_c, R_g], [gTres[tt // 4]], out=ps[0:4, bnk, col:col + 128], lhsT=gg_[:, tt, :], rhs=ident, start=True, stop=True)
            for hb in range(2):
                DV("tensor_reduce", [gTres[hb]], [R_g], out=Gmx[0:4, hb * 4:(hb + 1) * 4],
                   in_=ps[0:4, gT0 + hb, 0:512].rearrange("p (c t) -> p c t", t=128), axis=AX.X, op=ALU.max)
            DV("tensor_reduce", [gTres[2]], [R_g], out=Gmx[0:4, 8:24], in_=ps[0:4, gT0 + 2, 0:128].rearrange("p (j i) -> p j i", i=8),
               axis=AX.X, op=ALU.max)
            DV("memset", [], [R_g], ap=Mloc[0:4, :], constant=NEG_BIG)
            for c in range(NCH):
                DV("scalar_tensor_tensor", [R_g], [R_g], out=Mloc[0:4, :], in0=Mloc[0:4, :], scalar=Gmx[0:4, c:c + 1], in1=Bs[0:4, c:c + 1],
                   op0=ALU.max, op1=ALU.subtract)
            DV("tensor_reduce", [R_g], [R_g], out=Btot[0:4, :], in_=Bs[0:4, 0:8], axis=AX.X, op=ALU.add)
            DV("tensor_scalar", [R_g], [R_g], out=Btot[0:4, :], in0=Btot[0:4, :], scalar1=-1.0, scalar2=None, op0=ALU.mult)
            DV("tensor_tensor", [R_g], [R_g], out=mA[0:4, :], in0=Btot[0:4, :], in1=Mloc[0:4, :], op=ALU.max)
            ACT([R_g], [R_g], out=eBv[0:4, :], in_=Bs[0:4, 0:8], func=AF.Exp, scale=-1.0)

            bcast4(eBv[0:4, :], 8, EB, lambda a: a.rearrange("p (h c) -> p h c", c=8))
            S.dma("sp", mprev[0:4, :], smT_d[l], writes=[R_g], chan="gat")
            DV("tensor_tensor", [R_g], [R_g], out=mnew[0:4, :], in0=mprev[0:4, :], in1=Gmx[0:4, 8:24], op=ALU.max)
            DV("tensor_tensor", [R_g], [R_g], out=mnew[0:4, :], in0=mnew[0:4, :], in1=Bs[0:4, 8:24], op=ALU.subtract)
            DV("tensor_tensor", [R_g], [R_g], out=tmp4[0:4, :], in0=mprev[0:4, :], in1=mnew[0:4, :], op=ALU.subtract)
            DV("tensor_tensor", [R_g], [R_g], out=tmp4[0:4, :], in0=tmp4[0:4, :], in1=Bs[0:4, 8:24], op=ALU.subtract)
            ACT([R_g], [R_g], out=V3[0:4, 0, :], in_=tmp4[0:4, :], func=AF.Exp)
            DV("tensor_tensor", [R_g], [R_g], out=tmp4[0:4, :], in0=mnew[0:4, :], in1=Bs[0:4, 8:24], op=ALU.add)
            ACT([R_g], [R_g], out=V3[0:4, 1, :], in_=tmp4[0:4, :], func=AF.Exp, scale=-1.0)
            ACT([R_g], [R_g], out=V3[0:4, 2, :], in_=mprev[0:4, :], func=AF.Exp)
            bcast4(V3f[0:4, :], 48, BCs, lambda a: a.rearrange("p (h v j) -> p h v j", v=3, j=16))
            S.dma("sp", mso_d[l], mnew[0:4, :], reads=[R_g], chan="osm")


        A.push()
        conv_base = A.top
        ub = A.alloc("ub", [128, 8, 30 + NPT], BF16)
        ubs = A.alloc("ubs", [128, 8, NSEQ, 38], BF16)
        dgt = [A.alloc(f"dgt{i}", [128, CONVK, 128], BF16) for i in range(2)]
        ycv = A.alloc("ycv", [128, 8, T], F32)
        tl32 = A.alloc("tl32", [128, 8, 30], F32)
        us32 = A.alloc("us32", [128, 8, NSEQ, 8], F32)
        YA = SubArena(A, dgt[1], dgt[1].base, dgt[1].nwords)
        hsb = YA.alloc("hsb", [128, 8, NSEQ * 30], BF16)
        P1 = A.alloc("P1", [128, 248], F32)
        hrecv = A.alloc("hrecv", [128, 248], F32)
        sgt = [A.alloc(f"sgt{i}", [128, 384], F32) for i in range(6)]
        sgi = [0]
        RUB = [ub.R(ct) for ct in range(8)]
        RUS = [ubs.R(ct) for ct in range(8)]
        S.dma("pool", hsb.ap[:, :, :].rearrange("p c x -> p (c x)"), scvT_d[l], writes=[hsb.R()], chan="scv")
        ACT([hsb.R()], RUS, out=ubs.ap[:, :, :, 0:30], in_=hsb.ap[:, :, :].rearrange("p c (j t) -> p c j t", t=30), func=AF.Copy)
        YA.close()
        S.dma("sp", cso_old_d[l], scvK_d[l], chan="ocs0")

        for ct in range(8):
            i_w = wstate["i"] % 3
            wstate["i"] += 1
            wb = wbufs[i_w]
            for half, off in ((0, OFF_GA), (1, OFF_GG)):
                S.dma("pool", wb.ap[:, :, half * 128:(half + 1) * 128],
                      w_in_d[l][:, off + ct * 128:off + (ct + 1) * 128].rearrange("(k p) n -> p k n", p=128), writes=[wb.R(half)], chan=f"w{i_w}{'ab'[half]}")
            held = {}

            def evac_g(g, pap, bres, ct=ct):
                st = sgt[sgi[0] % 6]
                sgi[0] += 1
                ACT([bres, R_prm], [st.R()], out=st.ap[:, :], in_=pap, func=AF.Sigmoid, bias=P(l, "bfm")[:, 48 + ct:49 + ct])
                held[g] = st
            proj_fm(wb, wb, 1, evac_g)

            def evac_a(g, pa, bra, ct=ct):
                st = held[g]
                bga = P(l, "bfm")[:, 40 + ct:41 + ct]
                t0, t1 = TG[g]
                if g < 2:
                    DV("scalar_tensor_tensor", [bra, st.R(), R_prm], [RUB[ct]], out=ub.ap[:, ct, 30 + t0:30 + t1], in0=pa, scalar=bga, in1=st.ap[:, :],
                       op0=ALU.add, op1=ALU.mult)
                else:
                    DV("scalar_tensor_tensor", [bra, st.R(), R_prm], [RUB[ct]], out=ub.ap[:, ct, 30 + 768:30 + NPT], in0=pa[:, 0:256], scalar=bga,
                       in1=st.ap[:, 0:256], op0=ALU.add, op1=ALU.mult)
                    DV("scalar_tensor_tensor", [bra, st.R(), R_prm], [tl32.R()], out=tl32.ap[:, ct, :], in0=pa[:, 226:256], scalar=bga,
                       in1=st.ap[:, 226:256], op0=ALU.add, op1=ALU.mult)
                    DV("scalar_tensor_tensor", [bra, st.R(), R_prm], [us32.R()], out=us32.ap[:, ct, :, :], in0=pa[:, 256:384].rearrange("p (j i) -> p j i", i=8),
                       scalar=bga, in1=st.ap[:, 256:384].rearrange("p (j i) -> p j i", i=8), op0=ALU.add, op1=ALU.mult)
                    ACT([us32.R()], [RUS[ct]], out=ubs.ap[:, ct, :, 30:38], in_=us32.ap[:, ct, :, :], func=AF.Copy)
            proj_fm(wb, wb, 0, evac_a)
            if ct == 0:
                gate_chain()
        zc_pre = [load_w(w_in_d[l][:, OFF_ZC + pair * GW:OFF_ZC + (pair + 1) * GW]) for pair in range(2)]
        R_P1 = P1.R()
        DV("memset", [], [R_P1], ap=P1.ap[:, 240:248], constant=0.0)
        DV("tensor_copy", [tl32.R()], [R_P1], out=P1.ap[:, 0:240].rearrange("p (c t) -> p c t", t=30), in_=tl32.ap[:, :, :])
        DV("tensor_copy", [R_g], [R_P1], out=P1.ap[0:4, 240:241], in_=mA[0:4, :])
        R_b1 = Res("bnc1")
        R_g1 = Res("gat1")
        S.dma("sp", bnc1[l][:, :], P1.ap[:, :], reads=[R_P1], writes=[R_b1], chan="x1")
        S.dma("sp", cpo_d[l].rearrange("p c t -> p (c t)"), P1.ap[:, 0:240], reads=[R_P1], chan="ocp")
        S.dma("sp", cso_new_d[l], us32.ap[:, :, :, :].rearrange("p c j i -> p (c j i)"), reads=[us32.R()], chan="ocs1")
        S.op("pool", "collective_compute", dict(kind="AllGather", op=ALU.bypass, replica_groups=GROUPS, ins=[bnc1[l][:, :]], outs=[gat1[l][:, :]]),
             reads=[R_b1], writes=[R_g1], chan=S.chan(f"cc1_{l}"), amount=1)
        R_hr = hrecv.R()
        S.dma("sp", hrecv.ap[:, :], gat1[l][0:128, :], reads=[R_g1], writes=[R_hr], chan="x1r")
        for pair in range(4):
            wz, wzres = zc_pre[pair] if pair < 2 else load_w(w_in_d[l][:, OFF_ZC + pair * GW:OFF_ZC + (pair + 1) * GW])
            for j in range(2):
                ct = pair * 2 + j

                def evac_z(g, pap, bres, ct=ct):
                    t0, t1 = TG[g]
                    ACT([bres, R_prm], [mixc.R(ct, g)], out=mixc.ap[:, ct, t0:t1], in_=pap, func=AF.Silu, bias=P(l, "bfm")[:, 56 + ct:57 + ct])
                proj_fm(wz, wzres, j, evac_z)
        DV("tensor_scalar", [R_hr, R_c] + RUB, RUB, out=ub.ap[:, :, 0:30], in0=hrecv.ap[:, 0:240].rearrange("p (c t) -> p c t", t=30),
           scalar1=flag, scalar2=None, op0=ALU.mult)
        DV("tensor_scalar", [R_hr, R_c], [R_g], out=minit[0:4, :], in0=hrecv.ap[0:4, 240:241], scalar1=flag[0:4, :], scalar2=None, op0=ALU.mult)
        S.dma("pool", wob_d[l][:, :], w_out_d[l][:, :], writes=[R_wob[l]], chan="wob")
        wd = P(l, "wdw").rearrange("p (c j) -> p c j", j=CONVK)
        segs = [(0, 512), (512, 1024), (1024, 1152)]
        for ct in range(8):
            dg = dgt[ct % 2]
            DV("tensor_tensor", [R_cb, R_prm], [dg.R()], out=dg.ap[:, :, :], in0=identb.unsqueeze(1).to_broadcast([128, CONVK, 128]),
               in1=wd[:, ct, :].unsqueeze(2).to_broadcast([128, CONVK, 128]), op=ALU.mult)
            for s, (t0, t1) in enumerate(segs):
                n = t1 - t0
                b0, br = PS.get(1)
                for j in range(CONVK):
                    if s < 2:
                        rhs = ub.ap[:, ct, t0 + j:t0 + j + 512]
                        rr = RUB[ct]
                    else:
                        rhs = ubs.ap[:, ct, :, j:j + 8]
                        rr = RUS[ct]
                    last = j == CONVK - 1
                    MM([dg.R(), rr], (br if j == 0 or last else []), inc=last, out=ps[:, b0, 0:n], lhsT=dg.ap[:, j, :], rhs=rhs,
                       start=(j == 0), stop=last)
                ACT(br + [R_prm], [ycv.R(ct)], out=ycv.ap[:, ct, t0:t1], in_=ps[:, b0, 0:n], func=AF.Identity, bias=P(l, "bdw")[:, ct:ct + 1])
        ysq = [A.alloc(f"ysq{i}", [128, 512], F32) for i in range(2)]
        UA = SubArena(A, ub, ub.base, ub.nwords)
        mean = UA.alloc("cmean", [128, T], F32)
        rstd = UA.alloc("crstd", [128, T], F32)
        msq = UA.alloc("cmsq", [128, T], F32)
        R_st = mean.R()
        qi = [0]
        for s, (t0, t1) in enumerate(segs):
            n = t1 - t0
            b0, br = PS.get(2)
            for ct in range(8):
                yv = ycv.ap[:, ct, t0:t1]
                sq = ysq[qi[0] % 2]
                qi[0] += 1
                ACT([ycv.R(ct)], [sq.R()], out=sq.ap[:, 0:n], in_=yv, func=AF.Square)
                MM([ycv.R(ct), R_c], [br[0]], out=ps[:, b0, 0:n], lhsT=ones_c, rhs=yv, start=(ct == 0), stop=(ct == 7))
                MM([sq.R(), R_c], [br[1]], out=ps[:, b0 + 1, 0:n], lhsT=ones_c, rhs=sq.ap[:, 0:n], start=(ct == 0), stop=(ct == 7))
            DV("tensor_scalar", [br[0]], [R_st], out=mean.ap[:, t0:t1], in0=ps[:, b0, 0:n], scalar1=1.0 / CW, scalar2=None, op0=ALU.mult)
            DV("tensor_scalar", [br[1]], [R_st], out=rstd.ap[:, t0:t1], in0=ps[:, b0 + 1, 0:n], scalar1=1.0 / CW, scalar2=None, op0=ALU.mult)
        DV("tensor_tensor", [R_st], [msq.R()], out=msq.ap[:, :], in0=mean.ap[:, :], in1=mean.ap[:, :], op=ALU.mult)
        DV("tensor_tensor", [R_st, msq.R()], [R_st], out=rstd.ap[:, :], in0=rstd.ap[:, :], in1=msq.ap[:, :], op=ALU.subtract)
        ACT([R_st, R_c], [R_st], out=rstd.ap[:, :], in_=rstd.ap[:, :], func=AF.Sqrt, bias=epsc)
        DV("reciprocal", [R_st], [R_st], out=rstd.ap[:, :], in_=rstd.ap[:, :])
        swt = [A.alloc(f"swt{i}", [128, 384], BF16) for i in range(2)]
        wi = [0]
        assert conv_base + 4608 >= ub.base + ub.nwords and conv_base + 2 * 4608 <= ycv.base
        kT = arena_alloc_at(A, "kT", [128, 8, T], BF16, conv_base + 4608, [ubs] + dgt)
        kstate["kT"] = kT

        def norm_units():
            for ct in range(8):
                for g, (t0, t1) in enumerate(TG):
                    yv = ycv.ap[:, ct, t0:t1]
                    DV("tensor_tensor", [ycv.R(ct), R_st], [ycv.R(ct)], out=yv, in0=yv, in1=mean.ap[:, t0:t1], op=ALU.subtract)
                    DV("tensor_tensor", [ycv.R(ct), R_st], [ycv.R(ct)], out=yv, in0=yv, in1=rstd.ap[:, t0:t1], op=ALU.mult)
                    sw = swt[wi[0] % 2]
                    wi[0] += 1
                    ACT([ycv.R(ct), R_prm], [sw.R()], out=sw.ap[:, :], in_=yv, func=AF.Silu, scale=P(l, "clng")[:, ct:ct + 1], bias=P(l, "clnb")[:, ct:ct + 1])
                    DV("tensor_tensor", [sw.R(), mixc.R(ct, g)], [mixc.R(ct, g)], out=mixc.ap[:, ct, t0:t1], in0=mixc.ap[:, ct, t0:t1], in1=sw.ap[:, :], op=ALU.mult)
                    yield
        nu = norm_units()
        for grp in range(4):
            wk, wkres = load_w(w_in_d[l][:, OFF_K + grp * GW:OFF_K + (grp + 1) * GW])
            for j in range(2):
                tile_ = grp * 2 + j

                def evac_k(g, pap, bres, tile_=tile_):
                    t0, t1 = TG[g]
                    ACT([bres, R_prm], [kT.R(tile_, g)], out=kT.ap[:, tile_, t0:t1], in_=pap, func=AF.Identity, scale=1.0 / 16.0,
                        bias=P(l, "bk16")[:, tile_:tile_ + 1])
                for _ in proj_fm_gen(wk, wkres, j, evac_k, seg=6):
                    next(nu, None)
        for _ in nu:
            pass
        UA.close()
        A.pop()

        A.push()
        mixm = A.alloc("mixm", [128, 8, T], BF16)
        A.push()
        kT = kstate["kT"]
        arena_adopt(A, kT)
        vp = A.alloc("vp", [128, NTT, NH, 257], BF16)
        Dst = A.alloc("Dst", [128, NH, 2, 257], F32)
        bvb = A.alloc("bvb", [128, MW], F32)
        S.dma("sp", bvb.ap[:, :], bv_d[l].partition_broadcast(128), writes=[bvb.R()], chan="bvb")
        DV("tensor_copy", [R_g], [vp.R(tt, h) for tt in range(NTT) for h in range(NH)], out=vp.ap[:, :, :, 256], in_=eg)
        qTs = A.alloc("qTs", [128, 8, NST], BF16)
        hs = A.alloc("hs", [128, NCH + 1, 256], F32)
        hn = [A.alloc(f"hn{i}", [128, 256], BF16) for i in range(2)]
        atm = [A.alloc(f"atm{i}", [128, 128], BF16) for i in range(2)]
        stt_ = A.alloc("bnst", [128, NCH + 1, 6], F32)
        mvt = A.alloc("bnmv", [128, NCH + 1, 2], F32)
        rdt = A.alloc("rdt", [128, 4], F32)
        A.push()
        vtmp = [A.alloc(f"vtmp{i}", [128, 256], F32) for i in range(2)]
        vi = [0]
        vw = {}

        def vproj_unit(h, tt):
            if tt == 0:
                vw[h] = load_w(w_in_d[l][:, OFF_V + h * GW:OFF_V + (h + 1) * GW])
            wv, wvres = vw[h]
            b0, br = PS.get(1)
            for kt in range(KT):
                last = kt == KT - 1
                MM([wvres.R(0), wvres.R(1), xT.R(tt)], (br if kt == 0 or last else []), inc=last,
                   out=ps[:, b0, 0:256], lhsT=xT.ap[:, kt, tt * 128:(tt + 1) * 128], rhs=wv.ap[:, kt, :], start=(kt == 0), stop=last)
            vt = vtmp[vi[0] % 2]
            vi[0] += 1
            DV("tensor_tensor", br + [bvb.R()], [vt.R()], out=vt.ap[:, :], in0=ps[:, b0, 0:256], in1=bvb.ap[:, h * 256:(h + 1) * 256], op=ALU.add)
            ACT([vt.R(), R_g], [vp.R(tt, h)], out=vp.ap[:, tt, h, 0:256], in_=vt.ap[:, :], func=AF.Identity, scale=eg[:, tt, h:h + 1])

        k2t = [A.alloc(f"k2t{i}", [128, 256], BF16) for i in range(3)]
        k2i = [0]

        def state_step(h, c, Dt, DR):
            tb, tbr = PS.get(1)
            for dt_ in range(2):
                TR([kT.R(h * 2 + dt_, c // 3), R_cb], tbr, out=psb[:, tb, dt_ * 128:(dt_ + 1) * 128], in_=kT.ap[:, h * 2 + dt_, c * 128:(c + 1) * 128])
            k2 = k2t[k2i[0] % 3]
            k2i[0] += 1
            ACT(tbr + [R_g], [k2.R()], out=k2.ap[:, :], in_=psb[:, tb, 0:256], func=AF.Identity, scale=EB[:, h, c:c + 1])
            b0, br = PS.get(2)
            for dt_ in range(2):
                MM([k2.R(), vp.R(c, h)], [br[dt_]], out=ps[:, b0 + dt_, 0:257], lhsT=k2.ap[:, dt_ * 128:(dt_ + 1) * 128], rhs=vp.ap[:, c, h, :],
                   start=True, stop=True)
            DV("scalar_tensor_tensor", br + [R_g, DR], [DR], out=Dt, in0=Dt, scalar=EB[:, h, c:c + 1], in1=ps[:, b0:b0 + 2, 0:257],
               op0=ALU.mult, op1=ALU.add)

        for tt in range(NTT):
            vproj_unit(0, tt)
        for h in range(NH):
            DV("memset", [], [Dst.R(h)], ap=Dst.ap[:, h, :, :], constant=0.0)
            for i in range(NTT):
                if i < NCH:
                    state_step(h, i, Dst.ap[:, h, :, :], Dst.R(h))
                if h + 1 < NH:
                    vproj_unit(h + 1, i)
        DV("tensor_tensor", [R_g], [R_g], out=mfin[0:4, :], in0=minit[0:4, :], in1=Btot[0:4, :], op=ALU.add)
        DV("tensor_tensor", [R_g], [R_g], out=mfin[0:4, :], in0=mfin[0:4, :], in1=Mloc[0:4, :], op=ALU.max)
        S.dma("sp", mpo_d[l], mfin[0:4, :], reads=[R_g], chan="omp")
        ACT([R_g], [R_g], out=emfv[0:4, :], in_=mfin[0:4, :], func=AF.Exp, scale=-1.0)
        bcast4(emfv[0:4, :], 1, EMF, lambda a: a)

        wpre = {("q", 0): load_w(w_in_d[l][:, OFF_Q:OFF_Q + GW]), ("o", 0): load_w(w_in_d[l][:, OFF_O:OFF_O + GW])}
        R_b2 = Res("bnc2")
        R_g2 = Res("gat2")
        Dflat = Dst.ap[:, :, :, :].rearrange("p h d e -> p (h d e)")
        DstR = [Dst.R(h) for h in range(NH)]
        S.dma("sp", bnc2[l][:, :], Dflat, reads=DstR, writes=[R_b2], chan="x2")
        S.op("pool", "collective_compute", dict(kind="AllGather", op=ALU.bypass, replica_groups=GROUPS, ins=[bnc2[l][:, :]], outs=[gat2[l][:, :]]),
             reads=[R_b2], writes=[R_g2], chan=S.chan(f"cc2_{l}"), amount=1)
        S.dma("sp", Dflat, gat2[l][0:128, :], reads=[R_g2], writes=DstR, chan="x2r")
        qT = [A.alloc(f"qT{i}", [128, 2, T], BF16) for i in range(2)]
        Dbf = [A.alloc(f"Dbf{i}", [128, 2, 257], BF16) for i in range(2)]
        zmt = [A.alloc(f"zmt{i}", [128, 384], BF16) for i in range(3)]
        zi = [0]
        cot = [A.alloc(f"co{i}", [128, 2, 257], F32) for i in range(NH)]
        ai = [0]
        di = [0]
        hi_ = [0]

        def den_and_hs(U_bank, ubr, cidx, enb_col):
            R_rd = rdt.R()
            ACT(ubr, [R_rd], out=rdt.ap[:, 0:1], in_=ps[:, U_bank, 256:257], func=AF.Abs)
            DV("tensor_scalar", [R_rd, R_g], [R_rd], out=rdt.ap[:, 1:2], in0=rdt.ap[:, 0:1], scalar1=enb_col, scalar2=None, op0=ALU.max)
            DV("reciprocal", [R_rd], [R_rd], out=rdt.ap[:, 2:3], in_=rdt.ap[:, 1:2])
            ACT(ubr + [R_rd], [hs.R(cidx)], out=hs.ap[:, cidx, :], in_=ps[:, U_bank, 0:256], func=AF.Identity, scale=rdt.ap[:, 2:3])
            DV("bn_stats", [hs.R(cidx)], [stt_.R(cidx)], out=stt_.ap[:, cidx, :], in_=hs.ap[:, cidx, :])
            DV("bn_aggr", [stt_.R(cidx)], [mvt.R()], out=mvt.ap[:, cidx, :], in_=stt_.ap[:, cidx:cidx + 1, :])

        def finish_head(h, cidxs, tok0_of):
            c0, c1 = cidxs[0], cidxs[-1] + 1
            ACT([mvt.R(), R_c], [mvt.R()], out=mvt.ap[:, c0:c1, 1], in_=mvt.ap[:, c0:c1, 1], func=AF.Sqrt, bias=epsc)
            DV("reciprocal", [mvt.R()], [mvt.R()], out=mvt.ap[:, c0:c1, 1], in_=mvt.ap[:, c0:c1, 1])
            for cidx in cidxs:
                hb = hn[hi_[0] % 2]
                hi_[0] += 1
                DV("tensor_scalar", [hs.R(cidx), mvt.R()], [hb.R()], out=hb.ap[:, :], in0=hs.ap[:, cidx, :], scalar1=mvt.ap[:, cidx, 0:1],
                   scalar2=mvt.ap[:, cidx, 1:2], op0=ALU.subtract, op1=ALU.mult)
                tb, tbr = PS.get(1)
                for et in range(2):
                    TR([hb.R(), R_cb], tbr, out=psb[:, tb, et * 128:(et + 1) * 128], in_=hb.ap[:, et * 128:(et + 1) * 128])
                t0 = tok0_of(cidx)
                g = t0 // 384
                mr = [mixm.R(h * 2, g), mixm.R(h * 2 + 1, g)]
                DV("tensor_tensor", tbr + mr, mr, out=mixm.ap[:, h * 2:h * 2 + 2, t0:t0 + 128], in0=mixm.ap[:, h * 2:h * 2 + 2, t0:t0 + 128],
                   in1=psb[:, tb, 0:256].rearrange("p (a t) -> p a t", t=128), op=ALU.mult)

        def proj_tasks(h):
            q = qT[h % 2]
            wcache = {}

            def getw(nm, off):
                if nm not in wcache:
                    if (nm, h) in wpre:
                        wcache[nm] = wpre.pop((nm, h))
                    else:
                        wcache[nm] = load_w(w_in_d[l][:, off + h * GW:off + (h + 1) * GW])
                return wcache[nm]
            tasks = []
            for j in range(2):
                tile_ = h * 2 + j

                def task_q(j=j, tile_=tile_):
                    wq, wqres = getw("q", OFF_Q)

                    def evac_q(g, pap, bres):
                        t0, t1 = TG[g]
                        ACT([bres, R_prm], [q.R(j, g)], out=q.ap[:, j, t0:t1], in_=pap, func=AF.Identity, bias=P(l, "bfm")[:, tile_:tile_ + 1])
                        if g == 2:
                            ACT([bres, R_prm], [qTs.R(tile_)], out=qTs.ap[:, tile_, :], in_=pap[:, 256:384], func=AF.Identity,
                                bias=P(l, "bfm")[:, tile_:tile_ + 1])
                    yield from proj_fm_gen(wq, wqres, j, evac_q)
                tasks.append(task_q)
            for j in range(2):
                tile_ = h * 2 + j

                def task_o(j=j, tile_=tile_):
                    wo, wores = getw("o", OFF_O)

                    def evac_o(g, pap, bres):
                        t0, t1 = TG[g]
                        ACT([bres, R_prm], [mixm.R(tile_, g)], out=mixm.ap[:, tile_, t0:t1], in_=pap, func=AF.Sigmoid,
                            bias=P(l, "bfm")[:, 24 + tile_:25 + tile_])
                    yield from proj_fm_gen(wo, wores, j, evac_o)
                tasks.append(task_o)
            for j in range(2):
                tile_ = h * 2 + j

                def task_zm(j=j, tile_=tile_):
                    wz, wzres = getw("zm", OFF_ZM)

                    def evac_zm(g, pap, bres):
                        t0, t1 = TG[g]
                        zt = zmt[zi[0] % 3]
                        zi[0] += 1
                        ACT([bres, R_prm], [zt.R()], out=zt.ap[:, :], in_=pap, func=AF.Silu, bias=P(l, "bfm")[:, 32 + tile_:33 + tile_])
                        DV("scalar_tensor_tensor", [zt.R(), R_prm, mixm.R(tile_, g)], [mixm.R(tile_, g)], out=mixm.ap[:, tile_, t0:t1], in0=zt.ap[:, :],
                           scalar=P(l, "hng")[:, tile_:tile_ + 1], in1=mixm.ap[:, tile_, t0:t1], op0=ALU.mult, op1=ALU.mult)
                    yield from proj_fm_gen(wz, wzres, j, evac_zm)
                tasks.append(task_zm)
            return tasks

        def run_all(tasks):
            for t_ in tasks:
                for _ in t_():
                    pass

        def seg_stream(tasks):
            for t_ in tasks:
                yield from t_()

        run_all(proj_tasks(0))
        atm8 = [A.alloc(f"atm8_{i}", [128, 128], BF16) for i in range(NCH)]
        k2s8 = [A.alloc(f"k2s8_{i}", [128, 256], BF16) for i in range(NCH)]
        for h in range(NH):
            q = qT[h % 2]
            stream = seg_stream(proj_tasks(h + 1)) if h + 1 < NH else iter(())
            Dt = Dst.ap[:, h, :, :]
            DR = Dst.R(h)
            DV("tensor_scalar", [DR, R_c], [DR], out=Dt, in0=Dt, scalar1=flag, scalar2=None, op0=ALU.mult)
            for c in range(NCH):
                g = c // 3
                tk = slice(c * 128, (c + 1) * 128)
                a0, abr = PS.get(1)
                for dt_ in range(2):
                    MM([kT.R(h * 2 + dt_, g), q.R(dt_, g)], abr, out=ps[:, a0, 0:128], lhsT=kT.ap[:, h * 2 + dt_, tk], rhs=q.ap[:, dt_, tk],
                       start=(dt_ == 0), stop=(dt_ == 1))
                DV("tensor_tensor", abr + [R_c], [atm8[c].R()], out=atm8[c].ap[:, :], in0=ps[:, a0, 0:128], in1=maskp, op=ALU.mult)
                tb, tbr = PS.get(1)
                for dt_ in range(2):
                    TR([kT.R(h * 2 + dt_, g), R_cb], tbr, out=psb[:, tb, dt_ * 128:(dt_ + 1) * 128], in_=kT.ap[:, h * 2 + dt_, tk])
                ACT(tbr + [R_g], [k2s8[c].R()], out=k2s8[c].ap[:, :], in_=psb[:, tb, 0:256], func=AF.Identity, scale=EB[:, h, c:c + 1])
            for c in range(NCH):
                g = c // 3
                tk = slice(c * 128, (c + 1) * 128)
                k2 = k2s8[c]
                b0, br = PS.get(2)
                for dt_ in range(2):
                    MM([k2.R(), vp.R(c, h)], [br[dt_]], out=ps[:, b0 + dt_, 0:257], lhsT=k2.ap[:, dt_ * 128:(dt_ + 1) * 128], rhs=vp.ap[:, c, h, :],
                       start=True, stop=True)
                db = Dbf[di[0] % 2]
                di[0] += 1
                ACT([DR], [db.R()], out=db.ap[:, :, :], in_=Dt, func=AF.Copy)
                DV("scalar_tensor_tensor", br + [R_g, DR], [DR], out=Dt, in0=Dt, scalar=EB[:, h, c:c + 1], in1=ps[:, b0:b0 + 2, 0:257],
                   op0=ALU.mult, op1=ALU.add)
                u0, ubr = PS.get(1)
                MM([atm8[c].R(), vp.R(c, h)], ubr, out=ps[:, u0, 0:257], lhsT=atm8[c].ap[:, :], rhs=vp.ap[:, c, h, :], start=True, stop=False)
                for dt_ in range(2):
                    MM([q.R(dt_, g), db.R()], ubr, out=ps[:, u0, 0:257], lhsT=q.ap[:, dt_, tk], rhs=db.ap[:, dt_, :], start=False, stop=(dt_ == 1))
                den_and_hs(u0, ubr, c, enb[:, c, h:h + 1])
                for _ in range(3):
                    next(stream, None)
            for _ in stream:
                pass
            finish_head(h, list(range(NCH)), lambda cidx: cidx * 128)
            co = cot[h]
            DV("tensor_scalar", [DR, R_g], [co.R()], out=co.ap[:, :, :], in0=Dt, scalar1=EMF[:, h:h + 1], scalar2=None, op0=ALU.mult)
            S.dma("sp", Cp_d[l, h].rearrange("(d p) e -> p d e", p=128), co.ap[:, :, 0:256], reads=[co.R()], chan=f"oCp{h}")
            S.dma("sp", npo_d[l][:, h * 2:h * 2 + 2], co.ap[:, :, 256], reads=[co.R()], chan=f"onp{h}", noncontig=True)

        A.pop()
        A.push()
        XA = SubArena(A, xT, xT.base, xT.nwords)
        kts = A.alloc("kts", [128, MW], BF16)
        for tile_ in range(8):
            tb, tbr = PS.get(1)
            TR([kT.R(tile_, 2), R_cb], tbr, out=psb[:, tb, 0:128], in_=kT.ap[:, tile_, NPT:T])
            ACT(tbr, [kts.R()], out=kts.ap[:, tile_ * 128:(tile_ + 1) * 128], in_=psb[:, tb, 0:128], func=AF.Copy)
        snT = A.alloc("snT", [128, 128], F32)
        S.dma("sp", snT.ap[:, :], snT_d[l], writes=[snT.R()], chan="snT")
        nout = A.alloc("nout", [128, 128], F32)
        KZ = [XA.alloc(f"KZ{i}", [128, NSEQ, 256], BF16) for i in range(2)]
        QZ = [XA.alloc(f"QZ{i}", [128, 2, 2176], BF16) for i in range(2)]
        for i in range(2):
            PL("memset", [], [QZ[i].R()], ap=QZ[i].ap[:, :, :], constant=0.0)
        NCI = 6
        Cin = [A.alloc(f"Cin{i}", [128, 2, 257], F32) for i in range(NCI)]
        Cbf = [A.alloc(f"Cbf{i}", [128, 2, 257], BF16) for i in range(3)]
        Ctm = [A.alloc(f"Ctm{i}", [128, 2, 257], F32) for i in range(2)]
        Cou = [A.alloc(f"Cou{i}", [128, 2, 257], F32) for i in range(4)]
        it = [0]

        def issue_cin(idx):
            if idx >= NH * NSEQ:
                return
            hh, jj = idx // NSEQ, idx % NSEQ
            ci_ = Cin[idx % NCI]
            S.dma("sp", ci_.ap[:, :, 0:256], sC_d[l, jj, hh].rearrange("(d p) e -> p d e", p=128), writes=[ci_.R()], chan=f"ci{idx % NCI}")

        def prep_n(idx):
            if idx >= NH * NSEQ:
                return
            hh, jj = idx // NSEQ, idx % NSEQ
            ci_ = Cin[idx % NCI]
            col_ = (jj * NH + hh) * 2
            DV("tensor_copy", [snT.R()], [ci_.R("n")], out=ci_.ap[:, :, 256], in_=snT.ap[:, col_:col_ + 2])

        def prep_c(idx):
            if idx >= NH * NSEQ:
                return
            hh, jj = idx // NSEQ, idx % NSEQ
            ci_ = Cin[idx % NCI]
            cb_ = Cbf[idx % 3]
            ACT([ci_.R(), ci_.R("n"), R_g], [cb_.R()], out=cb_.ap[:, :, :], in_=ci_.ap[:, :, :], func=AF.Identity, scale=BCs[:, hh, 2, jj:jj + 1])

        for i in range(NCI - 1):
            issue_cin(i)
            prep_n(i)
        prep_c(0)
        ams = {}

        def make_setup(h):
            kz = KZ[h % 2]
            qz = QZ[h % 2]
            pieces = []
            for qq in range(4):
                def kz_piece(qq=qq):
                    DV("tensor_tensor", [kts.R(), R_c], [kz.R()], out=kz.ap[:, qq * 4:(qq + 1) * 4, :],
                       in0=kts.ap[:, h * 256:(h + 1) * 256].unsqueeze(1).to_broadcast([128, 4, 256]),
                       in1=blk[:, qq * 4:(qq + 1) * 4].unsqueeze(2).to_broadcast([128, 4, 256]), op=ALU.mult)
                pieces.append(kz_piece)

            def qz_piece():
                for dt_ in range(2):
                    DV("tensor_copy", [qTs.R(h * 2 + dt_)], [qz.R()], out=qz.ap[:, dt_, :].rearrange("p (j x) -> p j x", x=136)[:, :, 0:8],
                       in_=qTs.ap[:, h * 2 + dt_, :].rearrange("p (j i) -> p j i", i=8))
            pieces.append(qz_piece)

            def at_piece():
                a0, abr = PS.get(1)
                for dt_ in range(2):
                    MM([kT.R(h * 2 + dt_, 2), qTs.R(h * 2 + dt_)], abr, out=ps[:, a0, 0:128], lhsT=kT.ap[:, h * 2 + dt_, NPT:T], rhs=qTs.ap[:, h * 2 + dt_, :],
                       start=(dt_ == 0), stop=(dt_ == 1))
                am_ = A.alloc(f"ams{h}", [128, 128], BF16) if False else amsb[h % 2]
                DV("tensor_tensor", abr + [R_c], [am_.R()], out=am_.ap[:, :], in0=ps[:, a0, 0:128], in1=masks, op=ALU.mult)
                ams[h] = am_
            pieces.append(at_piece)
            return pieces

        amsb = [A.alloc(f"amsb{i}", [128, 128], BF16) for i in range(2)]
        setups = [make_setup(h) for h in range(NH)]
        for p_ in setups[0]:
            p_()
        for h in range(NH):
            kz = KZ[h % 2]
            qz = QZ[h % 2]
            am = ams[h]
            u0, ubr = PS.get(1)
            MM([am.R(), vp.R(NCH, h)], ubr, out=ps[:, u0, 0:257], lhsT=am.ap[:, :], rhs=vp.ap[:, NCH, h, :], start=True, stop=False)
            PS.pinned.add(u0)
            for j in range(NSEQ):
                k_ = it[0]
                it[0] += 1
                ci = Cin[k_ % NCI]
                cb = Cbf[k_ % 3]
                ctm = Ctm[k_ % 2]
                cu = Cou[k_ % 4]
                issue_cin(k_ + NCI - 1)
                prep_n(k_ + NCI - 1)
                if h + 1 < NH and 2 <= j < 2 + len(setups[h + 1]):
                    setups[h + 1][j - 2]()
                col = (j * NH + h) * 2
                prep_c(k_ + 1)
                for dt_ in range(2):
                    MM([qz.R(), cb.R()], ubr, out=ps[:, u0, 0:257], lhsT=qz.ap[:, dt_, j * 128:(j + 1) * 128], rhs=cb.ap[:, dt_, :],
                       start=False, stop=(j == NSEQ - 1 and dt_ == 1))
                b0, br = PS.get(2)
                for dt_ in range(2):
                    MM([kz.R(), vp.R(NCH, h)], [br[dt_]], out=ps[:, b0 + dt_, 0:257], lhsT=kz.ap[:, j, dt_ * 128:(dt_ + 1) * 128], rhs=vp.ap[:, NCH, h, :],
                       start=True, stop=True)
                ACT(br + [R_g], [ctm.R()], out=ctm.ap[:, :, :], in_=ps[:, b0:b0 + 2, 0:257], func=AF.Identity, scale=BCs[:, h, 1, j:j + 1])
                DV("scalar_tensor_tensor", [ci.R(), ci.R("n"), ctm.R(), R_g], [cu.R()], out=cu.ap[:, :, :], in0=ci.ap[:, :, :], scalar=BCs[:, h, 0, j:j + 1],
                   in1=ctm.ap[:, :, :], op0=ALU.mult, op1=ALU.add)
                S.dma("pool", Cs_d[l, j, h].rearrange("(d p) e -> p d e", p=128), cu.ap[:, :, 0:256], reads=[cu.R()], chan=f"cu{k_ % 4}")
                DV("tensor_copy", [cu.R()], [nout.R()], out=nout.ap[:, col:col + 2], in_=cu.ap[:, :, 256])
            PS.pinned.discard(u0)
            den_and_hs(u0, ubr, NCH, enb[:, NCH, h:h + 1])
            finish_head(h, [NCH], lambda cidx: NPT)
        S.dma("sp", nso_d[l], nout.ap[:, :], reads=[nout.R()], chan="ons")
        XA.close()
        A.pop()
        A.pop()

        A.push()
        z = A.alloc("z", [128, NTT, D_MODEL], F32)
        lgb = A.alloc("lgb", [128, D_MODEL], F32)
        lbb = A.alloc("lbb", [128, D_MODEL], F32)
        S.dma("sp", lgb.ap[:, :], lng_d[l].partition_broadcast(128), writes=[lgb.R()], chan="lnp0")
        S.dma("sp", lbb.ap[:, :], lnb_d[l].partition_broadcast(128), writes=[lbb.R()], chan="lnp1")
        for tt in range(NTT):
            if l == 0:
                S.dma("sp", z.ap[:, tt, :], xtok_d[tt * 128:(tt + 1) * 128, :], writes=[z.R(tt)], chan=f"xr{tt}")
            else:
                S.dma("sp", z.ap[:, tt, :], y1_d[tt * 128:(tt + 1) * 128, :], reads=[R_y1[tt]], writes=[z.R(tt)], chan=f"xr{tt}")

        if DEBUG and l == 0:
            dbg_d = nc.dram_tensor("dbg", [128, 16, NST], F32, kind="ExternalOutput").ap()
            XD = SubArena(A, xT, xT.base, xT.nwords)
            dbt = XD.alloc("dbt", [128, 16, NST], F32)
            ACT([mixm.R(k, 2) for k in range(8)], [dbt.R()], out=dbt.ap[:, 0:8, :], in_=mixm.ap[:, :, NPT:T], func=AF.Copy)
            ACT([mixc.R(k, 2) for k in range(8)], [dbt.R()], out=dbt.ap[:, 8:16, :], in_=mixc.ap[:, :, NPT:T], func=AF.Copy)
            S.dma("sp", dbg_d[:, :, :], dbt.ap[:, :, :], reads=[dbt.R()], chan="dbg")
            XD.close()

        def mix_tile(kt, tt):
            src = mixm if kt < 8 else mixc
            return src.ap[:, kt % 8, tt * 128:(tt + 1) * 128], src.R(kt % 8, tt // 3)

        st2 = A.alloc("st2", [128, 2, 4, 6], F32)
        mv2 = A.alloc("mv2", [128, 2, 4], F32)
        ybf = [A.alloc(f"ybf{i}", [128, D_MODEL], BF16) for i in range(2)]

        def ln_gen(tts):
            for tt in tts:
                zt_ = z.ap[:, tt, :]
                zR = z.R(tt)
                sb_ = tt % 2
                sR, mR = st2.R(sb_), mv2.R(sb_)
                mv = mv2.ap[:, sb_, :]
                for q4 in range(4):
                    DV("bn_stats", [zR], [sR], out=st2.ap[:, sb_, q4, :], in_=z.ap[:, tt, q4 * 512:(q4 + 1) * 512])
                    if q4 % 2 == 1:
                        yield
                DV("bn_aggr", [sR], [mR], out=mv[:, 0:2], in_=st2.ap[:, sb_, :, :])
                ACT([mR, R_c], [mR], out=mv[:, 1:2], in_=mv[:, 1:2], func=AF.Sqrt, bias=epsc)
                DV("reciprocal", [mR], [mR], out=mv[:, 1:2], in_=mv[:, 1:2])
                DV("scalar_tensor_tensor", [mR], [mR], out=mv[:, 2:3], in0=mv[:, 0:1], scalar=-1.0, in1=mv[:, 1:2], op0=ALU.mult, op1=ALU.mult)
                yield
                ACT([zR, mR], [zR], out=zt_, in_=zt_, func=AF.Identity, scale=mv[:, 1:2], bias=mv[:, 2:3])
                yield
                DV("tensor_tensor", [zR, lgb.R()], [zR], out=zt_, in0=zt_, in1=lgb.ap[:, :], op=ALU.mult)
                yield
                DV("tensor_tensor", [zR, lbb.R()], [zR], out=zt_, in0=zt_, in1=lbb.ap[:, :], op=ALU.add)
                if l == DEPTH - 1:
                    S.dma("sp", y_d[tt * 128:(tt + 1) * 128, :], zt_, reads=[zR], chan="oy")
                    yield
                else:
                    S.dma("sp", y1_d[tt * 128:(tt + 1) * 128, :], zt_, reads=[zR], writes=[R_y1[tt]], chan=f"oy1_{tt}")
                    yb = ybf[tt % 2]
                    ACT([zR], [yb.R()], out=yb.ap[:, :], in_=zt_, func=AF.Copy)
                    yield
                    for hf in range(2):
                        tb, tbr = PS.get(1)
                        for k8 in range(8):
                            kt = hf * 8 + k8
                            TR([yb.R(), R_cb], tbr, out=psb[:, tb, k8 * 128:(k8 + 1) * 128], in_=yb.ap[:, kt * 128:(kt + 1) * 128])
                        ACT(tbr, [xT.R(tt)], out=xT.ap[:, hf * 8:(hf + 1) * 8, tt * 128:(tt + 1) * 128],
                            in_=psb[:, tb, 0:1024].rearrange("p (k t) -> p k t", t=128), func=AF.Copy)
                        yield

        PASSES = [(0, 3), (3, 6), (6, 9)]
        prev = iter(())
        for (ta, tb_) in PASSES:
            for cg in range(D_MODEL // GW):
                i_w = wstate["i"] % 3
                wstate["i"] += 1
                wo_ = wores_ = wbufs[i_w]
                S.dma("pool", wo_.ap[:, :, :], wob_d[l][:, cg * GW:(cg + 1) * GW].rearrange("(k p) n -> p k n", p=128),
                      reads=[R_wob[l]], writes=[wo_.R(0), wo_.R(1)], chan=f"w{i_w}")
                for tt in range(ta, tb_):
                    b0, br = PS.get(1)
                    for kt in range(KT):
                        mt, mr = mix_tile(kt, tt)
                        last = kt == KT - 1
                        MM([wores_.R(0), wores_.R(1), mr], (br if kt == 0 or last else []), inc=last, out=ps[:, b0, 0:GW], lhsT=mt, rhs=wo_.ap[:, kt, :],
                           start=(kt == 0), stop=last)
                    zv = z.ap[:, tt, cg * GW:(cg + 1) * GW]
                    DV("scalar_tensor_tensor", br + [z.R(tt)], [z.R(tt)], out=zv, in0=zv, scalar=float(DN_ALPHA), in1=ps[:, b0, 0:GW],
                       op0=ALU.mult, op1=ALU.add)
                    next(prev, None)
            for _ in prev:
                pass
            prev = ln_gen(range(ta, tb_))
        for _ in prev:
            pass
        A.pop()
        A.pop()
        A.pop()

    for l in range(DEPTH):
        layer(l)

    final = {k: v for k, v in S.cnt.items() if k.startswith("D_") and v > 0}
    S.ops["sp"].append((final, None, None))
    S.emit()
    return nc, A.peak


def _get_program():
    nc, _peak = build_program()
    return nc


def kernel(x_prompt, x_sample, state_C, state_n, state_m, state_conv, w_in, b_in, hn_g, w_dw, b_dw,
           cln_g, cln_b, w_out, ln_g, ln_b):
    f32 = np.float32
    x_prompt = np.asarray(x_prompt, f32)
    x_sample = np.asarray(x_sample, f32)
    state_C = np.asarray(state_C, f32)
    state_n = np.asarray(state_n, f32)
    state_m = np.asarray(state_m, f32)
    state_conv = np.asarray(state_conv, f32)
    w_in = np.ascontiguousarray(np.asarray(w_in, f32))
    w_out = np.ascontiguousarray(np.asarray(w_out, f32))
    b_in = np.asarray(b_in, f32)
    hn_g = np.asarray(hn_g, f32)
    w_dw = np.asarray(w_dw, f32)
    b_dw = np.asarray(b_dw, f32)
    cln_g = np.asarray(cln_g, f32)
    cln_b = np.asarray(cln_b, f32)
    ln_g = np.asarray(ln_g, f32)
    ln_b = np.asarray(ln_b, f32)

    def fm(v, ntile):
        return np.ascontiguousarray(v.reshape(DEPTH, ntile, 128).transpose(0, 2, 1))

    shared = {
        "w_in": w_in, "w_out": w_out,
        "bfm": fm(b_in[:, :8192], 64),
        "bg": np.ascontiguousarray(b_in[:, 8192:8200].reshape(DEPTH, 1, 8)),
        "bv": np.ascontiguousarray(b_in[:, OFF_V:OFF_V + MW].reshape(DEPTH, 1, MW)),
        "hng": fm(hn_g, 8),
        "wdw": np.ascontiguousarray(w_dw.reshape(DEPTH, CONVK, 8, 128).transpose(0, 3, 2, 1)),
        "bdw": fm(b_dw, 8), "clng": fm(cln_g, 8), "clnb": fm(cln_b, 8),
        "lng": np.ascontiguousarray(ln_g.reshape(DEPTH, 1, D_MODEL)),
        "lnb": np.ascontiguousarray(ln_b.reshape(DEPTH, 1, D_MODEL)),
    }
    in_maps = []
    for c in range(8):
        b, hf = c // 2, c % 2
        xp = x_prompt[b, hf * NPT:(hf + 1) * NPT, :]
        xs = x_sample[c * NSEQ:(c + 1) * NSEQ].reshape(NST, D_MODEL)
        xtok = np.ascontiguousarray(np.concatenate([xp, xs], 0))
        sl = slice(c * NSEQ, (c + 1) * NSEQ)
        sn = state_n[:, sl]
        snT = sn.reshape(DEPTH, NSEQ, NH, 2, 128).transpose(0, 4, 1, 2, 3).reshape(DEPTH, 128, 128)
        scv = state_conv[:, sl]
        scvT = scv.reshape(DEPTH, NSEQ, 30, 8, 128).transpose(0, 4, 3, 1, 2)
        scvK = np.ascontiguousarray(scvT[:, :, :, :, 8:30]).reshape(DEPTH, 128, 8 * NSEQ * 22)
        m = dict(shared)
        m.update({
            "xT": np.ascontiguousarray(xtok.T), "xtok": xtok,
            "sC": np.ascontiguousarray(state_C[:, sl]),
            "snT": np.ascontiguousarray(snT),
            "smT": np.ascontiguousarray(state_m[:, sl].transpose(0, 2, 1)),
            "scvT": np.ascontiguousarray(scvT).reshape(DEPTH, 128, 8 * NSEQ * 30), "scvK": scvK,
            "flag": np.full((128, 1), float(hf), f32),
        })
        in_maps.append(m)
    nc = _get_program()
    res = run_bass_kernel_spmd(nc, in_maps, core_ids=list(range(8)))
    R = res.results
    B = x_prompt.shape[0]
    y_prompt = np.empty_like(x_prompt)
    y_sample = np.empty_like(x_sample)
    Cp = np.empty((DEPTH, B, NH, DH, DH), f32)
    npr = np.empty((DEPTH, B, NH, DH), f32)
    mp = np.empty((DEPTH, B, NH), f32)
    cp = np.empty((DEPTH, B, CONVK - 1, CW), f32)
    Cs = np.empty((DEPTH, 128, NH, DH, DH), f32)
    ns = np.empty((DEPTH, 128, NH, DH), f32)
    ms = np.empty((DEPTH, 128, NH), f32)
    cs = np.empty((DEPTH, 128, CONVK - 1, CW), f32)
    for c in range(8):
        r = R[c]
        b, hf = c // 2, c % 2
        y = np.asarray(r["y"], f32)
        y_prompt[b, hf * NPT:(hf + 1) * NPT] = y[:NPT]
        y_sample[c * NSEQ:(c + 1) * NSEQ] = y[NPT:].reshape(NSEQ, 8, D_MODEL)
        sl = slice(c * NSEQ, (c + 1) * NSEQ)
        Cs[:, sl] = np.asarray(r["Cs"], f32)
        ns[:, sl] = np.asarray(r["nso"], f32).reshape(DEPTH, 128, NSEQ, NH, 2).transpose(0, 2, 3, 4, 1).reshape(DEPTH, NSEQ, NH, DH)
        ms[:, sl] = np.asarray(r["mso"], f32).transpose(0, 2, 1)
        cso = np.concatenate([np.asarray(r["cso_old"], f32).reshape(DEPTH, 128, 8, NSEQ, 22),
                              np.asarray(r["cso_new"], f32).reshape(DEPTH, 128, 8, NSEQ, 8)], axis=4)
        cs[:, sl] = cso.transpose(0, 3, 4, 2, 1).reshape(DEPTH, NSEQ, 30, CW)
        if hf == 1:
            Cp[:, b] = np.asarray(r["Cp"], f32)
            npr[:, b] = np.asarray(r["npo"], f32).reshape(DEPTH, 128, NH, 2).transpose(0, 2, 3, 1).reshape(DEPTH, NH, DH)
            mp[:, b] = np.asarray(r["mpo"], f32).reshape(DEPTH, NH)
            cp[:, b] = np.asarray(r["cpo"], f32).transpose(0, 3, 2, 1).reshape(DEPTH, 30, CW)
    return (y_prompt, y_sample, Cp, npr, mp, cp, Cs, ns, ms, cs)
```

```python
import numpy as np
import concourse.bass as bass
import concourse.mybir as mybir
from concourse.bass_utils import run_bass_kernel_spmd

F32 = mybir.dt.float32
BF16 = mybir.dt.bfloat16
AF = mybir.ActivationFunctionType
ALU = mybir.AluOpType
AX = mybir.AxisListType

D_MODEL = 2048
DEPTH = 2
NH = 4
DH = 256
MW = 1024
CW = 1024
CONVK = 31
PROJ = 8200
NPT = 1024
NST = 128
T = NPT + NST
NTT = T // 128
NCH = NPT // 128
KT = D_MODEL // 128
NSEQ = 16
LN_EPS = 1e-5
DN_ALPHA = (2 * DEPTH) ** 0.25
GW = 256
NEG_BIG = -1.0e30
import os
DEBUG = bool(int(os.environ.get('KDEBUG', '0')))

OFF_Q, OFF_K, OFF_V, OFF_O, OFF_ZM = 0, 1024, 2048, 3072, 4096
OFF_GA, OFF_GG, OFF_ZC, OFF_GATE = 5120, 6144, 7168, 8192


class Res:
    __slots__ = ("name", "w", "r")

    def __init__(self, name, inherit=None):
        self.name = name
        self.w = None
        self.r = dict(inherit) if inherit else {}


class Sched:
    ENGS = ("pe", "act", "dve", "pool", "sp")

    def __init__(self, nc):
        self.nc = nc
        self.ops = {e: [] for e in self.ENGS}
        self.sems = {}
        self.cnt = {}
        self.waited = {e: {} for e in self.ENGS}
        self._ctx = []
        for e in self.ENGS:
            self._mksem("E_" + e)

    def _mksem(self, key):
        cm = self.nc.semaphore(key)
        h = cm.__enter__()
        self._ctx.append(cm)
        self.sems[key] = h
        self.cnt[key] = 0
        return h

    def chan(self, name):
        key = "D_" + name
        if key not in self.sems:
            self._mksem(key)
        return key

    def _need(self, eng, tick, waits):
        if tick is None:
            return
        key, val = tick
        if self.waited[eng].get(key, 0) >= val:
            return
        self.waited[eng][key] = val
        waits[key] = max(waits.get(key, 0), val)

    def op(self, eng, method, kw, reads=(), writes=(), inc=True, chan=None, amount=16, noncontig=False):
        fn = (method, kw, noncontig)
        own = "E_" + eng
        waits = {}
        same = chan is None
        for r in reads:
            self._need(eng, r.w, waits)
        for w in writes:
            strict = not (same and eng == "pe")
            if w.w is not None and (strict or w.w[0] != own):
                self._need(eng, w.w, waits)
            for k, v in w.r.items():
                if not strict and k == own:
                    continue
                self._need(eng, (k, v), waits)
        if chan is not None:
            self.cnt[chan] += amount
            tick = (chan, self.cnt[chan])
            self.ops[eng].append((waits, fn, (chan, amount)))
        elif inc:
            self.cnt[own] += 1
            tick = (own, self.cnt[own])
            self.ops[eng].append((waits, fn, (own, 1)))
        else:
            tick = (own, self.cnt[own] + 1)
            self.ops[eng].append((waits, fn, None))
        for r in reads:
            if r.r.get(tick[0], 0) < tick[1]:
                r.r[tick[0]] = tick[1]
        for w in writes:
            w.w = tick
            w.r = {}
        return tick

    def dma(self, eng, out, in_, reads=(), writes=(), chan="x", noncontig=False):
        ck = self.chan(chan)
        return self.op(eng, "dma_start", dict(out=out, in_=in_), reads=reads, writes=writes, chan=ck, noncontig=noncontig)

    def wait_all(self, eng, ress):
        waits = {}
        for r in ress:
            self._need(eng, r.w, waits)
            for k, v in r.r.items():
                self._need(eng, (k, v), waits)
        self.ops[eng].append((waits, None, None))

    def emit(self):
        nc = self.nc
        getters = {"pe": "tensor", "act": "scalar", "dve": "vector", "pool": "gpsimd", "sp": "sync"}
        with nc.Block() as block:
            for e in self.ENGS:
                ops = self.ops[e]
                if not ops:
                    continue

                def body(engh, ops=ops):
                    for waits, fn, inc in ops:
                        for k, v in waits.items():
                            engh.wait_ge(self.sems[k], v)
                        if fn is None:
                            continue
                        method, kw, noncontig = fn
                        if noncontig:
                            with nc.allow_non_contiguous_dma(reason="tiny strided transfer"):
                                ins = getattr(engh, method)(**kw)
                        else:
                            ins = getattr(engh, method)(**kw)
                        if inc is not None:
                            ins.then_inc(self.sems[inc[0]], inc[1])

                getattr(block, getters[e])(body)


class Arena:
    def __init__(self, nc, words):
        self.words = words
        self.t = nc.alloc_sbuf_tensor("arena", [128, words], F32)
        self.top = 0
        self.stack = []
        self.live = []
        self.dead = []
        self.peak = 0

    def push(self):
        self.stack.append((self.top, len(self.live)))

    def pop(self):
        top, nlive = self.stack.pop()
        for (s, e, ress) in self.live[nlive:]:
            ticks = {}
            for r in ress:
                if r.w is not None:
                    ticks[r.w[0]] = max(ticks.get(r.w[0], 0), r.w[1])
                for k, v in r.r.items():
                    ticks[k] = max(ticks.get(k, 0), v)
            self.dead.append((s, e, ticks))
        del self.live[nlive:]
        self.top = top

    def alloc(self, name, shape, dt):
        n = int(np.prod(shape[1:]))
        words = n if dt == F32 else (n + 1) // 2
        words = (words + 7) // 8 * 8
        a0 = self.top
        a1 = a0 + words
        if a1 > self.words:
            raise RuntimeError(f"SBUF arena overflow allocating {name}: need {a1} words of {self.words}")
        self.top = a1
        self.peak = max(self.peak, a1)
        inherit = {}
        keep = []
        for (s, e, ticks) in self.dead:
            if s < a1 and a0 < e:
                for k, v in ticks.items():
                    inherit[k] = max(inherit.get(k, 0), v)
                if not (a0 <= s and e <= a1):
                    keep.append((s, e, ticks))
            else:
                keep.append((s, e, ticks))
        self.dead = keep
        v = self.t[:, a0:a1]
        if dt != F32:
            v = v.bitcast(dt)
        v = v[:, 0:n]
        if len(shape) == 3:
            v = v.rearrange("p (a b) -> p a b", b=shape[2])
        elif len(shape) == 4:
            v = v.rearrange("p (a b c) -> p a b c", b=shape[2], c=shape[3])
        ress = []
        self.live.append((a0, a1, ress))
        t_ = Tile(name, v, ress, inherit)
        t_.base, t_.nwords = a0, words
        return t_


def _ticks_of(tiles):
    t = {}
    for tl in tiles:
        for r in tl._ress:
            if r.w is not None:
                t[r.w[0]] = max(t.get(r.w[0], 0), r.w[1])
            for k, v in r.r.items():
                t[k] = max(t.get(k, 0), v)
    return t


def arena_alloc_at(arena, name, shape, dt, a0, over_tiles):
    n = int(np.prod(shape[1:]))
    words = n if dt == F32 else (n + 1) // 2
    words = (words + 7) // 8 * 8
    v = arena.t[:, a0:a0 + words]
    if dt != F32:
        v = v.bitcast(dt)
    v = v[:, 0:n]
    if len(shape) == 3:
        v = v.rearrange("p (a b) -> p a b", b=shape[2])
    inh = _ticks_of(over_tiles)
    for tl in over_tiles:
        for k, vv in (tl._inh or {}).items():
            inh[k] = max(inh.get(k, 0), vv)
    for (s_, e_, ticks) in arena.dead:
        if s_ < a0 + words and a0 < e_:
            for k, vv in ticks.items():
                inh[k] = max(inh.get(k, 0), vv)
    t_ = Tile(name, v, [], inh)
    t_.base, t_.nwords = a0, words
    return t_


def arena_adopt(arena, tile):
    assert arena.top == tile.base, (arena.top, tile.base)
    arena.top += tile.nwords
    arena.peak = max(arena.peak, arena.top)
    arena.live.append((tile.base, tile.base + tile.nwords, tile._ress))


class SubArena:
    def __init__(self, arena, tile, base_words, nwords):
        self.arena = arena
        self.tile = tile
        self.base = base_words
        self.nwords = nwords
        self.top = 0
        self.inherit = dict(tile._inh) if tile._inh else {}
        for r in tile._ress:
            if r.w is not None:
                self.inherit[r.w[0]] = max(self.inherit.get(r.w[0], 0), r.w[1])
            for k, v in r.r.items():
                self.inherit[k] = max(self.inherit.get(k, 0), v)
        self.ress = []

    def alloc(self, name, shape, dt):
        n = int(np.prod(shape[1:]))
        words = n if dt == F32 else (n + 1) // 2
        words = (words + 7) // 8 * 8
        a0 = self.base + self.top
        a1 = a0 + words
        if self.top + words > self.nwords:
            raise RuntimeError(f"sub-arena overflow allocating {name}")
        self.top += words
        v = self.arena.t[:, a0:a1]
        if dt != F32:
            v = v.bitcast(dt)
        v = v[:, 0:n]
        if len(shape) == 3:
            v = v.rearrange("p (a b) -> p a b", b=shape[2])
        elif len(shape) == 4:
            v = v.rearrange("p (a b c) -> p a b c", b=shape[2], c=shape[3])
        return Tile(name, v, self.ress, self.inherit)

    def close(self):
        ticks = {}
        for r in self.ress:
            if r.w is not None:
                ticks[r.w[0]] = max(ticks.get(r.w[0], 0), r.w[1])
            for k, v in r.r.items():
                ticks[k] = max(ticks.get(k, 0), v)
        for r in self.tile._ress:
            for k, v in ticks.items():
                if r.r.get(k, 0) < v:
                    r.r[k] = v


class Tile:
    def __init__(self, name, ap, ress, inherit):
        self.name = name
        self.ap = ap
        self._ress = ress
        self._inh = inherit
        self._d = {}

    def R(self, *idx):
        r = self._d.get(idx)
        if r is None:
            r = Res(f"{self.name}{idx}", self._inh)
            self._d[idx] = r
            self._ress.append(r)
        return r

    def __getitem__(self, k):
        return self.ap[k]


class PSum:
    def __init__(self, nc):
        self.t = nc.alloc_psum_tensor("ps", [128, 8, 512], F32)
        self.tb = self.t.bitcast(BF16)
        self.res = [Res(f"bank{i}") for i in range(8)]
        self.ptr = 0
        self.pinned = set()

    def get(self, n=1):
        for _ in range(16):
            if self.ptr + n > 8:
                self.ptr = 0
            b = self.ptr
            if any((b + i) in self.pinned for i in range(n)):
                self.ptr = (b + 1) % 8
                continue
            self.ptr = (self.ptr + n) % 8
            return b, self.res[b:b + n]
        raise RuntimeError("no free PSUM banks")


def build_program():
    nc = bass.Bass("TRN2", target_bir_lowering=False)
    S = Sched(nc)

    def din(name, shape):
        return nc.dram_tensor(name, shape, F32, kind="ExternalInput").ap()

    def dout(name, shape):
        return nc.dram_tensor(name, shape, F32, kind="ExternalOutput").ap()

    xT_d = din("xT", [D_MODEL, T])
    xtok_d = din("xtok", [T, D_MODEL])
    sC_d = din("sC", [DEPTH, NSEQ, NH, DH, DH])
    snT_d = din("snT", [DEPTH, 128, 128])
    smT_d = din("smT", [DEPTH, NH, NSEQ])
    scvT_d = din("scvT", [DEPTH, 128, 8 * NSEQ * 30])
    scvK_d = din("scvK", [DEPTH, 128, 8 * NSEQ * 22])
    w_in_d = din("w_in", [DEPTH, D_MODEL, PROJ])
    w_out_d = din("w_out", [DEPTH, D_MODEL, D_MODEL])
    bfm_d = din("bfm", [DEPTH, 128, 64])
    bg_d = din("bg", [DEPTH, 1, 8])
    bv_d = din("bv", [DEPTH, 1, MW])
    hng_d = din("hng", [DEPTH, 128, 8])
    wdw_d = din("wdw", [DEPTH, 128, 8, CONVK])
    bdw_d = din("bdw", [DEPTH, 128, 8])
    clng_d = din("clng", [DEPTH, 128, 8])
    clnb_d = din("clnb", [DEPTH, 128, 8])
    lng_d = din("lng", [DEPTH, 1, D_MODEL])
    lnb_d = din("lnb", [DEPTH, 1, D_MODEL])
    flag_d = din("flag", [128, 1])
    y_d = dout("y", [T, D_MODEL])
    Cp_d = dout("Cp", [DEPTH, NH, DH, DH])
    npo_d = dout("npo", [DEPTH, 128, 8])
    mpo_d = dout("mpo", [DEPTH, NH, 1])
    cpo_d = dout("cpo", [DEPTH, 128, 8, 30])
    Cs_d = dout("Cs", [DEPTH, NSEQ, NH, DH, DH])
    nso_d = dout("nso", [DEPTH, 128, 128])
    mso_d = dout("mso", [DEPTH, NH, NSEQ])
    cso_old_d = dout("cso_old", [DEPTH, 128, 8 * NSEQ * 22])
    cso_new_d = dout("cso_new", [DEPTH, 128, 8 * NSEQ * 8])

    y1_d = nc.dram_tensor("y1scr", [T, D_MODEL], F32).ap()
    R_y1 = [Res(f"y1_{tt}") for tt in range(NTT)]
    bnc1 = [nc.dram_tensor(f"bnc1_{l}", [128, 248], F32).ap() for l in range(DEPTH)]
    gat1 = [nc.dram_tensor(f"gat1_{l}", [256, 248], F32).ap() for l in range(DEPTH)]
    bnc2 = [nc.dram_tensor(f"bnc2_{l}", [128, 2056], F32).ap() for l in range(DEPTH)]
    gat2 = [nc.dram_tensor(f"gat2_{l}", [256, 2056], F32).ap() for l in range(DEPTH)]
    wob_d = [nc.dram_tensor(f"wob{l}", [D_MODEL, D_MODEL], BF16).ap() for l in range(DEPTH)]
    R_wob = [Res(f"wob{l}") for l in range(DEPTH)]
    GROUPS = [[0, 1], [2, 3], [4, 5], [6, 7]]

    A = Arena(nc, 52800)
    PS = PSum(nc)
    ps = PS.t
    psb = PS.tb

    def ACT(R, W, **kw):
        S.op("act", "activation", kw, R, W)

    def DV(m, R, W, **kw):
        S.op("dve", m, kw, R, W)

    def PL(m, R, W, **kw):
        S.op("pool", m, kw, R, W)

    def MM(R, W, inc=True, **kw):
        S.op("pe", "matmul", kw, R, W, inc=inc)

    def TR(R, W, out, in_):
        S.op("pe", "transpose", dict(out=out, in_=in_, identity=identb), R, W)

    cst = A.alloc("cst", [128, 544], F32)
    c_off = [0]

    def cslice(n):
        a = c_off[0]
        c_off[0] += n
        return cst.ap[:, a:a + n]

    ident = cslice(128)
    maskp = cslice(128)
    masks = cslice(128)
    blk = cslice(16)
    ones_c = cslice(128)
    i4 = cslice(4)
    epsc = cslice(1)
    flag = cslice(1)
    R_c = cst.R()
    cstb = A.alloc("cstb", [128, 128], BF16)
    identb = cstb.ap[:, 0:128]
    R_cb = cstb.R()

    PL("memset", [], [R_c], ap=cst.ap[:, 0:544], constant=0.0)
    PL("memset", [], [R_c], ap=ident, constant=1.0)
    PL("affine_select", [R_c], [R_c], out=ident, in_=ident, pattern=[[-1, 128]], compare_op=ALU.is_equal, fill=0.0, base=0, channel_multiplier=1)
    PL("memset", [], [R_c], ap=maskp, constant=1.0)
    PL("affine_select", [R_c], [R_c], out=maskp, in_=maskp, pattern=[[1, 128]], compare_op=ALU.is_ge, fill=0.0, base=0, channel_multiplier=-1)
    PL("memset", [], [R_c], ap=blk, constant=1.0)
    PL("affine_select", [R_c], [R_c], out=blk, in_=blk, pattern=[[-8, 16]], compare_op=ALU.is_ge, fill=0.0, base=0, channel_multiplier=1)
    PL("affine_select", [R_c], [R_c], out=blk, in_=blk, pattern=[[8, 16]], compare_op=ALU.is_ge, fill=0.0, base=7, channel_multiplier=-1)
    PL("tensor_tensor", [R_c], [R_c], out=masks.rearrange("p (j i) -> p j i", i=8), in0=maskp.rearrange("p (j i) -> p j i", i=8),
       in1=blk.unsqueeze(2).to_broadcast([128, 16, 8]), op=ALU.mult)
    PL("memset", [], [R_c], ap=ones_c, constant=1.0)
    PL("tensor_copy", [R_c], [R_c], out=i4[0:4, :], in_=ident[0:4, 0:4])
    PL("memset", [], [R_c], ap=epsc, constant=LN_EPS)
    S.dma("sp", flag, flag_d[:, :], writes=[R_c], chan="flag")
    PL("tensor_copy", [R_c], [R_cb], out=identb, in_=ident)

    PO = {}
    o = 0
    for nm, n in (("bfm", 64), ("hng", 8), ("bk16", 8), ("wdw", 8 * CONVK), ("bdw", 8), ("clng", 8), ("clnb", 8), ("bg", 8)):
        PO[nm] = (o, n)
        o += n
    prm = A.alloc("prm", [128, DEPTH, o], F32)
    R_prm = prm.R()

    def P(l, nm):
        a, n = PO[nm]
        return prm.ap[:, l, a:a + n]

    for l in range(DEPTH):
        S.dma("sp", P(l, "bfm"), bfm_d[l], writes=[R_prm], chan="cst")
        S.dma("sp", P(l, "hng"), hng_d[l], writes=[R_prm], chan="cst")
        S.dma("sp", P(l, "wdw"), wdw_d[l].rearrange("p c j -> p (c j)"), writes=[R_prm], chan="cst")
        S.dma("sp", P(l, "bdw"), bdw_d[l], writes=[R_prm], chan="cst")
        S.dma("sp", P(l, "clng"), clng_d[l], writes=[R_prm], chan="cst")
        S.dma("sp", P(l, "clnb"), clnb_d[l], writes=[R_prm], chan="cst")
        S.dma("sp", P(l, "bg"), bg_d[l].partition_broadcast(128), writes=[R_prm], chan="cst")
    for l in range(DEPTH):
        DV("tensor_scalar", [R_prm], [R_prm], out=P(l, "bk16"), in0=P(l, "bfm")[:, 8:16], scalar1=1.0 / 16.0, scalar2=None, op0=ALU.mult)

    xT = A.alloc("xT", [128, KT, T], BF16)
    wbufs = [A.alloc(f"wb{i}", [128, KT, GW], BF16) for i in range(3)]
    wstate = {"i": 0}

    S.dma("pool", xT.ap[:, :, :], xT_d.rearrange("(k p) t -> p k t", p=128), writes=[xT.R(tt) for tt in range(NTT)], chan="xT")

    def load_w(src_ap, ncols=GW):
        i = wstate["i"] % 3
        wstate["i"] += 1
        wb = wbufs[i]
        S.dma("pool", wb.ap[:, :, 0:ncols], src_ap.rearrange("(k p) n -> p k n", p=128), writes=[wb.R(0), wb.R(1)], chan=f"w{i}")
        return wb, wb

    def xT_res(t0, t1):
        return [xT.R(tt) for tt in range(t0 // 128, (t1 + 127) // 128)]

    TG = [(0, 384), (384, 768), (768, 1152)]

    def proj_fm(wb, wres, j, evac):
        b0, bres = PS.get(3)
        for kt in range(KT):
            for g, (t0, t1) in enumerate(TG):
                last = (kt == KT - 1 and g == 2)
                MM([wres.R(j)] + xT_res(t0, t1), ([bres[g]] if kt == 0 else []) + (bres if last else []), inc=last,
                   out=ps[:, b0 + g, 0:384], lhsT=wb.ap[:, kt, j * 128:(j + 1) * 128], rhs=xT.ap[:, kt, t0:t1],
                   start=(kt == 0), stop=(kt == KT - 1))
        for g in range(3):
            evac(g, ps[:, b0 + g, 0:384], bres[g])

    def proj_fm_gen(wb, wres, j, evac, seg=4):
        b0, bres = PS.get(3)
        PS.pinned.update((b0, b0 + 1, b0 + 2))
        for kt in range(KT):
            for g, (t0, t1) in enumerate(TG):
                last = (kt == KT - 1 and g == 2)
                MM([wres.R(j)] + xT_res(t0, t1), ([bres[g]] if kt == 0 else []) + (bres if last else []), inc=last,
                   out=ps[:, b0 + g, 0:384], lhsT=wb.ap[:, kt, j * 128:(j + 1) * 128], rhs=xT.ap[:, kt, t0:t1],
                   start=(kt == 0), stop=(kt == KT - 1))
            if kt % seg == seg - 1 and kt != KT - 1:
                yield
        PS.pinned.difference_update((b0, b0 + 1, b0 + 2))
        for g in range(3):
            evac(g, ps[:, b0 + g, 0:384], bres[g])
        yield

    def layer(l):
        A.push()
        kstate = {}
        mixc = A.alloc("mixc", [128, 8, T], BF16)

        gt = A.alloc("gates", [128, 1100], F32)
        g_off = [0]

        def gsl(n):
            a = g_off[0]
            g_off[0] += n
            assert g_off[0] <= 1100
            return gt.ap[:, a:a + n]

        Gt = gsl(72).rearrange("p (t c) -> p t c", c=8)
        Lt = gsl(36).rearrange("p (t c) -> p t c", c=4)
        gg_ = gsl(36).rearrange("p (t c) -> p t c", c=4)
        nb = gsl(36).rearrange("p (t c) -> p t c", c=4)
        eg = gsl(36).rearrange("p (t c) -> p t c", c=4)
        enb = gsl(36).rearrange("p (t c) -> p t c", c=4)
        Et = gsl(36).rearrange("p (t c) -> p t c", c=4)
        EB = gsl(32).rearrange("p (h c) -> p h c", c=8)
        BCs = gsl(192).rearrange("p (h v j) -> p h v j", v=3, j=16)
        EMF = gsl(4)
        Bs = gsl(24)
        Gmx = gsl(24)
        eBv = gsl(8)
        Mloc = gsl(1)
        Btot = gsl(1)
        mA = gsl(1)
        minit = gsl(1)
        mfin = gsl(1)
        emfv = gsl(1)
        mprev = gsl(16)
        mnew = gsl(16)
        V3f = gsl(48)
        V3 = V3f.rearrange("p (v j) -> p v j", j=16)
        tmp4 = gsl(16)
        X4 = gsl(192)
        R_g = gt.R()

        def bcast4(src, n, dst, dst_view):
            X = X4[0:4, 0:4 * n].rearrange("p (h n) -> p h n", n=n)
            DV("tensor_tensor", [R_g, R_c], [R_g], out=X, in0=src.unsqueeze(1).to_broadcast([4, 4, n]),
               in1=i4[0:4, :].unsqueeze(2).to_broadcast([4, 4, n]), op=ALU.mult)
            b0, br = PS.get(1)
            MM([R_g, R_c], br, out=ps[:, b0, 0:4 * n], lhsT=ones_c[0:4, :], rhs=X4[0:4, 0:4 * n], start=True, stop=True)
            ACT(br, [R_g], out=dst, in_=dst_view(ps[:, b0, 0:4 * n]), func=AF.Copy)

        def gate_chain():
            wg, wgres = load_w(w_in_d[l][:, OFF_GATE:OFF_GATE + 8], ncols=8)
            bg0, bgres = PS.get(1)
            for tt in range(NTT):
                for kt in range(KT):
                    last = (tt == NTT - 1 and kt == KT - 1)
                    MM([wgres.R(0), xT.R(tt)], (bgres if (tt == 0 and kt == 0) or last else []), inc=last,
                       out=ps[:, bg0, tt * 8:(tt + 1) * 8], lhsT=xT.ap[:, kt, tt * 128:(tt + 1) * 128], rhs=wg.ap[:, kt, 0:8],
                       start=(kt == 0), stop=(kt == KT - 1))
            DV("tensor_tensor", bgres + [R_prm], [R_g], out=Gt, in0=ps[:, bg0, 0:72].rearrange("p (t c) -> p t c", c=8),
               in1=P(l, "bg").unsqueeze(1).to_broadcast([128, NTT, 8]), op=ALU.add)
            ACT([R_g], [R_g], out=Et, in_=Gt[:, :, 4:8], func=AF.Exp, scale=-1.0)
            ACT([R_g], [R_g], out=Lt, in_=Et, func=AF.Ln, bias=1.0)
            bc0, bcres = PS.get(1)
            for tt in range(NTT):
                m_ = maskp if tt < NCH else masks
                MM([R_c, R_g], bcres, out=ps[:, bc0, tt * 4:(tt + 1) * 4], lhsT=m_, rhs=Lt[:, tt, :], start=True, stop=True)
            csv = ps[:, bc0, 0:36].rearrange("p (t c) -> p t c", c=4)
            DV("tensor_tensor", bcres + [R_g], [R_g], out=gg_, in0=Gt[:, :, 0:4], in1=csv, op=ALU.add)
            ACT(bcres, [R_g], out=nb, in_=csv, func=AF.Copy)
            ACT([R_g], [R_g], out=eg, in_=gg_, func=AF.Exp)
            ACT([R_g], [R_g], out=enb, in_=nb, func=AF.Exp)
            bs0, bsres = PS.get(1)
            for c in range(NCH):
                MM([R_c, R_g], bsres, out=ps[0:4, bs0, c:c + 1], lhsT=Lt[:, c, :], rhs=ones_c[:, 0:1], start=True, stop=True)
            MM([R_c, R_g], bsres, out=ps[0:4, bs0, 8:24], lhsT=Lt[:, NCH, :], rhs=blk, start=True, stop=True)
            DV("tensor_copy", bsres, [R_g], out=Bs[0:4, :], in_=ps[0:4, bs0, 0:24])
            gT0, gTres = PS.get(3)
            for tt in range(NTT):
                bnk, col = gT0 + tt // 4, (tt % 4) * 128
                MM([R_c, R_g], [gTres[tt // 4]], out=ps[0:4, bnk, col:col + 128], lhsT=gg_[:, tt, :], rhs=ident, start=True, stop=True)
            for hb in range(2):
                DV("tensor_reduce", [gTres[hb]], [R_g], out=Gmx[0:4, hb * 4:(hb + 1) * 4],
                   in_=ps[0:4, gT0 + hb, 0:512].rearrange("p (c t) -> p c t", t=128), axis=AX.X, op=ALU.max)
            DV("tensor_reduce", [gTres[2]], [R_g], out=Gmx[0:4, 8:24], in_=ps[0:4, gT0 + 2, 0:128].rearrange("p (j i) -> p j i", i=8),
               axis=AX.X, op=ALU.max)
            DV("memset", [], [R_g], ap=Mloc[0:4, :], constant=NEG_BIG)
            for c in range(NCH):
                DV("scalar_tensor_tensor", [R_g], [R_g], out=Mloc[0:4, :], in0=Mloc[0:4, :], scalar=Gmx[0:4, c:c + 1], in1=Bs[0:4, c:c + 1],
                   op0=ALU.max, op1=ALU.subtract)
            DV("tensor_reduce", [R_g], [R_g], out=Btot[0:4, :], in_=Bs[0:4, 0:8], axis=AX.X, op=ALU.add)
            DV("tensor_scalar", [R_g], [R_g], out=Btot[0:4, :], in0=Btot[0:4, :], scalar1=-1.0, scalar2=None, op0=ALU.mult)
            DV("tensor_tensor", [R_g], [R_g], out=mA[0:4, :], in0=Btot[0:4, :], in1=Mloc[0:4, :], op=ALU.max)
            ACT([R_g], [R_g], out=eBv[0:4, :], in_=Bs[0:4, 0:8], func=AF.Exp, scale=-1.0)

            bcast4(eBv[0:4, :], 8, EB, lambda a: a.rearrange("p (h c) -> p h c", c=8))
            S.dma("sp", mprev[0:4, :], smT_d[l], writes=[R_g], chan="gat")
            DV("tensor_tensor", [R_g], [R_g], out=mnew[0:4, :], in0=mprev[0:4, :], in1=Gmx[0:4, 8:24], op=ALU.max)
            DV("tensor_tensor", [R_g], [R_g], out=mnew[0:4, :], in0=mnew[0:4, :], in1=Bs[0:4, 8:24], op=ALU.subtract)
            DV("tensor_tensor", [R_g], [R_g], out=tmp4[0:4, :], in0=mprev[0:4, :], in1=mnew[0:4, :], op=ALU.subtract)
            DV("tensor_tensor", [R_g], [R_g], out=tmp4[0:4, :], in0=tmp4[0:4, :], in1=Bs[0:4, 8:24], op=ALU.subtract)
            ACT([R_g], [R_g], out=V3[0:4, 0, :], in_=tmp4[0:4, :], func=AF.Exp)
            DV("tensor_tensor", [R_g], [R_g], out=tmp4[0:4, :], in0=mnew[0:4, :], in1=Bs[0:4, 8:24], op=ALU.add)
            ACT([R_g], [R_g], out=V3[0:4, 1, :], in_=tmp4[0:4, :], func=AF.Exp, scale=-1.0)
            ACT([R_g], [R_g], out=V3[0:4, 2, :], in_=mprev[0:4, :], func=AF.Exp)
            bcast4(V3f[0:4, :], 48, BCs, lambda a: a.rearrange("p (h v j) -> p h v j", v=3, j=16))
            S.dma("sp", mso_d[l], mnew[0:4, :], reads=[R_g], chan="osm")


        A.push()
        conv_base = A.top
        ub = A.alloc("ub", [128, 8, 30 + NPT], BF16)
        ubs = A.alloc("ubs", [128, 8, NSEQ, 38], BF16)
        dgt = [A.alloc(f"dgt{i}", [128, CONVK, 128], BF16) for i in range(2)]
        ycv = A.alloc("ycv", [128, 8, T], F32)
        tl32 = A.alloc("tl32", [128, 8, 30], F32)
        us32 = A.alloc("us32", [128, 8, NSEQ, 8], F32)
        YA = SubArena(A, dgt[1], dgt[1].base, dgt[1].nwords)
        hsb = YA.alloc("hsb", [128, 8, NSEQ * 30], BF16)
        P1 = A.alloc("P1", [128, 248], F32)
        hrecv = A.alloc("hrecv", [128, 248], F32)
        sgt = [A.alloc(f"sgt{i}", [128, 384], F32) for i in range(6)]
        sgi = [0]
        RUB = [ub.R(ct) for ct in range(8)]
        RUS = [ubs.R(ct) for ct in range(8)]
        S.dma("pool", hsb.ap[:, :, :].rearrange("p c x -> p (c x)"), scvT_d[l], writes=[hsb.R()], chan="scv")
        ACT([hsb.R()], RUS, out=ubs.ap[:, :, :, 0:30], in_=hsb.ap[:, :, :].rearrange("p c (j t) -> p c j t", t=30), func=AF.Copy)
        YA.close()
        S.dma("sp", cso_old_d[l], scvK_d[l], chan="ocs0")

        for ct in range(8):
            i_w = wstate["i"] % 3
            wstate["i"] += 1
            wb = wbufs[i_w]
            for half, off in ((0, OFF_GA), (1, OFF_GG)):
                S.dma("pool", wb.ap[:, :, half * 128:(half + 1) * 128],
                      w_in_d[l][:, off + ct * 128:off + (ct + 1) * 128].rearrange("(k p) n -> p k n", p=128), writes=[wb.R(half)], chan=f"w{i_w}{'ab'[half]}")
            held = {}

            def evac_g(g, pap, bres, ct=ct):
                st = sgt[sgi[0] % 6]
                sgi[0] += 1
                ACT([bres, R_prm], [st.R()], out=st.ap[:, :], in_=pap, func=AF.Sigmoid, bias=P(l, "bfm")[:, 48 + ct:49 + ct])
                held[g] = st
            proj_fm(wb, wb, 1, evac_g)

            def evac_a(g, pa, bra, ct=ct):
                st = held[g]
                bga = P(l, "bfm")[:, 40 + ct:41 + ct]
                t0, t1 = TG[g]
                if g < 2:
                    DV("scalar_tensor_tensor", [bra, st.R(), R_prm], [RUB[ct]], out=ub.ap[:, ct, 30 + t0:30 + t1], in0=pa, scalar=bga, in1=st.ap[:, :],
                       op0=ALU.add, op1=ALU.mult)
                else:
                    DV("scalar_tensor_tensor", [bra, st.R(), R_prm], [RUB[ct]], out=ub.ap[:, ct, 30 + 768:30 + NPT], in0=pa[:, 0:256], scalar=bga,
                       in1=st.ap[:, 0:256], op0=ALU.add, op1=ALU.mult)
                    DV("scalar_tensor_tensor", [bra, st.R(), R_prm], [tl32.R()], out=tl32.ap[:, ct, :], in0=pa[:, 226:256], scalar=bga,
                       in1=st.ap[:, 226:256], op0=ALU.add, op1=ALU.mult)
                    DV("scalar_tensor_tensor", [bra, st.R(), R_prm], [us32.R()], out=us32.ap[:, ct, :, :], in0=pa[:, 256:384].rearrange("p (j i) -> p j i", i=8),
                       scalar=bga, in1=st.ap[:, 256:384].rearrange("p (j i) -> p j i", i=8), op0=ALU.add, op1=ALU.mult)
                    ACT([us32.R()], [RUS[ct]], out=ubs.ap[:, ct, :, 30:38], in_=us32.ap[:, ct, :, :], func=AF.Copy)
            proj_fm(wb, wb, 0, evac_a)
            if ct == 0:
                gate_chain()
        zc_pre = [load_w(w_in_d[l][:, OFF_ZC + pair * GW:OFF_ZC + (pair + 1) * GW]) for pair in range(2)]
        R_P1 = P1.R()
        DV("memset", [], [R_P1], ap=P1.ap[:, 240:248], constant=0.0)
        DV("tensor_copy", [tl32.R()], [R_P1], out=P1.ap[:, 0:240].rearrange("p (c t) -> p c t", t=30), in_=tl32.ap[:, :, :])
        DV("tensor_copy", [R_g], [R_P1], out=P1.ap[0:4, 240:241], in_=mA[0:4, :])
        R_b1 = Res("bnc1")
        R_g1 = Res("gat1")
        S.dma("sp", bnc1[l][:, :], P1.ap[:, :], reads=[R_P1], writes=[R_b1], chan="x1")
        S.dma("sp", cpo_d[l].rearrange("p c t -> p (c t)"), P1.ap[:, 0:240], reads=[R_P1], chan="ocp")
        S.dma("sp", cso_new_d[l], us32.ap[:, :, :, :].rearrange("p c j i -> p (c j i)"), reads=[us32.R()], chan="ocs1")
        S.op("pool", "collective_compute", dict(kind="AllGather", op=ALU.bypass, replica_groups=GROUPS, ins=[bnc1[l][:, :]], outs=[gat1[l][:, :]]),
             reads=[R_b1], writes=[R_g1], chan=S.chan(f"cc1_{l}"), amount=1)
        R_hr = hrecv.R()
        S.dma("sp", hrecv.ap[:, :], gat1[l][0:128, :], reads=[R_g1], writes=[R_hr], chan="x1r")
        for pair in range(4):
            wz, wzres = zc_pre[pair] if pair < 2 else load_w(w_in_d[l][:, OFF_ZC + pair * GW:OFF_ZC + (pair + 1) * GW])
            for j in range(2):
                ct = pair * 2 + j

                def evac_z(g, pap, bres, ct=ct):
                    t0, t1 = TG[g]
                    ACT([bres, R_prm], [mixc.R(ct, g)], out=mixc.ap[:, ct, t0:t1], in_=pap, func=AF.Silu, bias=P(l, "bfm")[:, 56 + ct:57 + ct])
                proj_fm(wz, wzres, j, evac_z)
        DV("tensor_scalar", [R_hr, R_c] + RUB, RUB, out=ub.ap[:, :, 0:30], in0=hrecv.ap[:, 0:240].rearrange("p (c t) -> p c t", t=30),
           scalar1=flag, scalar2=None, op0=ALU.mult)
        DV("tensor_scalar", [R_hr, R_c], [R_g], out=minit[0:4, :], in0=hrecv.ap[0:4, 240:241], scalar1=flag[0:4, :], scalar2=None, op0=ALU.mult)
        S.dma("pool", wob_d[l][:, :], w_out_d[l][:, :], writes=[R_wob[l]], chan="wob")
        wd = P(l, "wdw").rearrange("p (c j) -> p c j", j=CONVK)
        segs = [(0, 512), (512, 1024), (1024, 1152)]
        for ct in range(8):
            dg = dgt[ct % 2]
            DV("tensor_tensor", [R_cb, R_prm], [dg.R()], out=dg.ap[:, :, :], in0=identb.unsqueeze(1).to_broadcast([128, CONVK, 128]),
               in1=wd[:, ct, :].unsqueeze(2).to_broadcast([128, CONVK, 128]), op=ALU.mult)
            for s, (t0, t1) in enumerate(segs):
                n = t1 - t0
                b0, br = PS.get(1)
                for j in range(CONVK):
                    if s < 2:
                        rhs = ub.ap[:, ct, t0 + j:t0 + j + 512]
                        rr = RUB[ct]
                    else:
                        rhs = ubs.ap[:, ct, :, j:j + 8]
                        rr = RUS[ct]
                    last = j == CONVK - 1
                    MM([dg.R(), rr], (br if j == 0 or last else []), inc=last, out=ps[:, b0, 0:n], lhsT=dg.ap[:, j, :], rhs=rhs,
                       start=(j == 0), stop=last)
                ACT(br + [R_prm], [ycv.R(ct)], out=ycv.ap[:, ct, t0:t1], in_=ps[:, b0, 0:n], func=AF.Identity, bias=P(l, "bdw")[:, ct:ct + 1])
        ysq = [A.alloc(f"ysq{i}", [128, 512], F32) for i in range(2)]
        UA = SubArena(A, ub, ub.base, ub.nwords)
        mean = UA.alloc("cmean", [128, T], F32)
        rstd = UA.alloc("crstd", [128, T], F32)
        msq = UA.alloc("cmsq", [128, T], F32)
        R_st = mean.R()
        qi = [0]
        for s, (t0, t1) in enumerate(segs):
            n = t1 - t0
            b0, br = PS.get(2)
            for ct in range(8):
                yv = ycv.ap[:, ct, t0:t1]
                sq = ysq[qi[0] % 2]
                qi[0] += 1
                ACT([ycv.R(ct)], [sq.R()], out=sq.ap[:, 0:n], in_=yv, func=AF.Square)
                MM([ycv.R(ct), R_c], [br[0]], out=ps[:, b0, 0:n], lhsT=ones_c, rhs=yv, start=(ct == 0), stop=(ct == 7))
                MM([sq.R(), R_c], [br[1]], out=ps[:, b0 + 1, 0:n], lhsT=ones_c, rhs=sq.ap[:, 0:n], start=(ct == 0), stop=(ct == 7))
            DV("tensor_scalar", [br[0]], [R_st], out=mean.ap[:, t0:t1], in0=ps[:, b0, 0:n], scalar1=1.0 / CW, scalar2=None, op0=ALU.mult)
            DV("tensor_scalar", [br[1]], [R_st], out=rstd.ap[:, t0:t1], in0=ps[:, b0 + 1, 0:n], scalar1=1.0 / CW, scalar2=None, op0=ALU.mult)
        DV("tensor_tensor", [R_st], [msq.R()], out=msq.ap[:, :], in0=mean.ap[:, :], in1=mean.ap[:, :], op=ALU.mult)
        DV("tensor_tensor", [R_st, msq.R()], [R_st], out=rstd.ap[:, :], in0=rstd.ap[:, :], in1=msq.ap[:, :], op=ALU.subtract)
        ACT([R_st, R_c], [R_st], out=rstd.ap[:, :], in_=rstd.ap[:, :], func=AF.Sqrt, bias=epsc)
        DV("reciprocal", [R_st], [R_st], out=rstd.ap[:, :], in_=rstd.ap[:, :])
        swt = [A.alloc(f"swt{i}", [128, 384], BF16) for i in range(2)]
        wi = [0]
        assert conv_base + 4608 >= ub.base + ub.nwords and conv_base + 2 * 4608 <= ycv.base
        kT = arena_alloc_at(A, "kT", [128, 8, T], BF16, conv_base + 4608, [ubs] + dgt)
        kstate["kT"] = kT

        def norm_units():
            for ct in range(8):
                for g, (t0, t1) in enumerate(TG):
                    yv = ycv.ap[:, ct, t0:t1]
                    DV("tensor_tensor", [ycv.R(ct), R_st], [ycv.R(ct)], out=yv, in0=yv, in1=mean.ap[:, t0:t1], op=ALU.subtract)
                    DV("tensor_tensor", [ycv.R(ct), R_st], [ycv.R(ct)], out=yv, in0=yv, in1=rstd.ap[:, t0:t1], op=ALU.mult)
                    sw = swt[wi[0] % 2]
                    wi[0] += 1
                    ACT([ycv.R(ct), R_prm], [sw.R()], out=sw.ap[:, :], in_=yv, func=AF.Silu, scale=P(l, "clng")[:, ct:ct + 1], bias=P(l, "clnb")[:, ct:ct + 1])
                    DV("tensor_tensor", [sw.R(), mixc.R(ct, g)], [mixc.R(ct, g)], out=mixc.ap[:, ct, t0:t1], in0=mixc.ap[:, ct, t0:t1], in1=sw.ap[:, :], op=ALU.mult)
                    yield
        nu = norm_units()
        for grp in range(4):
            wk, wkres = load_w(w_in_d[l][:, OFF_K + grp * GW:OFF_K + (grp + 1) * GW])
            for j in range(2):
                tile_ = grp * 2 + j

                def evac_k(g, pap, bres, tile_=tile_):
                    t0, t1 = TG[g]
                    ACT([bres, R_prm], [kT.R(tile_, g)], out=kT.ap[:, tile_, t0:t1], in_=pap, func=AF.Identity, scale=1.0 / 16.0,
                        bias=P(l, "bk16")[:, tile_:tile_ + 1])
                for _ in proj_fm_gen(wk, wkres, j, evac_k, seg=6):
                    next(nu, None)
        for _ in nu:
            pass
        UA.close()
        A.pop()

        A.push()
        mixm = A.alloc("mixm", [128, 8, T], BF16)
        A.push()
        kT = kstate["kT"]
        arena_adopt(A, kT)
        vp = A.alloc("vp", [128, NTT, NH, 257], BF16)
        Dst = A.alloc("Dst", [128, NH, 2, 257], F32)
        bvb = A.alloc("bvb", [128, MW], F32)
        S.dma("sp", bvb.ap[:, :], bv_d[l].partition_broadcast(128), writes=[bvb.R()], chan="bvb")
        DV("tensor_copy", [R_g], [vp.R(tt, h) for tt in range(NTT) for h in range(NH)], out=vp.ap[:, :, :, 256], in_=eg)
        qTs = A.alloc("qTs", [128, 8, NST], BF16)
        hs = A.alloc("hs", [128, NCH + 1, 256], F32)
        hn = [A.alloc(f"hn{i}", [128, 256], BF16) for i in range(2)]
        atm = [A.alloc(f"atm{i}", [128, 128], BF16) for i in range(2)]
        stt_ = A.alloc("bnst", [128, NCH + 1, 6], F32)
        mvt = A.alloc("bnmv", [128, NCH + 1, 2], F32)
        rdt = A.alloc("rdt", [128, 4], F32)
        A.push()
        vtmp = [A.alloc(f"vtmp{i}", [128, 256], F32) for i in range(2)]
        vi = [0]
        vw = {}

        def vproj_unit(h, tt):
            if tt == 0:
                vw[h] = load_w(w_in_d[l][:, OFF_V + h * GW:OFF_V + (h + 1) * GW])
            wv, wvres = vw[h]
            b0, br = PS.get(1)
            for kt in range(KT):
                last = kt == KT - 1
                MM([wvres.R(0), wvres.R(1), xT.R(tt)], (br if kt == 0 or last else []), inc=last,
                   out=ps[:, b0, 0:256], lhsT=xT.ap[:, kt, tt * 128:(tt + 1) * 128], rhs=wv.ap[:, kt, :], start=(kt == 0), stop=last)
            vt = vtmp[vi[0] % 2]
            vi[0] += 1
            DV("tensor_tensor", br + [bvb.R()], [vt.R()], out=vt.ap[:, :], in0=ps[:, b0, 0:256], in1=bvb.ap[:, h * 256:(h + 1) * 256], op=ALU.add)
            ACT([vt.R(), R_g], [vp.R(tt, h)], out=vp.ap[:, tt, h, 0:256], in_=vt.ap[:, :], func=AF.Identity, scale=eg[:, tt, h:h + 1])

        k2t = [A.alloc(f"k2t{i}", [128, 256], BF16) for i in range(3)]
        k2i = [0]

        def state_step(h, c, Dt, DR):
            tb, tbr = PS.get(1)
            for dt_ in range(2):
                TR([kT.R(h * 2 + dt_, c // 3), R_cb], tbr, out=psb[:, tb, dt_ * 128:(dt_ + 1) * 128], in_=kT.ap[:, h * 2 + dt_, c * 128:(c + 1) * 128])
            k2 = k2t[k2i[0] % 3]
            k2i[0] += 1
            ACT(tbr + [R_g], [k2.R()], out=k2.ap[:, :], in_=psb[:, tb, 0:256], func=AF.Identity, scale=EB[:, h, c:c + 1])
            b0, br = PS.get(2)
            for dt_ in range(2):
                MM([k2.R(), vp.R(c, h)], [br[dt_]], out=ps[:, b0 + dt_, 0:257], lhsT=k2.ap[:, dt_ * 128:(dt_ + 1) * 128], rhs=vp.ap[:, c, h, :],
                   start=True, stop=True)
            DV("scalar_tensor_tensor", br + [R_g, DR], [DR], out=Dt, in0=Dt, scalar=EB[:, h, c:c + 1], in1=ps[:, b0:b0 + 2, 0:257],
               op0=ALU.mult, op1=ALU.add)

        k2p = [A.alloc(f"k2p{i}", [128, 256], BF16) for i in range(NCH)]

        def pre_k2(h, c):
            tb, tbr = PS.get(1)
            for dt_ in range(2):
                TR([kT.R(h * 2 + dt_, c // 3), R_cb], tbr, out=psb[:, tb, dt_ * 128:(dt_ + 1) * 128], in_=kT.ap[:, h * 2 + dt_, c * 128:(c + 1) * 128])
            ACT(tbr + [R_g], [k2p[c].R()], out=k2p[c].ap[:, :], in_=psb[:, tb, 0:256], func=AF.Identity, scale=EB[:, h, c:c + 1])

        def pre_upd(h, c, Dt, DR):
            k2 = k2p[c]
            b0, br = PS.get(2)
            for dt_ in range(2):
                MM([k2.R(), vp.R(c, h)], [br[dt_]], out=ps[:, b0 + dt_, 0:257], lhsT=k2.ap[:, dt_ * 128:(dt_ + 1) * 128], rhs=vp.ap[:, c, h, :],
                   start=True, stop=True)
            DV("scalar_tensor_tensor", br + [R_g, DR], [DR], out=Dt, in0=Dt, scalar=EB[:, h, c:c + 1], in1=ps[:, b0:b0 + 2, 0:257],
               op0=ALU.mult, op1=ALU.add)

        for tt in range(NTT):
            vproj_unit(0, tt)
        for c in range(NCH):
            pre_k2(0, c)
        for h in range(NH):
            DV("memset", [], [Dst.R(h)], ap=Dst.ap[:, h, :, :], constant=0.0)
            for i in range(NTT):
                if i < NCH:
                    pre_upd(h, i, Dst.ap[:, h, :, :], Dst.R(h))
                if h + 1 < NH:
                    vproj_unit(h + 1, i)
                    if i < NCH:
                        pre_k2(h + 1, i)
        DV("tensor_tensor", [R_g], [R_g], out=mfin[0:4, :], in0=minit[0:4, :], in1=Btot[0:4, :], op=ALU.add)
        DV("tensor_tensor", [R_g], [R_g], out=mfin[0:4, :], in0=mfin[0:4, :], in1=Mloc[0:4, :], op=ALU.max)
        S.dma("sp", mpo_d[l], mfin[0:4, :], reads=[R_g], chan="omp")
        ACT([R_g], [R_g], out=emfv[0:4, :], in_=mfin[0:4, :], func=AF.Exp, scale=-1.0)
        bcast4(emfv[0:4, :], 1, EMF, lambda a: a)

        wpre = {("q", 0): load_w(w_in_d[l][:, OFF_Q:OFF_Q + GW]), ("o", 0): load_w(w_in_d[l][:, OFF_O:OFF_O + GW])}
        R_b2 = Res("bnc2")
        R_g2 = Res("gat2")
        Dflat = Dst.ap[:, :, :, :].rearrange("p h d e -> p (h d e)")
        DstR = [Dst.R(h) for h in range(NH)]
        S.dma("sp", bnc2[l][:, :], Dflat, reads=DstR, writes=[R_b2], chan="x2")
        S.op("pool", "collective_compute", dict(kind="AllGather", op=ALU.bypass, replica_groups=GROUPS, ins=[bnc2[l][:, :]], outs=[gat2[l][:, :]]),
             reads=[R_b2], writes=[R_g2], chan=S.chan(f"cc2_{l}"), amount=1)
        S.dma("sp", Dflat, gat2[l][0:128, :], reads=[R_g2], writes=DstR, chan="x2r")
        qT = [A.alloc(f"qT{i}", [128, 2, T], BF16) for i in range(2)]
        Dbf = [A.alloc(f"Dbf{i}", [128, 2, 257], BF16) for i in range(2)]
        zmt = [A.alloc(f"zmt{i}", [128, 384], BF16) for i in range(3)]
        zi = [0]
        cot = [A.alloc(f"co{i}", [128, 2, 257], F32) for i in range(NH)]
        ai = [0]
        di = [0]
        hi_ = [0]

        def den_and_hs(U_bank, ubr, cidx, enb_col):
            R_rd = rdt.R()
            ACT(ubr, [R_rd], out=rdt.ap[:, 0:1], in_=ps[:, U_bank, 256:257], func=AF.Abs)
            DV("tensor_scalar", [R_rd, R_g], [R_rd], out=rdt.ap[:, 1:2], in0=rdt.ap[:, 0:1], scalar1=enb_col, scalar2=None, op0=ALU.max)
            DV("reciprocal", [R_rd], [R_rd], out=rdt.ap[:, 2:3], in_=rdt.ap[:, 1:2])
            ACT(ubr + [R_rd], [hs.R(cidx)], out=hs.ap[:, cidx, :], in_=ps[:, U_bank, 0:256], func=AF.Identity, scale=rdt.ap[:, 2:3])
            DV("bn_stats", [hs.R(cidx)], [stt_.R(cidx)], out=stt_.ap[:, cidx, :], in_=hs.ap[:, cidx, :])
            DV("bn_aggr", [stt_.R(cidx)], [mvt.R()], out=mvt.ap[:, cidx, :], in_=stt_.ap[:, cidx:cidx + 1, :])

        def finish_head(h, cidxs, tok0_of):
            c0, c1 = cidxs[0], cidxs[-1] + 1
            ACT([mvt.R(), R_c], [mvt.R()], out=mvt.ap[:, c0:c1, 1], in_=mvt.ap[:, c0:c1, 1], func=AF.Sqrt, bias=epsc)
            DV("reciprocal", [mvt.R()], [mvt.R()], out=mvt.ap[:, c0:c1, 1], in_=mvt.ap[:, c0:c1, 1])
            for cidx in cidxs:
                hb = hn[hi_[0] % 2]
                hi_[0] += 1
                DV("tensor_scalar", [hs.R(cidx), mvt.R()], [hb.R()], out=hb.ap[:, :], in0=hs.ap[:, cidx, :], scalar1=mvt.ap[:, cidx, 0:1],
                   scalar2=mvt.ap[:, cidx, 1:2], op0=ALU.subtract, op1=ALU.mult)
                tb, tbr = PS.get(1)
                for et in range(2):
                    TR([hb.R(), R_cb], tbr, out=psb[:, tb, et * 128:(et + 1) * 128], in_=hb.ap[:, et * 128:(et + 1) * 128])
                t0 = tok0_of(cidx)
                g = t0 // 384
                mr = [mixm.R(h * 2, g), mixm.R(h * 2 + 1, g)]
                DV("tensor_tensor", tbr + mr, mr, out=mixm.ap[:, h * 2:h * 2 + 2, t0:t0 + 128], in0=mixm.ap[:, h * 2:h * 2 + 2, t0:t0 + 128],
                   in1=psb[:, tb, 0:256].rearrange("p (a t) -> p a t", t=128), op=ALU.mult)

        def proj_tasks(h):
            q = qT[h % 2]
            wcache = {}

            def getw(nm, off):
                if nm not in wcache:
                    if (nm, h) in wpre:
                        wcache[nm] = wpre.pop((nm, h))
                    else:
                        wcache[nm] = load_w(w_in_d[l][:, off + h * GW:off + (h + 1) * GW])
                return wcache[nm]
            tasks = []
            for j in range(2):
                tile_ = h * 2 + j

                def task_q(j=j, tile_=tile_):
                    wq, wqres = getw("q", OFF_Q)

                    def evac_q(g, pap, bres):
                        t0, t1 = TG[g]
                        ACT([bres, R_prm], [q.R(j, g)], out=q.ap[:, j, t0:t1], in_=pap, func=AF.Identity, bias=P(l, "bfm")[:, tile_:tile_ + 1])
                        if g == 2:
                            ACT([bres, R_prm], [qTs.R(tile_)], out=qTs.ap[:, tile_, :], in_=pap[:, 256:384], func=AF.Identity,
                                bias=P(l, "bfm")[:, tile_:tile_ + 1])
                    yield from proj_fm_gen(wq, wqres, j, evac_q)
                tasks.append(task_q)
            for j in range(2):
                tile_ = h * 2 + j

                def task_o(j=j, tile_=tile_):
                    wo, wores = getw("o", OFF_O)

                    def evac_o(g, pap, bres):
                        t0, t1 = TG[g]
                        ACT([bres, R_prm], [mixm.R(tile_, g)], out=mixm.ap[:, tile_, t0:t1], in_=pap, func=AF.Sigmoid,
                            bias=P(l, "bfm")[:, 24 + tile_:25 + tile_])
                    yield from proj_fm_gen(wo, wores, j, evac_o)
                tasks.append(task_o)
            for j in range(2):
                tile_ = h * 2 + j

                def task_zm(j=j, tile_=tile_):
                    wz, wzres = getw("zm", OFF_ZM)

                    def evac_zm(g, pap, bres):
                        t0, t1 = TG[g]
                        zt = zmt[zi[0] % 3]
                        zi[0] += 1
                        ACT([bres, R_prm], [zt.R()], out=zt.ap[:, :], in_=pap, func=AF.Silu, bias=P(l, "bfm")[:, 32 + tile_:33 + tile_])
                        DV("scalar_tensor_tensor", [zt.R(), R_prm, mixm.R(tile_, g)], [mixm.R(tile_, g)], out=mixm.ap[:, tile_, t0:t1], in0=zt.ap[:, :],
                           scalar=P(l, "hng")[:, tile_:tile_ + 1], in1=mixm.ap[:, tile_, t0:t1], op0=ALU.mult, op1=ALU.mult)
                    yield from proj_fm_gen(wz, wzres, j, evac_zm)
                tasks.append(task_zm)
            return tasks

        def run_all(tasks):
            for t_ in tasks:
                for _ in t_():
                    pass

        def seg_stream(tasks):
            for t_ in tasks:
                yield from t_()

        run_all(proj_tasks(0))
        atm8 = [A.alloc(f"atm8_{i}", [128, 128], BF16) for i in range(NCH)]
        k2s8 = [A.alloc(f"k2s8_{i}", [128, 256], BF16) for i in range(NCH)]
        for h in range(NH):
            q = qT[h % 2]
            stream = seg_stream(proj_tasks(h + 1)) if h + 1 < NH else iter(())
            Dt = Dst.ap[:, h, :, :]
            DR = Dst.R(h)
            DV("tensor_scalar", [DR, R_c], [DR], out=Dt, in0=Dt, scalar1=flag, scalar2=None, op0=ALU.mult)
            for c in range(NCH):
                g = c // 3
                tk = slice(c * 128, (c + 1) * 128)
                a0, abr = PS.get(1)
                for dt_ in range(2):
                    MM([kT.R(h * 2 + dt_, g), q.R(dt_, g)], abr, out=ps[:, a0, 0:128], lhsT=kT.ap[:, h * 2 + dt_, tk], rhs=q.ap[:, dt_, tk],
                       start=(dt_ == 0), stop=(dt_ == 1))
                DV("tensor_tensor", abr + [R_c], [atm8[c].R()], out=atm8[c].ap[:, :], in0=ps[:, a0, 0:128], in1=maskp, op=ALU.mult)
                tb, tbr = PS.get(1)
                for dt_ in range(2):
                    TR([kT.R(h * 2 + dt_, g), R_cb], tbr, out=psb[:, tb, dt_ * 128:(dt_ + 1) * 128], in_=kT.ap[:, h * 2 + dt_, tk])
                ACT(tbr + [R_g], [k2s8[c].R()], out=k2s8[c].ap[:, :], in_=psb[:, tb, 0:256], func=AF.Identity, scale=EB[:, h, c:c + 1])
            for c in range(NCH):
                g = c // 3
                tk = slice(c * 128, (c + 1) * 128)
                k2 = k2s8[c]
                b0, br = PS.get(2)
                for dt_ in range(2):
                    MM([k2.R(), vp.R(c, h)], [br[dt_]], out=ps[:, b0 + dt_, 0:257], lhsT=k2.ap[:, dt_ * 128:(dt_ + 1) * 128], rhs=vp.ap[:, c, h, :],
                       start=True, stop=True)
                db = Dbf[di[0] % 2]
                di[0] += 1
                ACT([DR], [db.R()], out=db.ap[:, :, :], in_=Dt, func=AF.Copy)
                DV("scalar_tensor_tensor", br + [R_g, DR], [DR], out=Dt, in0=Dt, scalar=EB[:, h, c:c + 1], in1=ps[:, b0:b0 + 2, 0:257],
                   op0=ALU.mult, op1=ALU.add)
                u0, ubr = PS.get(1)
                MM([atm8[c].R(), vp.R(c, h)], ubr, out=ps[:, u0, 0:257], lhsT=atm8[c].ap[:, :], rhs=vp.ap[:, c, h, :], start=True, stop=False)
                for dt_ in range(2):
                    MM([q.R(dt_, g), db.R()], ubr, out=ps[:, u0, 0:257], lhsT=q.ap[:, dt_, tk], rhs=db.ap[:, dt_, :], start=False, stop=(dt_ == 1))
                den_and_hs(u0, ubr, c, enb[:, c, h:h + 1])
                for _ in range(3):
                    next(stream, None)
            for _ in stream:
                pass
            finish_head(h, list(range(NCH)), lambda cidx: cidx * 128)
            co = cot[h]
            DV("tensor_scalar", [DR, R_g], [co.R()], out=co.ap[:, :, :], in0=Dt, scalar1=EMF[:, h:h + 1], scalar2=None, op0=ALU.mult)
            S.dma("sp", Cp_d[l, h].rearrange("(d p) e -> p d e", p=128), co.ap[:, :, 0:256], reads=[co.R()], chan=f"oCp{h}")
            S.dma("sp", npo_d[l][:, h * 2:h * 2 + 2], co.ap[:, :, 256], reads=[co.R()], chan=f"onp{h}", noncontig=True)

        A.pop()
        A.push()
        XA = SubArena(A, xT, xT.base, xT.nwords)
        kts = A.alloc("kts", [128, MW], BF16)
        for tile_ in range(8):
            tb, tbr = PS.get(1)
            TR([kT.R(tile_, 2), R_cb], tbr, out=psb[:, tb, 0:128], in_=kT.ap[:, tile_, NPT:T])
            ACT(tbr, [kts.R()], out=kts.ap[:, tile_ * 128:(tile_ + 1) * 128], in_=psb[:, tb, 0:128], func=AF.Copy)
        snT = A.alloc("snT", [128, 128], F32)
        S.dma("sp", snT.ap[:, :], snT_d[l], writes=[snT.R()], chan="snT")
        nout = A.alloc("nout", [128, 128], F32)
        KZ = [XA.alloc(f"KZ{i}", [128, NSEQ, 256], BF16) for i in range(2)]
        QZ = [XA.alloc(f"QZ{i}", [128, 2, 2176], BF16) for i in range(2)]
        for i in range(2):
            PL("memset", [], [QZ[i].R()], ap=QZ[i].ap[:, :, :], constant=0.0)
        NCI = 6
        Cin = [A.alloc(f"Cin{i}", [128, 2, 257], F32) for i in range(NCI)]
        Cbf = [A.alloc(f"Cbf{i}", [128, 2, 257], BF16) for i in range(3)]
        Ctm = [A.alloc(f"Ctm{i}", [128, 2, 257], F32) for i in range(2)]
        Cou = [A.alloc(f"Cou{i}", [128, 2, 257], F32) for i in range(4)]
        it = [0]

        def issue_cin(idx):
            if idx >= NH * NSEQ:
                return
            hh, jj = idx // NSEQ, idx % NSEQ
            ci_ = Cin[idx % NCI]
            S.dma("sp", ci_.ap[:, :, 0:256], sC_d[l, jj, hh].rearrange("(d p) e -> p d e", p=128), writes=[ci_.R()], chan=f"ci{idx % NCI}")

        def prep_n(idx):
            if idx >= NH * NSEQ:
                return
            hh, jj = idx // NSEQ, idx % NSEQ
            ci_ = Cin[idx % NCI]
            col_ = (jj * NH + hh) * 2
            DV("tensor_copy", [snT.R()], [ci_.R("n")], out=ci_.ap[:, :, 256], in_=snT.ap[:, col_:col_ + 2])

        def prep_c(idx):
            if idx >= NH * NSEQ:
                return
            hh, jj = idx // NSEQ, idx % NSEQ
            ci_ = Cin[idx % NCI]
            cb_ = Cbf[idx % 3]
            ACT([ci_.R(), ci_.R("n"), R_g], [cb_.R()], out=cb_.ap[:, :, :], in_=ci_.ap[:, :, :], func=AF.Identity, scale=BCs[:, hh, 2, jj:jj + 1])

        for i in range(NCI - 1):
            issue_cin(i)
            prep_n(i)
        prep_c(0)
        ams = {}

        def make_setup(h):
            kz = KZ[h % 2]
            qz = QZ[h % 2]
            pieces = []
            for qq in range(4):
                def kz_piece(qq=qq):
                    DV("tensor_tensor", [kts.R(), R_c], [kz.R()], out=kz.ap[:, qq * 4:(qq + 1) * 4, :],
                       in0=kts.ap[:, h * 256:(h + 1) * 256].unsqueeze(1).to_broadcast([128, 4, 256]),
                       in1=blk[:, qq * 4:(qq + 1) * 4].unsqueeze(2).to_broadcast([128, 4, 256]), op=ALU.mult)
                pieces.append(kz_piece)

            def qz_piece():
                for dt_ in range(2):
                    DV("tensor_copy", [qTs.R(h * 2 + dt_)], [qz.R()], out=qz.ap[:, dt_, :].rearrange("p (j x) -> p j x", x=136)[:, :, 0:8],
                       in_=qTs.ap[:, h * 2 + dt_, :].rearrange("p (j i) -> p j i", i=8))
            pieces.append(qz_piece)

            def at_piece():
                a0, abr = PS.get(1)
                for dt_ in range(2):
                    MM([kT.R(h * 2 + dt_, 2), qTs.R(h * 2 + dt_)], abr, out=ps[:, a0, 0:128], lhsT=kT.ap[:, h * 2 + dt_, NPT:T], rhs=qTs.ap[:, h * 2 + dt_, :],
                       start=(dt_ == 0), stop=(dt_ == 1))
                am_ = A.alloc(f"ams{h}", [128, 128], BF16) if False else amsb[h % 2]
                DV("tensor_tensor", abr + [R_c], [am_.R()], out=am_.ap[:, :], in0=ps[:, a0, 0:128], in1=masks, op=ALU.mult)
                ams[h] = am_
            pieces.append(at_piece)
            return pieces

        amsb = [A.alloc(f"amsb{i}", [128, 128], BF16) for i in range(2)]
        setups = [make_setup(h) for h in range(NH)]
        for p_ in setups[0]:
            p_()
        for h in range(NH):
            kz = KZ[h % 2]
            qz = QZ[h % 2]
            am = ams[h]
            u0, ubr = PS.get(1)
            MM([am.R(), vp.R(NCH, h)], ubr, out=ps[:, u0, 0:257], lhsT=am.ap[:, :], rhs=vp.ap[:, NCH, h, :], start=True, stop=False)
            PS.pinned.add(u0)
            for j in range(NSEQ):
                k_ = it[0]
                it[0] += 1
                ci = Cin[k_ % NCI]
                cb = Cbf[k_ % 3]
                ctm = Ctm[k_ % 2]
                cu = Cou[k_ % 4]
                issue_cin(k_ + NCI - 1)
                prep_n(k_ + NCI - 1)
                if h + 1 < NH and 2 <= j < 2 + len(setups[h + 1]):
                    setups[h + 1][j - 2]()
                col = (j * NH + h) * 2
                prep_c(k_ + 1)
                for dt_ in range(2):
                    MM([qz.R(), cb.R()], ubr, out=ps[:, u0, 0:257], lhsT=qz.ap[:, dt_, j * 128:(j + 1) * 128], rhs=cb.ap[:, dt_, :],
                       start=False, stop=(j == NSEQ - 1 and dt_ == 1))
                b0, br = PS.get(2)
                for dt_ in range(2):
                    MM([kz.R(), vp.R(NCH, h)], [br[dt_]], out=ps[:, b0 + dt_, 0:257], lhsT=kz.ap[:, j, dt_ * 128:(dt_ + 1) * 128], rhs=vp.ap[:, NCH, h, :],
                       start=True, stop=True)
                ACT(br + [R_g], [ctm.R()], out=ctm.ap[:, :, :], in_=ps[:, b0:b0 + 2, 0:257], func=AF.Identity, scale=BCs[:, h, 1, j:j + 1])
                DV("scalar_tensor_tensor", [ci.R(), ci.R("n"), ctm.R(), R_g], [cu.R()], out=cu.ap[:, :, :], in0=ci.ap[:, :, :], scalar=BCs[:, h, 0, j:j + 1],
                   in1=ctm.ap[:, :, :], op0=ALU.mult, op1=ALU.add)
                S.dma("sp", Cs_d[l, j, h].rearrange("(d p) e -> p d e", p=128), cu.ap[:, :, 0:256], reads=[cu.R()], chan=f"cu{k_ % 4}")
                DV("tensor_copy", [cu.R()], [nout.R()], out=nout.ap[:, col:col + 2], in_=cu.ap[:, :, 256])
            PS.pinned.discard(u0)
            den_and_hs(u0, ubr, NCH, enb[:, NCH, h:h + 1])
            finish_head(h, [NCH], lambda cidx: NPT)
        S.dma("sp", nso_d[l], nout.ap[:, :], reads=[nout.R()], chan="ons")
        XA.close()
        A.pop()
        A.pop()

        A.push()
        z = A.alloc("z", [128, NTT, D_MODEL], F32)
        lgb = A.alloc("lgb", [128, D_MODEL], F32)
        lbb = A.alloc("lbb", [128, D_MODEL], F32)
        S.dma("sp", lgb.ap[:, :], lng_d[l].partition_broadcast(128), writes=[lgb.R()], chan="lnp0")
        S.dma("sp", lbb.ap[:, :], lnb_d[l].partition_broadcast(128), writes=[lbb.R()], chan="lnp1")
        for tt in range(NTT):
            if l == 0:
                S.dma("sp", z.ap[:, tt, :], xtok_d[tt * 128:(tt + 1) * 128, :], writes=[z.R(tt)], chan=f"xr{tt}")
            else:
                S.dma("sp", z.ap[:, tt, :], y1_d[tt * 128:(tt + 1) * 128, :], reads=[R_y1[tt]], writes=[z.R(tt)], chan=f"xr{tt}")

        if DEBUG and l == 0:
            dbg_d = nc.dram_tensor("dbg", [128, 16, NST], F32, kind="ExternalOutput").ap()
            XD = SubArena(A, xT, xT.base, xT.nwords)
            dbt = XD.alloc("dbt", [128, 16, NST], F32)
            ACT([mixm.R(k, 2) for k in range(8)], [dbt.R()], out=dbt.ap[:, 0:8, :], in_=mixm.ap[:, :, NPT:T], func=AF.Copy)
            ACT([mixc.R(k, 2) for k in range(8)], [dbt.R()], out=dbt.ap[:, 8:16, :], in_=mixc.ap[:, :, NPT:T], func=AF.Copy)
            S.dma("sp", dbg_d[:, :, :], dbt.ap[:, :, :], reads=[dbt.R()], chan="dbg")
            XD.close()

        def mix_tile(kt, tt):
            src = mixm if kt < 8 else mixc
            return src.ap[:, kt % 8, tt * 128:(tt + 1) * 128], src.R(kt % 8, tt // 3)

        st2 = A.alloc("st2", [128, 2, 4, 6], F32)
        mv2 = A.alloc("mv2", [128, 2, 4], F32)
        ybf = [A.alloc(f"ybf{i}", [128, D_MODEL], BF16) for i in range(2)]

        def ln_gen(tts):
            for tt in tts:
                zt_ = z.ap[:, tt, :]
                zR = z.R(tt)
                sb_ = tt % 2
                sR, mR = st2.R(sb_), mv2.R(sb_)
                mv = mv2.ap[:, sb_, :]
                for q4 in range(4):
                    DV("bn_stats", [zR], [sR], out=st2.ap[:, sb_, q4, :], in_=z.ap[:, tt, q4 * 512:(q4 + 1) * 512])
                    if q4 % 2 == 1:
                        yield
                DV("bn_aggr", [sR], [mR], out=mv[:, 0:2], in_=st2.ap[:, sb_, :, :])
                ACT([mR, R_c], [mR], out=mv[:, 1:2], in_=mv[:, 1:2], func=AF.Sqrt, bias=epsc)
                DV("reciprocal", [mR], [mR], out=mv[:, 1:2], in_=mv[:, 1:2])
                DV("scalar_tensor_tensor", [mR], [mR], out=mv[:, 2:3], in0=mv[:, 0:1], scalar=-1.0, in1=mv[:, 1:2], op0=ALU.mult, op1=ALU.mult)
                yield
                ACT([zR, mR], [zR], out=zt_, in_=zt_, func=AF.Identity, scale=mv[:, 1:2], bias=mv[:, 2:3])
                yield
                DV("tensor_tensor", [zR, lgb.R()], [zR], out=zt_, in0=zt_, in1=lgb.ap[:, :], op=ALU.mult)
                yield
                DV("tensor_tensor", [zR, lbb.R()], [zR], out=zt_, in0=zt_, in1=lbb.ap[:, :], op=ALU.add)
                if l == DEPTH - 1:
                    S.dma("sp", y_d[tt * 128:(tt + 1) * 128, :], zt_, reads=[zR], chan="oy")
                    yield
                else:
                    S.dma("sp", y1_d[tt * 128:(tt + 1) * 128, :], zt_, reads=[zR], writes=[R_y1[tt]], chan=f"oy1_{tt}")
                    yb = ybf[tt % 2]
                    ACT([zR], [yb.R()], out=yb.ap[:, :], in_=zt_, func=AF.Copy)
                    yield
                    for hf in range(2):
                        tb, tbr = PS.get(1)
                        for k8 in range(8):
                            kt = hf * 8 + k8
                            TR([yb.R(), R_cb], tbr, out=psb[:, tb, k8 * 128:(k8 + 1) * 128], in_=yb.ap[:, kt * 128:(kt + 1) * 128])
                        ACT(tbr, [xT.R(tt)], out=xT.ap[:, hf * 8:(hf + 1) * 8, tt * 128:(tt + 1) * 128],
                            in_=psb[:, tb, 0:1024].rearrange("p (k t) -> p k t", t=128), func=AF.Copy)
                        yield

        PASSES = [(0, 3), (3, 6), (6, 9)]
        prev = iter(())
        for (ta, tb_) in PASSES:
            for cg in range(D_MODEL // GW):
                i_w = wstate["i"] % 3
                wstate["i"] += 1
                wo_ = wores_ = wbufs[i_w]
                S.dma("pool", wo_.ap[:, :, :], wob_d[l][:, cg * GW:(cg + 1) * GW].rearrange("(k p) n -> p k n", p=128),
                      reads=[R_wob[l]], writes=[wo_.R(0), wo_.R(1)], chan=f"w{i_w}")
                for tt in range(ta, tb_):
                    b0, br = PS.get(1)
                    for kt in range(KT):
                        mt, mr = mix_tile(kt, tt)
                        last = kt == KT - 1
                        MM([wores_.R(0), wores_.R(1), mr], (br if kt == 0 or last else []), inc=last, out=ps[:, b0, 0:GW], lhsT=mt, rhs=wo_.ap[:, kt, :],
                           start=(kt == 0), stop=last)
                    zv = z.ap[:, tt, cg * GW:(cg + 1) * GW]
                    DV("scalar_tensor_tensor", br + [z.R(tt)], [z.R(tt)], out=zv, in0=zv, scalar=float(DN_ALPHA), in1=ps[:, b0, 0:GW],
                       op0=ALU.mult, op1=ALU.add)
                    next(prev, None)
            for _ in prev:
                pass
            prev = ln_gen(range(ta, tb_))
        for _ in prev:
            pass
        A.pop()
        A.pop()
        A.pop()

    for l in range(DEPTH):
        layer(l)

    final = {k: v for k, v in S.cnt.items() if k.startswith("D_") and v > 0}
    S.ops["sp"].append((final, None, None))
    S.emit()
    return nc, A.peak


def _get_program():
    nc, _peak = build_program()
    return nc


def kernel(x_prompt, x_sample, state_C, state_n, state_m, state_conv, w_in, b_in, hn_g, w_dw, b_dw,
           cln_g, cln_b, w_out, ln_g, ln_b):
    f32 = np.float32
    x_prompt = np.asarray(x_prompt, f32)
    x_sample = np.asarray(x_sample, f32)
    state_C = np.asarray(state_C, f32)
    state_n = np.asarray(state_n, f32)
    state_m = np.asarray(state_m, f32)
    state_conv = np.asarray(state_conv, f32)
    w_in = np.ascontiguousarray(np.asarray(w_in, f32))
    w_out = np.ascontiguousarray(np.asarray(w_out, f32))
    b_in = np.asarray(b_in, f32)
    hn_g = np.asarray(hn_g, f32)
    w_dw = np.asarray(w_dw, f32)
    b_dw = np.asarray(b_dw, f32)
    cln_g = np.asarray(cln_g, f32)
    cln_b = np.asarray(cln_b, f32)
    ln_g = np.asarray(ln_g, f32)
    ln_b = np.asarray(ln_b, f32)

    def fm(v, ntile):
        return np.ascontiguousarray(v.reshape(DEPTH, ntile, 128).transpose(0, 2, 1))

    shared = {
        "w_in": w_in, "w_out": w_out,
        "bfm": fm(b_in[:, :8192], 64),
        "bg": np.ascontiguousarray(b_in[:, 8192:8200].reshape(DEPTH, 1, 8)),
        "bv": np.ascontiguousarray(b_in[:, OFF_V:OFF_V + MW].reshape(DEPTH, 1, MW)),
        "hng": fm(hn_g, 8),
        "wdw": np.ascontiguousarray(w_dw.reshape(DEPTH, CONVK, 8, 128).transpose(0, 3, 2, 1)),
        "bdw": fm(b_dw, 8), "clng": fm(cln_g, 8), "clnb": fm(cln_b, 8),
        "lng": np.ascontiguousarray(ln_g.reshape(DEPTH, 1, D_MODEL)),
        "lnb": np.ascontiguousarray(ln_b.reshape(DEPTH, 1, D_MODEL)),
    }
    in_maps = []
    for c in range(8):
        b, hf = c // 2, c % 2
        xp = x_prompt[b, hf * NPT:(hf + 1) * NPT, :]
        xs = x_sample[c * NSEQ:(c + 1) * NSEQ].reshape(NST, D_MODEL)
        xtok = np.ascontiguousarray(np.concatenate([xp, xs], 0))
        sl = slice(c * NSEQ, (c + 1) * NSEQ)
        sn = state_n[:, sl]
        snT = sn.reshape(DEPTH, NSEQ, NH, 2, 128).transpose(0, 4, 1, 2, 3).reshape(DEPTH, 128, 128)
        scv = state_conv[:, sl]
        scvT = scv.reshape(DEPTH, NSEQ, 30, 8, 128).transpose(0, 4, 3, 1, 2)
        scvK = np.ascontiguousarray(scvT[:, :, :, :, 8:30]).reshape(DEPTH, 128, 8 * NSEQ * 22)
        m = dict(shared)
        m.update({
            "xT": np.ascontiguousarray(xtok.T), "xtok": xtok,
            "sC": np.ascontiguousarray(state_C[:, sl]),
            "snT": np.ascontiguousarray(snT),
            "smT": np.ascontiguousarray(state_m[:, sl].transpose(0, 2, 1)),
            "scvT": np.ascontiguousarray(scvT).reshape(DEPTH, 128, 8 * NSEQ * 30), "scvK": scvK,
            "flag": np.full((128, 1), float(hf), f32),
        })
        in_maps.append(m)
    nc = _get_program()
    res = run_bass_kernel_spmd(nc, in_maps, core_ids=list(range(8)))
    R = res.results
    B = x_prompt.shape[0]
    y_prompt = np.empty_like(x_prompt)
    y_sample = np.empty_like(x_sample)
    Cp = np.empty((DEPTH, B, NH, DH, DH), f32)
    npr = np.empty((DEPTH, B, NH, DH), f32)
    mp = np.empty((DEPTH, B, NH), f32)
    cp = np.empty((DEPTH, B, CONVK - 1, CW), f32)
    Cs = np.empty((DEPTH, 128, NH, DH, DH), f32)
    ns = np.empty((DEPTH, 128, NH, DH), f32)
    ms = np.empty((DEPTH, 128, NH), f32)
    cs = np.empty((DEPTH, 128, CONVK - 1, CW), f32)
    for c in range(8):
        r = R[c]
        b, hf = c // 2, c % 2
        y = np.asarray(r["y"], f32)
        y_prompt[b, hf * NPT:(hf + 1) * NPT] = y[:NPT]
        y_sample[c * NSEQ:(c + 1) * NSEQ] = y[NPT:].reshape(NSEQ, 8, D_MODEL)
        sl = slice(c * NSEQ, (c + 1) * NSEQ)
        Cs[:, sl] = np.asarray(r["Cs"], f32)
        ns[:, sl] = np.asarray(r["nso"], f32).reshape(DEPTH, 128, NSEQ, NH, 2).transpose(0, 2, 3, 4, 1).reshape(DEPTH, NSEQ, NH, DH)
        ms[:, sl] = np.asarray(r["mso"], f32).transpose(0, 2, 1)
        cso = np.concatenate([np.asarray(r["cso_old"], f32).reshape(DEPTH, 128, 8, NSEQ, 22),
                              np.asarray(r["cso_new"], f32).reshape(DEPTH, 128, 8, NSEQ, 8)], axis=4)
        cs[:, sl] = cso.transpose(0, 3, 4, 2, 1).reshape(DEPTH, NSEQ, 30, CW)
        if hf == 1:
            Cp[:, b] = np.asarray(r["Cp"], f32)
            npr[:, b] = np.asarray(r["npo"], f32).reshape(DEPTH, 128, NH, 2).transpose(0, 2, 3, 1).reshape(DEPTH, NH, DH)
            mp[:, b] = np.asarray(r["mpo"], f32).reshape(DEPTH, NH)
            cp[:, b] = np.asarray(r["cpo"], f32).transpose(0, 3, 2, 1).reshape(DEPTH, 30, CW)
    return (y_prompt, y_sample, Cp, npr, mp, cp, Cs, ns, ms, cs)
```
